# Optimizing a Trainium2 kernel written in Bass

```python
import jax, jax.numpy as jnp
from jax import lax
import numpy as np

D_MODEL = 2048
BATCH = 4
SEQ = 2048
DEPTH = 2

N_A_LAYERS = DEPTH // 2
N_B_LAYERS = DEPTH - N_A_LAYERS
PLE_DIM = 256
D_FF = 4 * D_MODEL
HGRN_HEAD_DIM = 128
HGRN_HEADS = D_MODEL // HGRN_HEAD_DIM
HGRN_CHUNK = 16
FOX_HEAD_DIM = 128
FOX_HEADS = D_MODEL // FOX_HEAD_DIM
Q_BLOCK = 128
EPS = 1e-6

kernel_name = "yoco_hgrn2_fox_hybrid"


def rms_norm(x, gain):
    xf = x.astype(jnp.float32)
    y = xf * lax.rsqrt(jnp.mean(xf * xf, axis=-1, keepdims=True) + EPS)
    return (y * gain.astype(jnp.float32)).astype(x.dtype)


def hgrn2_mixer(u, w_in, lb, head_gain, w_out):
    bsz, seq, _ = u.shape
    nc = seq // HGRN_CHUNK
    q, f, i, g = jnp.split(u @ w_in, 4, axis=-1)

    def to_chunks(t):
        t = t.reshape(bsz, nc, HGRN_CHUNK, HGRN_HEADS, HGRN_HEAD_DIM)
        return t.transpose(0, 3, 1, 2, 4).astype(jnp.float32)

    lb_h = lb.astype(jnp.float32).reshape(HGRN_HEADS, 1, 1, HGRN_HEAD_DIM)
    fg = lb_h + (1.0 - lb_h) * jax.nn.sigmoid(to_chunks(f))
    k = 1.0 - fg
    b = jnp.cumsum(jnp.log(fg), axis=3)
    b_last = b[:, :, :, -1:, :]
    qc = jax.nn.silu(to_chunks(q)) * (HGRN_HEAD_DIM ** -0.5)
    v = to_chunks(i)

    q_in = qc * jnp.exp(b)
    k_in = k * jnp.exp(-b)
    k_end = k * jnp.exp(b_last - b)
    causal = jnp.tril(jnp.ones((HGRN_CHUNK, HGRN_CHUNK), dtype=bool))
    att = jnp.where(causal, jnp.einsum('bhncd,bhnsd->bhncs', q_in, k_in), 0.0)
    o_intra = jnp.einsum('bhncs,bhnse->bhnce', att, v)

    def step(state, inp):
        q_n, k_n, v_n, dec_n = inp
        o_n = jnp.einsum('bhcd,bhde->bhce', q_n, state)
        state = dec_n[..., None] * state + jnp.einsum('bhcd,bhce->bhde', k_n, v_n)
        return state, o_n

    xs = (jnp.moveaxis(q_in, 2, 0), jnp.moveaxis(k_end, 2, 0), jnp.moveaxis(v, 2, 0),
          jnp.moveaxis(jnp.exp(b_last[:, :, :, 0, :]), 2, 0))
    init = jnp.zeros((bsz, HGRN_HEADS, HGRN_HEAD_DIM, HGRN_HEAD_DIM), jnp.float32)
    _, o_inter = lax.scan(step, init, xs)
    o = o_intra + jnp.moveaxis(o_inter, 0, 2)

    o = o.transpose(0, 2, 3, 1, 4).reshape(bsz, seq, HGRN_HEADS, HGRN_HEAD_DIM)
    o = o * lax.rsqrt(jnp.mean(o * o, axis=-1, keepdims=True) + EPS) * head_gain.astype(jnp.float32)
    o = o.reshape(bsz, seq, D_MODEL) * jax.nn.silu(g.astype(jnp.float32))
    return o.astype(u.dtype) @ w_out


def shared_kv(stream, kv_norm, w_kvf, b_f):
    bsz, seq, _ = stream.shape
    hk = rms_norm(stream, kv_norm) @ w_kvf
    k = hk[..., :D_MODEL].reshape(bsz, seq, FOX_HEADS, FOX_HEAD_DIM)
    v = hk[..., D_MODEL:2 * D_MODEL].reshape(bsz, seq, FOX_HEADS, FOX_HEAD_DIM)
    f_logit = hk[..., 2 * D_MODEL:].astype(jnp.float32) + b_f.astype(jnp.float32)
    dcum = jnp.cumsum(jax.nn.log_sigmoid(f_logit), axis=1).transpose(0, 2, 1)
    return k, v, dcum


def fox_mixer(u, k, v, dcum, w_q, w_out):
    bsz, seq, _ = u.shape
    q = (u @ w_q).reshape(bsz, seq, FOX_HEADS, FOX_HEAD_DIM) * (FOX_HEAD_DIM ** -0.5)
    outs = []
    for blk in range(seq // Q_BLOCK):
        start, end = blk * Q_BLOCK, (blk + 1) * Q_BLOCK
        logits = jnp.einsum('bqhd,bkhd->bhqk', q[:, start:end], k[:, :end]).astype(jnp.float32)
        logits = logits + dcum[:, :, start:end, None] - dcum[:, :, None, :end]
        causal = jnp.arange(start, end)[:, None] >= jnp.arange(end)[None, :]
        probs = jax.nn.softmax(jnp.where(causal, logits, -jnp.inf), axis=-1)
        outs.append(jnp.einsum('bhqk,bkhd->bqhd', probs.astype(v.dtype), v[:, :end]))
    o = jnp.concatenate(outs, axis=1).reshape(bsz, seq, D_MODEL)
    return o @ w_out


def sq_relu_mlp(u, w_up, w_down):
    hid = jax.nn.relu(u @ w_up)
    return (hid * hid) @ w_down


def setup_inputs(seed: int = 0) -> dict:
    key = jax.random.key(seed)
    ks = jax.random.split(key, 20)
    f32 = jnp.float32

    def dense(k, shape, fan_in):
        return jax.random.normal(k, shape, f32) * (fan_in ** -0.5)

    def gain(k, shape):
        return 1.0 + 0.02 * jax.random.normal(k, shape, f32)

    return {
        "x": jax.random.normal(ks[0], (BATCH, SEQ, D_MODEL), f32),
        "p": jax.random.normal(ks[1], (DEPTH, BATCH, SEQ, PLE_DIM), f32),
        "mix_norm": gain(ks[2], (DEPTH, D_MODEL)),
        "mlp_norm": gain(ks[3], (DEPTH, D_MODEL)),
        "ple_norm": gain(ks[4], (DEPTH, D_MODEL)),
        "w_a_in": dense(ks[5], (N_A_LAYERS, D_MODEL, 4 * D_MODEL), D_MODEL),
        "a_lb_logits": 0.3 * jax.random.normal(ks[6], (N_A_LAYERS + 1, D_MODEL), f32),
        "a_head_gain": gain(ks[7], (N_A_LAYERS, HGRN_HEAD_DIM)),
        "w_a_out": dense(ks[8], (N_A_LAYERS, D_MODEL, D_MODEL), D_MODEL),
        "kv_norm": gain(ks[9], (D_MODEL,)),
        "w_kvf": dense(ks[10], (D_MODEL, 2 * D_MODEL + FOX_HEADS), D_MODEL),
        "b_f": 2.0 + 0.5 * jax.random.normal(ks[11], (FOX_HEADS,), f32),
        "w_b_q": dense(ks[12], (N_B_LAYERS, D_MODEL, D_MODEL), D_MODEL),
        "w_b_out": dense(ks[13], (N_B_LAYERS, D_MODEL, D_MODEL), D_MODEL),
        "w_mlp_up": dense(ks[14], (DEPTH, D_MODEL, D_FF), D_MODEL),
        "w_mlp_down": dense(ks[15], (DEPTH, D_FF, D_MODEL), D_FF),
        "w_ple_gate": dense(ks[16], (DEPTH, D_MODEL, D_MODEL), D_MODEL),
        "w_ple_up": dense(ks[17], (DEPTH, PLE_DIM, D_MODEL), PLE_DIM),
        "final_norm": gain(ks[18], (D_MODEL,)),
    }


def reference(x, p, mix_norm, mlp_norm, ple_norm, w_a_in, a_lb_logits, a_head_gain, w_a_out,
              kv_norm, w_kvf, b_f, w_b_q, w_b_out, w_mlp_up, w_mlp_down, w_ple_gate, w_ple_up,
              final_norm):
    lb_all = jnp.cumsum(jax.nn.softmax(a_lb_logits.astype(jnp.float32), axis=0), axis=0)
    h = x
    k_sh = v_sh = d_sh = None
    for layer in range(DEPTH):
        u = rms_norm(h, mix_norm[layer])
        if layer < N_A_LAYERS:
            h = h + hgrn2_mixer(u, w_a_in[layer], lb_all[layer], a_head_gain[layer], w_a_out[layer])
        else:
            j = layer - N_A_LAYERS
            h = h + fox_mixer(u, k_sh, v_sh, d_sh, w_b_q[j], w_b_out[j])
        h = h + sq_relu_mlp(rms_norm(h, mlp_norm[layer]), w_mlp_up[layer], w_mlp_down[layer])
        gate = jax.nn.sigmoid(rms_norm(h, ple_norm[layer]) @ w_ple_gate[layer])
        h = h + (p[layer].astype(h.dtype) @ w_ple_up[layer]) * gate
        if layer == N_A_LAYERS - 1:
            k_sh, v_sh, d_sh = shared_kv(h, kv_norm, w_kvf, b_f)
    return rms_norm(h, final_norm)
```

```python
import numpy as np
from contextlib import ExitStack
import concourse.bass as bass
import concourse.mybir as mybir
from concourse.bass_utils import run_bass_kernel_spmd

F32 = mybir.dt.float32
BF16 = mybir.dt.bfloat16
AF = mybir.ActivationFunctionType
ALU = mybir.AluOpType

D = 2048
NCH = 16
T = 1024
SEQ = 2048
NH = 16
HD = 128
PLE = 256
DFF = 8192
EPS = 1e-6
CH = 32
NCHUNK = T // CH
NSLOT = 6
NEG = -30000.0


class Buf:
    __slots__ = ("name", "w", "rs", "excl")

    def __init__(self, name, excl=False):
        self.name = name
        self.w = None
        self.rs = {}
        self.excl = excl


class Op:
    __slots__ = ("eng", "emit", "deps", "sig", "val", "chan", "idx", "inc")

    def __init__(self, eng, emit, chan=None):
        self.eng = eng
        self.emit = emit
        self.deps = []
        self.sig = False
        self.val = None
        self.chan = chan
        self.idx = None
        self.inc = 16


class _Rec:
    def __init__(self):
        self.calls = []

    def __getattr__(self, name):
        def f(*a, **k):
            self.calls.append((name, a, k))
            return self
        return f


def _freeze(emit):
    r = _Rec()
    emit(r)
    assert len(r.calls) == 1, r.calls
    name, a, k = r.calls[0]
    return lambda eng: getattr(eng, name)(*a, **k)


class Prog:
    ENGS = ("pe", "act", "dve", "pool", "sp")

    def __init__(self, same_engine_sync=True):
        self.ops = {e: [] for e in self.ENGS}
        self.chan_count = {}
        self.same_engine_sync = same_engine_sync
        self.nops = 0

    @staticmethod
    def _flat(x):
        out = []
        for b in x:
            if isinstance(b, (list, tuple)):
                out.extend(Prog._flat(b))
            else:
                out.append(b)
        return out

    def _track(self, op, reads, writes):
        reads = self._flat(reads)
        writes = self._flat(writes)
        writes = writes + [b for b in reads if b.excl]
        reads = [b for b in reads if not b.excl]
        deps = []
        for b in reads:
            if b.w is not None:
                deps.append(b.w)
        for b in writes:
            if b.w is not None:
                deps.append(b.w)
            deps.extend(b.rs.values())
        for b in writes:
            b.w = op
            b.rs = {}
        for b in reads:
            key = op.chan if op.chan is not None else op.eng
            b.rs[key] = op
        seen = set()
        for d in deps:
            if d is op or id(d) in seen:
                continue
            seen.add(id(d))
            if d.chan is None and d.eng == op.eng:
                if op.eng == "pe" or not self.same_engine_sync:
                    continue
                if op.chan is not None:
                    pass
            d.sig = True
            op.deps.append(d)

    def op(self, eng, emit, reads=(), writes=()):
        o = Op(eng, _freeze(emit))
        o.idx = self.nops
        self.nops += 1
        self._track(o, reads, writes)
        self.ops[eng].append(o)
        return o

    def dma(self, queue, emit, chan, reads=(), writes=()):
        o = Op(queue, _freeze(emit), chan=chan)
        o.idx = self.nops
        self.nops += 1
        self._track(o, reads, writes)
        c = self.chan_count.get(chan, 0) + 16
        self.chan_count[chan] = c
        o.val = c
        o.sig = True
        self.ops[queue].append(o)
        return o

    def coll(self, emit, chan, reads=(), writes=()):
        o = Op("pool", _freeze(emit), chan=chan)
        o.idx = self.nops
        self.nops += 1
        self._track(o, reads, writes)
        c = self.chan_count.get(chan, 0) + 1
        self.chan_count[chan] = c
        o.val = c
        o.sig = True
        o.inc = 1
        self.ops["pool"].append(o)
        return o

    def emit_all(self, nc, final_chans):
        with ExitStack() as es:
            esem = {e: es.enter_context(nc.semaphore("s_" + e)) for e in self.ENGS}
            csem = {c: es.enter_context(nc.semaphore("c_" + c)) for c in self.chan_count}
            for e in self.ENGS:
                cnt = 0
                for o in self.ops[e]:
                    if o.chan is None and o.sig:
                        cnt += 1
                        o.val = cnt

            def run(eng_name, eng):
                seen = {}
                for o in self.ops[eng_name]:
                    for d in o.deps:
                        if d.chan is not None:
                            key, sem = ("c", d.chan), csem[d.chan]
                        else:
                            key, sem = ("e", d.eng), esem[d.eng]
                        if seen.get(key, 0) >= d.val:
                            continue
                        seen[key] = d.val
                        eng.wait_ge(sem, d.val)
                    ins = o.emit(eng)
                    if o.chan is not None:
                        ins.then_inc(csem[o.chan], o.inc)
                    elif o.sig:
                        ins.then_inc(esem[eng_name], 1)
                if eng_name == "sp":
                    for c in final_chans:
                        eng.wait_ge(csem[c], self.chan_count[c])

            with nc.Block() as block:
                @block.tensor
                def _(e):
                    run("pe", e)

                @block.scalar
                def _(e):
                    run("act", e)

                @block.vector
                def _(e):
                    run("dve", e)

                @block.gpsimd
                def _(e):
                    run("pool", e)

                @block.sync
                def _(e):
                    run("sp", e)


VEC_COLS = {}
_c = 0
for _name, _n in [("mix0", 16), ("mix1", 16), ("mlp0", 16), ("mlp1", 16), ("ple0", 16), ("ple1", 16),
                  ("kvn", 16), ("fin", 16), ("lb0", 16), ("lb1", 16), ("hg", 1), ("bf", 1),
                  ("flag", 1), ("prevbias", 1)]:
    VEC_COLS[_name] = (_c, _n)
    _c += _n
NVEC = _c


def pack_vecs(inp, rank):
    v = np.zeros((128, NVEC), np.float32)

    def put(name, arr):
        c0, n = VEC_COLS[name]
        v[:, c0:c0 + n] = np.asarray(arr, np.float32).reshape(n, 128).T

    put("mix0", inp["mix_norm"][0]); put("mix1", inp["mix_norm"][1])
    put("mlp0", inp["mlp_norm"][0]); put("mlp1", inp["mlp_norm"][1])
    put("ple0", inp["ple_norm"][0]); put("ple1", inp["ple_norm"][1])
    put("kvn", inp["kv_norm"]); put("fin", inp["final_norm"])
    put("lb0", inp["a_lb_logits"][0]); put("lb1", inp["a_lb_logits"][1])
    put("hg", inp["a_head_gain"][0])
    c0, _ = VEC_COLS["bf"]
    v[:16, c0] = np.asarray(inp["b_f"], np.float32)
    odd = rank % 2
    v[:, VEC_COLS["flag"][0]] = 1.0 if odd else 0.0
    v[:, VEC_COLS["prevbias"][0]] = 0.0 if odd else NEG
    return v


def pack_slab(inp, spec):
    kind = spec[0]
    if kind == "std":
        _, name, layer, r0, c0 = spec
        W = inp[name] if layer is None else inp[name][layer]
        blk = np.asarray(W[r0:r0 + 2048, c0:c0 + 128], np.float32)
        return blk.reshape(16, 128, 128).transpose(1, 0, 2).reshape(128, 2048)
    if kind == "ple":
        _, layer, o8 = spec
        W = np.asarray(inp["w_ple_up"][layer][:, o8 * 1024:(o8 + 1) * 1024], np.float32)
        blk = W.reshape(2, 128, 8, 128).transpose(1, 2, 0, 3)
        return blk.reshape(128, 2048)
    if kind == "flog":
        W = np.asarray(inp["w_kvf"][:, 4096:4112], np.float32)
        out = np.zeros((128, 2048), np.float32)
        out[:, :256] = W.reshape(16, 128, 16).transpose(1, 0, 2).reshape(128, 256)
        return out
    raise ValueError(kind)


class Builder:
    def __init__(self, segs, fused):
        self.segs = segs
        self.fused = fused
        self.plan = []
        self.nc = bass.Bass("TRN2", target_bir_lowering=False)
        self.P = Prog()
        self.es = ExitStack()
        self.final_chans = []
        self._n = 0

    def sb(self, name, shape, dt):
        return self.es.enter_context(self.nc.sbuf_tensor(name, shape, dt))

    def dram(self, name, shape, dt, kind):
        if kind == "Internal":
            return self.nc.dram_tensor(name, shape, dt).ap()
        return self.nc.dram_tensor(name, shape, dt, kind=kind).ap()

    def uniq(self, s):
        self._n += 1
        return f"{s}{self._n}"

    def wplan(self, specs):
        i0 = len(self.plan)
        self.plan.extend(specs)
        return i0

    def wacquire(self, idx):
        hi = min(len(self.plan), idx + NSLOT)
        while self.w_issued < hi:
            j = self.w_issued
            s = j % NSLOT
            src = self.walls[j // self.wch][j % self.wch]
            dst = self.wslot[s]
            self.P.dma("pool", lambda e, dst=dst, src=src: e.dma_start(out=dst[:, :], in_=src),
                       chan=f"w{s}", writes=[self.wbuf[s]])
            self.w_issued += 1
        s = idx % NSLOT
        return self.wslot[s], self.wbuf[s]


def make_plan(segs):
    plan = []

    def mlp(l):
        for g in range(4):
            for j in range(16):
                plan.append(("std", "w_mlp_up", l, 0, (g * 16 + j) * 128))
            for o in range(16):
                plan.append(("std", "w_mlp_down", l, g * 2048, o * 128))

    def ple(l):
        for o in range(16):
            plan.append(("std", "w_ple_gate", l, 0, o * 128))

    for s in segs:
        if s == "S1":
            for h in range(NH):
                for j in (1, 0, 2, 3):
                    plan.append(("std", "w_a_in", 0, 0, j * 2048 + h * 128))
        elif s == "S2a":
            for o in range(16):
                plan.append(("std", "w_a_out", 0, 0, o * 128))
        elif s == "S2b":
            mlp(0); ple(0)
        elif s == "S2c":
            plan.append(("flog",))
            for h in range(NH):
                plan.append(("std", "w_kvf", None, 0, h * 128))
            for h in range(NH):
                plan.append(("std", "w_kvf", None, 0, 2048 + h * 128))
        elif s == "S3a":
            for h in range(NH):
                plan.append(("std", "w_b_q", 0, 0, h * 128))
            for o in range(16):
                plan.append(("std", "w_b_out", 0, 0, o * 128))
        elif s == "S3b":
            mlp(1); ple(1)
    return plan


class Arena:
    PAGE = 256

    def __init__(self, b, name, nbytes):
        self.t = b.sb(name, [128, nbytes // 4], F32)
        self.pages = [Buf(f"{name}_pg{i}") for i in range(nbytes // self.PAGE)]
        self.nbytes = nbytes

    def view(self, off, nelem, dt, parts=128):
        esz = 4 if dt == F32 else 2
        nb = nelem * esz
        assert off % 4 == 0 and off + nb <= self.nbytes, (off, nb, self.nbytes)
        ap = self.t[0:parts, off // 4:(off + nb + 3) // 4]
        if dt != F32:
            ap = ap.bitcast(dt)
        pgs = self.pages[off // self.PAGE:(off + nb - 1) // self.PAGE + 1]
        return ap, pgs


def build(segs, fused=False):
    b = Builder(segs, fused)
    nc, P = b.nc, b.P
    b.plan = make_plan(segs)
    nslab = max(1, len(b.plan))
    WCH = 224
    b.walls = [b.dram(f"wall{i}", [min(WCH, nslab - i * WCH), 128, 2048], F32, "ExternalInput") for i in range((nslab + WCH - 1) // WCH)]
    b.wch = WCH
    vecs_d = b.dram("vecs", [128, NVEC], F32, "ExternalInput")
    hin_d = b.dram("hin", [D, T], F32, "ExternalInput")
    hout_d = b.dram("hout", [D, T], F32, "ExternalOutput")
    pT_d = b.dram("pT", [2, PLE, T], F32, "ExternalInput")
    wple_d = b.dram("wple", [2, 128, 2 * D], F32, "ExternalInput")
    io = {}

    def xio(name, shape, dt, produced):
        kind = "Internal" if fused else ("ExternalOutput" if produced else "ExternalInput")
        io[name] = b.dram(name, shape, dt, kind)
        return io[name]

    PAIRS = [[2 * i, 2 * i + 1] for i in range(4)]
    dbufs = {}

    def dbuf(name):
        if name not in dbufs:
            dbufs[name] = Buf("dram_" + name)
        return dbufs[name]

    def allgather(name, src2d, rows, cols, dt, src_bufs):
        dst = b.dram(name + "_all", [2 * rows, cols], dt, "Internal")
        P.coll(lambda e: e.collective_compute("AllGather", ALU.bypass, replica_groups=PAIRS, ins=[src2d.opt()], outs=[dst.opt()]),
               chan="cc_" + name, reads=src_bufs, writes=[dbuf(name + "_all")])
        return dst[0:rows, :]

    has = lambda s: s in segs
    DBG = getattr(build, "DBG", False)

    def dbgout(name, ap, reads, shape, dt):
        if not DBG:
            return
        dd = b.dram("dbg_" + name, shape, dt, "ExternalOutput")
        P.dma("sp", lambda e: e.dma_start(out=dd, in_=ap), chan="out_dbg", reads=reads)
    if has("S2c") or has("S3a"):
        Kown_d = xio("Kown", [NH, 128, T], BF16, has("S2c"))
        Vown_d = xio("Vown", [NH, 128, 8 * 128], BF16, has("S2c"))
        Down_d = xio("Dloc", [NH, T], F32, has("S2c"))
    if has("S3a") and not fused:
        Kprev_d = xio("Kprev", [NH, 128, T], BF16, False)
        Vprev_d = xio("Vprev", [NH, 128, 8 * 128], BF16, False)
        Dprev_d = xio("Dprev", [NH, T], F32, False)
    fz = {}

    hT = b.sb("hT", [128, NCH, T], F32)
    hbuf = [Buf(f"h{c}") for c in range(NCH)]
    uT = b.sb("uT", [128, NCH, T], BF16)
    ubuf = [Buf(f"u{c}") for c in range(NCH)]
    A = Arena(b, "A", 32 * 1024)
    X = Arena(b, "X", 24 * 1024)
    b.wslot = [b.sb(f"w{s}", [128, 2048], BF16) for s in range(NSLOT)]
    b.wbuf = [Buf(f"w{s}") for s in range(NSLOT)]
    b.w_issued = 0
    vecs = b.sb("vecs_sb", [128, NVEC], F32); vecbuf = Buf("vecs")
    ones_b = b.sb("ones_b", [128, 128], BF16); onesbuf = Buf("ones")
    ident_b = b.sb("ident_b", [128, 128], BF16); identbuf = Buf("ident")
    ident_f = b.sb("ident_f", [128, 128], F32); identfbuf = Buf("identf")
    rstd = b.sb("rstd", [128, T], F32); rstdbuf = Buf("rstd")
    epsc = b.sb("epsc", [128, 2], F32); epsb = Buf("epsc")
    sq = [b.sb(f"sq{i}", [128, 512], BF16) for i in range(3)]
    sqbuf = [Buf(f"sq{i}") for i in range(3)]
    tmp = [b.sb(f"tmp{i}", [128, 512], F32) for i in range(4)]
    tmpbuf = [Buf(f"tmp{i}") for i in range(4)]
    ps = [b.es.enter_context(nc.psum_tensor(f"ps{i}", [128, 512], F32)) for i in range(8)]
    psbuf = [[Buf(f"ps{i}", excl=True)] for i in range(8)]
    Y = Arena(b, "Y", 14080)
    cnt = {"tmp": 0, "sq": 0, "par": 0}

    def vcol(name, i=0):
        c0, _ = VEC_COLS[name]
        return vecs[:, c0 + i:c0 + i + 1]

    P.dma("sp", lambda e: e.dma_start(out=vecs[:, :], in_=vecs_d), chan="misc", writes=[vecbuf])
    P.op("pool", lambda e: e.memset(ones_b[:, :], 1.0), writes=[onesbuf])
    P.op("pool", lambda e: e.memset(epsc[:, 0:1], EPS), writes=[epsb])
    P.op("pool", lambda e: e.memset(epsc[:, 1:2], 1.0), writes=[epsb])
    zf = tmp[0]
    P.op("pool", lambda e: e.memset(zf[:, 0:128], 0.0), writes=[tmpbuf[0]])
    P.op("pool", lambda e: e.affine_select(out=ident_f[:, :], in_=zf[:, 0:128], pattern=[[-1, 128]],
                                           compare_op=ALU.not_equal, fill=1.0, base=0, channel_multiplier=1),
         reads=[tmpbuf[0]], writes=[identfbuf])
    P.op("pool", lambda e: e.tensor_copy(out=ident_b[:, :], in_=ident_f[:, :]), reads=[identfbuf], writes=[identbuf])
    for c in range(NCH):
        P.dma(("sp", "act")[c % 2], lambda e, c=c: e.dma_start(out=hT[:, c, :], in_=hin_d[c * 128:(c + 1) * 128, :]),
              chan=f"hin{c}", writes=[hbuf[c]])

    pending = []

    def flush_stats():
        while pending:
            stats_chunk(pending.pop(0))

    def stats_chunk(c, sb=6):
        for hf in range(2):
            i = cnt["sq"] % 3; cnt["sq"] += 1
            P.op("act", lambda e: e.activation(out=sq[i][:, :], in_=hT[:, c, hf * 512:(hf + 1) * 512], func=AF.Square),
                 reads=[hbuf[c]], writes=[sqbuf[i]])
            P.op("pe", lambda e: e.matmul(ps[sb + hf][:, :], lhsT=ones_b[:, :], rhs=sq[i][:, :], start=(c == 0), stop=(c == NCH - 1)),
                 reads=[sqbuf[i], onesbuf], writes=[psbuf[sb + hf]])

    def rmsnorm(gname, out_f32_inplace=False, stats="compute"):
        sb = 4
        if stats == "compute":
            for c in range(NCH):
                stats_chunk(c, sb=4)
        elif stats == "piped":
            flush_stats()
            sb = 6
        if stats != "reuse":
            for hf in range(2):
                P.op("act", lambda e, hf=hf: e.activation(out=rstd[:, hf * 512:(hf + 1) * 512], in_=ps[sb + hf][:, :], func=AF.Ln, scale=1.0 / D, bias=epsc[:, 0:1]),
                     reads=[psbuf[sb + hf], epsb], writes=[rstdbuf])
            P.op("act", lambda e: e.activation(out=rstd[:, :], in_=rstd[:, :], func=AF.Exp, scale=-0.5), reads=[rstdbuf], writes=[rstdbuf])
        for c in range(NCH):
            if out_f32_inplace:
                P.op("dve", lambda e, c=c: e.scalar_tensor_tensor(out=hT[:, c, :], in0=hT[:, c, :], scalar=vcol(gname, c), in1=rstd[:, :],
                                                                  op0=ALU.mult, op1=ALU.mult),
                     reads=[hbuf[c], rstdbuf, vecbuf], writes=[hbuf[c]])
            else:
                P.op("dve", lambda e, c=c: e.scalar_tensor_tensor(out=uT[:, c, :], in0=hT[:, c, :], scalar=vcol(gname, c), in1=rstd[:, :],
                                                                  op0=ALU.mult, op1=ALU.mult),
                     reads=[hbuf[c], rstdbuf, vecbuf], writes=[ubuf[c]])

    widx = {"i": 0}

    def dense_gen(rhs, kch, epi, mcols=128, group=None):
        idx = widx["i"]; widx["i"] += 1
        slot, wb = b.wacquire(idx)
        w3 = slot[:, :].rearrange("p (k n) -> p k n", k=16) if mcols == 128 else slot[:, 0:16 * mcols].rearrange("p (k n) -> p k n", k=16)
        par = 0 if cnt.get("single") else cnt["par"] % 2
        cnt["par"] += 1
        n = 0
        group = group or cnt.get("group", 4)
        for k in range(kch):
            rap, rb = rhs(k)
            for hf in range(2):
                pi = 2 * par + hf
                P.op("pe", lambda e: e.matmul(ps[pi][0:mcols, :], lhsT=w3[:, k, :], rhs=rap[:, hf * 512:(hf + 1) * 512],
                                              start=(k == 0), stop=(k == kch - 1)),
                     reads=[wb] + rb, writes=[psbuf[pi]])
                n += 1
                if n % group == 0 and n < 2 * kch:
                    yield
        flush_stats()
        for hf in range(2):
            pi = 2 * par + hf
            epi(hf, ps[pi], psbuf[pi])

    def dense(rhs, kch, epi, mcols=128):
        for _ in dense_gen(rhs, kch, epi, mcols=mcols):
            pass

    def rhs_u(k):
        return uT[:, k, :], [ubuf[k]]

    def newtmp():
        i = cnt["tmp"] % 4; cnt["tmp"] += 1
        return tmp[i], tmpbuf[i]

    def epi_resid(o, final=False):
        def f(hf, p, pb):
            P.op("dve", lambda e: e.tensor_tensor(out=hT[:, o, hf * 512:(hf + 1) * 512], in0=hT[:, o, hf * 512:(hf + 1) * 512], in1=p[:, :], op=ALU.add),
                 reads=[pb, hbuf[o]], writes=[hbuf[o]])
            if final and hf == 1:
                pending.append(o)
        return f

    def mlp():
        hid = [A.view(j * 2048, T, BF16) for j in range(16)]

        def rhs_a(k):
            return hid[k][0], hid[k][1]
        for g in range(4):
            for j in range(16):
                def epi(hf, p, pb, j=j):
                    t, tb = newtmp()
                    P.op("act", lambda e: e.activation(out=t[:, :], in_=p[:, :], func=AF.Relu), reads=[pb], writes=[tb])
                    P.op("dve", lambda e: e.tensor_tensor(out=hid[j][0][:, hf * 512:(hf + 1) * 512], in0=t[:, :], in1=t[:, :], op=ALU.mult),
                         reads=[tb], writes=hid[j][1])
                dense(rhs_u, 16, epi)
            for o in range(16):
                dense(rhs_a, 16, epi_resid(o, final=(g == 3)))

    def ple(l):
        pT_sb, pTb = X.view(0, 2 * T, BF16)
        pT3 = pT_sb.rearrange("p (k t) -> p k t", k=2)
        wp_sb, wpb = X.view(4096, 2 * D, BF16)
        wp3 = wp_sb.rearrange("p (k n) -> p k n", k=2)
        P.dma("pool", lambda e: e.dma_start(out=pT3, in_=pT_d[l].rearrange("(k p) t -> p k t", p=128)), chan="pT", writes=pTb)
        for k in range(2):
            P.dma("pool", lambda e, k=k: e.dma_start(out=wp3[:, k, :], in_=wple_d[l][:, k * D:(k + 1) * D]), chan="wple", writes=wpb)
        for o in range(16):
            def epi(hf, p, pb, o=o):
                pp = ps[4 + hf]; ppb = psbuf[4 + hf]
                for k in range(2):
                    P.op("pe", lambda e, k=k: e.matmul(pp[:, :], lhsT=wp3[:, k, o * 128:(o + 1) * 128], rhs=pT3[:, k, hf * 512:(hf + 1) * 512],
                                                       start=(k == 0), stop=(k == 1)), reads=wpb + pTb, writes=[ppb])
                t, tb = newtmp()
                P.op("act", lambda e: e.activation(out=t[:, :], in_=p[:, :], func=AF.Sigmoid), reads=[pb], writes=[tb])
                t2, tb2 = newtmp()
                P.op("dve", lambda e: e.tensor_tensor(out=t2[:, :], in0=t[:, :], in1=pp[:, :], op=ALU.mult), reads=[tb, ppb], writes=[tb2])
                P.op("dve", lambda e: e.tensor_tensor(out=hT[:, o, hf * 512:(hf + 1) * 512], in0=hT[:, o, hf * 512:(hf + 1) * 512], in1=t2[:, :], op=ALU.add),
                     reads=[tb2, hbuf[o]], writes=[hbuf[o]])
                if hf == 1:
                    pending.append(o)
            dense(rhs_u, 16, epi)


    negD_, negDb = Y.view(0, 16 * NH, F32)
    negD = negD_.rearrange("p (j h) -> p j h", j=16)
    Dsb, Dsbb = Y.view(2048, 2 * T, F32, parts=16)
    dadd, daddb = Y.view(10240, 1, F32, parts=16)
    Dbf, Dbfb = Y.view(10496, T, BF16)
    sel_, selb = X.view(20480, NH * 128, BF16)
    sel = sel_.rearrange("p (h m) -> p h m", h=NH)
    dprep_done = []

    def d_prep():
        dprep_done.append(1)
        Dp = fz["Dprev"] if fused else Dprev_d
        P.dma("sp", lambda e: e.dma_start(out=Dsb[:, 0:T], in_=Dp), chan="dld", reads=[dbuf("D_all")], writes=[Dsbb])
        P.dma("sp", lambda e: e.dma_start(out=Dsb[:, T:2 * T], in_=Down_d), chan="dld", reads=[dbuf("D")], writes=[Dsbb])
        P.op("dve", lambda e: e.tensor_tensor(out=dadd[:, :], in0=Dsb[:, T - 1:T], in1=vcol("flag")[0:16, :], op=ALU.mult), reads=[Dsbb, vecbuf], writes=[daddb])
        P.op("dve", lambda e: e.tensor_scalar(out=Dsb[:, T:2 * T], in0=Dsb[:, T:2 * T], scalar1=dadd[:, 0:1], scalar2=None, op0=ALU.add),
             reads=[Dsbb, daddb], writes=[Dsbb])
        P.op("pool", lambda e: e.memset(Dbf, 0.0), writes=Dbfb)
        P.op("dve", lambda e: e.tensor_copy(out=Dbf[0:16, :], in_=Dsb[:, T:2 * T]), reads=[Dsbb] + Dbfb, writes=Dbfb)
        P.op("dve", lambda e: e.tensor_copy(out=sel, in_=ident_b[:, 0:16].unsqueeze(2).broadcast_to([128, NH, 128])), reads=[identbuf], writes=selb)
        for j in range(16):
            pj = ps[6 + j % 2]
            P.op("pe", lambda e, j=j, pj=pj: e.transpose(out=pj[:, 0:16], in_=Dsb[:, j * 128:(j + 1) * 128], identity=ident_f[0:16, 0:16]),
                 reads=[Dsbb, identfbuf], writes=[psbuf[6 + j % 2]])
            if j < 8:
                P.op("dve", lambda e, j=j, pj=pj: e.tensor_scalar(out=negD[:, j, :], in0=pj[:, 0:16], scalar1=-1.0, scalar2=vcol("prevbias"),
                                                                  op0=ALU.mult, op1=ALU.add), reads=[psbuf[6 + j % 2], vecbuf], writes=[negDb])
            else:
                P.op("dve", lambda e, j=j, pj=pj: e.tensor_scalar(out=negD[:, j, :], in0=pj[:, 0:16], scalar1=-1.0, scalar2=None, op0=ALU.mult),
                     reads=[psbuf[6 + j % 2]], writes=[negDb])

    def kv_phase():
        rmsnorm("kvn", stats="piped" if has("S2b") else "compute")
        kst = [X.view(i * 2048, T, BF16) for i in range(2)]
        vst = [X.view(4096 + i * 2048, T, BF16) for i in range(2)]
        vtm = [X.view(8192 + i * 2048, T, BF16) for i in range(2)]
        ones_f, onesfb = X.view(12288, T, F32)
        sp1, sp1b = X.view(16384, T, F32)
        dl, dlb = X.view(20480, T, F32)
        nbf, nbfb = tmp[3][:, 0:1], tmpbuf[3]
        P.op("pool", lambda e: e.memset(ones_f[:, :], 1.0), writes=onesfb)
        P.op("dve", lambda e: e.tensor_scalar(out=nbf[0:16, :], in0=vcol("bf")[0:16, :], scalar1=-1.0, scalar2=None, op0=ALU.mult),
             reads=[vecbuf], writes=[nbfb])

        def epi(hf, p, pb):
            sl = slice(hf * 512, (hf + 1) * 512)
            P.op("act", lambda e: e.activation(out=sp1[0:16, sl], in_=p[0:16, :], func=AF.Exp, scale=-1.0, bias=nbf[0:16, :]),
                 reads=[pb, nbfb], writes=sp1b)
            P.op("act", lambda e: e.activation(out=sp1[0:16, sl], in_=sp1[0:16, sl], func=AF.Ln, bias=epsc[0:16, 1:2]), reads=sp1b + [epsb], writes=sp1b)
            if hf == 1:
                P.op("dve", lambda e: e.tensor_tensor_scan(out=dl[0:16, :], data0=ones_f[0:16, :], data1=sp1[0:16, :], initial=0.0,
                                                           op0=ALU.mult, op1=ALU.add), reads=sp1b + onesfb, writes=dlb)
                P.op("dve", lambda e: e.tensor_scalar(out=dl[0:16, :], in0=dl[0:16, :], scalar1=-1.0, scalar2=None, op0=ALU.mult), reads=dlb, writes=dlb)
                P.dma("sp", lambda e: e.dma_start(out=Down_d, in_=dl[0:16, :]), chan="out_d", reads=dlb, writes=[dbuf("D")])
        dense(rhs_u, 16, epi, mcols=16)
        if fused:
            fz["Dprev"] = allgather("D", Down_d, NH, T, F32, [dbuf("D")])
        if fused:
            d_prep()
        for h in range(NH):
            def epi(hf, p, pb, h=h):
                k_ap, kb = kst[h % 2]
                P.op("act", lambda e: e.activation(out=k_ap[:, hf * 512:(hf + 1) * 512], in_=p[:, :], func=AF.Copy), reads=[pb], writes=kb)
                if hf == 1:
                    P.dma("sp", lambda e: e.dma_start(out=Kown_d[h], in_=k_ap), chan=f"out_k{h % 2}", reads=kb, writes=[dbuf(f"K{h}")])
            dense(rhs_u, 16, epi)
            if fused and h % 4 == 3:
                g = h // 4
                fz[f"Kprev{g}"] = allgather(f"K{g}_", Kown_d[4 * g:4 * g + 4].rearrange("h p t -> (h p) t"), 512, T, BF16,
                                            [dbuf(f"K{hh}") for hh in range(4 * g, 4 * g + 4)]).rearrange("(h p) t -> h p t", h=4)
        for h in range(NH):
            def epi(hf, p, pb, h=h):
                v_ap, vb = vst[h % 2]
                t_ap, tb = vtm[h % 2]
                P.op("act", lambda e: e.activation(out=v_ap[:, hf * 512:(hf + 1) * 512], in_=p[:, :], func=AF.Copy), reads=[pb], writes=vb)
                pt = ps[6 + hf][:, :].bitcast(BF16)
                for j in range(4):
                    jj = hf * 4 + j
                    P.op("pe", lambda e, j=j, jj=jj: e.transpose(out=pt[:, j * 128:(j + 1) * 128], in_=v_ap[:, jj * 128:(jj + 1) * 128], identity=ident_b[:, :]),
                         reads=vb + [identbuf], writes=[psbuf[6 + hf]])
                P.op("dve", lambda e: e.tensor_copy(out=t_ap[:, hf * 512:(hf + 1) * 512], in_=pt[:, 0:512]), reads=[psbuf[6 + hf]], writes=tb)
                if hf == 1:
                    P.dma("sp", lambda e: e.dma_start(out=Vown_d[h], in_=t_ap), chan=f"out_v{h % 2}", reads=tb, writes=[dbuf(f"V{h}")])
            dense(rhs_u, 16, epi)
            if fused and h % 4 == 3:
                g = h // 4
                fz[f"Vprev{g}"] = allgather(f"V{g}_", Vown_d[4 * g:4 * g + 4].rearrange("h p t -> (h p) t"), 512, T, BF16,
                                            [dbuf(f"V{hh}") for hh in range(4 * g, 4 * g + 4)]).rearrange("(h p) t -> h p t", h=4)

    def attn_phase():
        rmsnorm("mix1", stats="reuse" if fused else "compute")
        kh = [X.view(i * 4096, 2 * T, BF16) for i in range(2)]
        vh = [X.view(8192 + i * 4096, 2 * T, BF16) for i in range(2)]
        qh = [X.view(16384 + i * 2048, T, BF16) for i in range(2)]
        masks = rstd[:, :].bitcast(BF16).rearrange("p (r t) -> p r t", r=4)
        maskb = [rstdbuf]
        pt, ptb = sq, sqbuf
        rden, rdenb = tmp[2], tmpbuf[2]
        og = [A.view(h * 2048, T, BF16) for h in range(NH)]
        P.op("pool", lambda e: e.memset(masks[:, :, :], 0.0), writes=maskb)
        for r in range(4):
            P.op("pool", lambda e, r=r: e.affine_select(out=masks[:, r, :], in_=masks[:, r, :], pattern=[[1, 512]], compare_op=ALU.is_ge,
                                                       fill=NEG, base=-128 * r, channel_multiplier=-1), reads=maskb, writes=maskb)
        if not dprep_done:
            d_prep()
        dbgout("negD", negD_, negDb, [128, 16 * NH], F32)
        dbgout("sel", sel_, selb, [128, NH * 128], BF16)
        dbgout("Dbf", Dbf, Dbfb, [128, T], BF16)
        dbgout("Dsb", Dsb, Dsbb, [16, 2 * T], F32)
        dbgout("masks", rstd[:, :], maskb, [128, T], F32)
        npt = [0]
        cnt["single"] = True
        SB = (6, 7, 2, 3)
        NHL = getattr(build, "NHL", NH)
        for h in range(NH):
            s = h % 2
            k_ap, kb = kh[s]; v_ap, vb = vh[s]; q_ap, qb_ = qh[s]
            if h >= NHL:
                def epi0(hf, p, pb, h=h):
                    P.op("act", lambda e: e.activation(out=og[h][0][:, hf * 512:(hf + 1) * 512], in_=p[:, :], func=AF.Copy), reads=[pb], writes=og[h][1])
                dense(rhs_u, 16, epi0)
                continue
            v3 = v_ap.rearrange("p (j e) -> p j e", j=16)
            Kph = fz[f"Kprev{h // 4}"][h % 4] if fused else Kprev_d[h]
            Vph = fz[f"Vprev{h // 4}"][h % 4] if fused else Vprev_d[h]
            P.dma("sp", lambda e, h=h, k_ap=k_ap: e.dma_start(out=k_ap[:, 0:T], in_=Kph), chan=f"kld{s}", reads=[dbuf(f"K{h // 4}__all")], writes=kb)
            P.dma("sp", lambda e, h=h, k_ap=k_ap: e.dma_start(out=k_ap[:, T:2 * T], in_=Kown_d[h]), chan=f"kld{s}", reads=[dbuf(f"K{h}")], writes=kb)
            P.dma("sp", lambda e, h=h, v_ap=v_ap: e.dma_start(out=v_ap[:, 0:T], in_=Vph), chan=f"vld{s}", reads=[dbuf(f"V{h // 4}__all")], writes=vb)
            P.dma("sp", lambda e, h=h, v_ap=v_ap: e.dma_start(out=v_ap[:, T:2 * T], in_=Vown_d[h]), chan=f"vld{s}", reads=[dbuf(f"V{h}")], writes=vb)

            def epi(hf, p, pb, q_ap=q_ap, qb_=qb_):
                P.op("act", lambda e: e.activation(out=q_ap[:, hf * 512:(hf + 1) * 512], in_=p[:, :], func=AF.Copy, scale=HD ** -0.5), reads=[pb], writes=qb_)
            dense(rhs_u, 16, epi)
            if h == 0:
                dbgout("kh", k_ap, kb, [128, 2 * T], BF16)
                dbgout("vh", v_ap, vb, [128, 2 * T], BF16)
                dbgout("qh", q_ap, qb_, [128, T], BF16)
            for qb in range(2):
                qs = slice(qb * 512, (qb + 1) * 512)
                tiles = [(j, None) for j in range(8)]
                for jj in range(4 * qb + 4):
                    tiles.append((8 + jj, jj - 4 * qb if jj >= 4 * qb else None))
                po, pd = ps[4], ps[5]
                ptidx = {}

                def qk(ti):
                    j, mr = tiles[ti]
                    pS = ps[SB[ti % 4]]; pSb = psbuf[SB[ti % 4]]
                    P.op("pe", lambda e: e.matmul(pS[:, :], lhsT=k_ap[:, j * 128:(j + 1) * 128], rhs=q_ap[:, qs], start=True, stop=False),
                         reads=kb + qb_, writes=[pSb])
                    P.op("pe", lambda e: e.matmul(pS[:, :], lhsT=sel[:, h, :], rhs=Dbf[:, qs], start=False, stop=True),
                         reads=selb + Dbfb, writes=[pSb])
                    if mr is not None:
                        P.op("dve", lambda e: e.tensor_tensor(out=pS[:, :], in0=pS[:, :], in1=masks[:, mr, :], op=ALU.add),
                             reads=[pSb] + maskb, writes=[pSb])

                def ex(ti):
                    j, mr = tiles[ti]
                    pS = ps[SB[ti % 4]]; pSb = psbuf[SB[ti % 4]]
                    i = npt[0] % 3; npt[0] += 1
                    ptidx[ti] = i
                    P.op("act", lambda e: e.activation(out=pt[i][:, :], in_=pS[:, :], func=AF.Exp, bias=negD[:, j, h:h + 1], scale=1.0),
                         reads=[pSb, negDb], writes=[ptb[i]])

                def pv(ti):
                    j, mr = tiles[ti]
                    i = ptidx[ti]
                    first, last = ti == 0, ti == len(tiles) - 1
                    P.op("pe", lambda e: e.matmul(po[:, :], lhsT=v3[:, j, :], rhs=pt[i][:, :], start=first, stop=last),
                         reads=vb + [ptb[i]], writes=[psbuf[4]])
                    P.op("pe", lambda e: e.matmul(pd[:, :], lhsT=ones_b[:, :], rhs=pt[i][:, :], start=first, stop=last),
                         reads=[onesbuf, ptb[i]], writes=[psbuf[5]])
                qk(0)
                if len(tiles) > 1:
                    qk(1)
                for ti in range(len(tiles)):
                    if ti + 2 < len(tiles):
                        qk(ti + 2)
                    ex(ti)
                    pv(ti)
                P.op("dve", lambda e: e.reciprocal(out=rden[:, :], in_=pd[:, :]), reads=[psbuf[5]], writes=[rdenb])
                P.op("dve", lambda e, h=h: e.tensor_tensor(out=og[h][0][:, qs], in0=po[:, :], in1=rden[:, :], op=ALU.mult),
                     reads=[psbuf[4], rdenb], writes=og[h][1])
                if h == 0 and qb == 0:
                    dbgout("rden", rden[:, :], [rdenb], [128, 512], F32)
                if h == 0 and qb == 1:
                    dbgout("og0", og[0][0], og[0][1], [128, T], BF16)

        cnt["single"] = False

        def rhs_og(k):
            return og[k][0], og[k][1]
        for o in range(16):
            dense(rhs_og, 16, epi_resid(o, final=True))


    if has("S1") or has("S2a"):
        OL_d = xio("OL", [NH, 128, T], F32, has("S1"))
        QC_d = xio("QC", [NH, 128, T], BF16, has("S1"))
        G_d = xio("G", [NH, 128, T], BF16, has("S1"))
    if has("S1"):
        SF_d = xio("SF", [NH, 128, 128], F32, True)
    if has("S2a") and not fused:
        SFp_d = xio("SFprev", [NH, 128, 128], F32, False)

    def hgrn1():
        rmsnorm("mix0")
        cnt["single"] = True
        cnt["group"] = 2
        lbv, lbb = Y.view(0, 16, F32)
        oml, omlb = Y.view(64, 16, F32)
        noml, nomlb = Y.view(128, 16, F32)
        dec = [Y.view(256 + i * 256, NCHUNK, F32) for i in range(2)]
        attm = [Y.view(768 + i * 256, 128, BF16) for i in range(2)]
        S32 = [Y.view(1280, 128, F32), Y.view(13568, 128, F32)]
        m01, m01b = Y.view(1792, 128, F32)
        cmask, cmb = Y.view(2304, T, F32)
        onesf, onfb = Y.view(6400, T, F32)
        S16 = [Y.view(10496 + i * 256, 128, BF16) for i in range(12)]
        constb = lbb
        P.op("dve", lambda e: e.tensor_tensor(out=lbv, in0=vecs[:, VEC_COLS["lb0"][0]:VEC_COLS["lb0"][0] + 16],
                                              in1=vecs[:, VEC_COLS["lb1"][0]:VEC_COLS["lb1"][0] + 16], op=ALU.subtract), reads=[vecbuf], writes=constb)
        P.op("act", lambda e: e.activation(out=lbv, in_=lbv, func=AF.Sigmoid), reads=constb, writes=constb)
        P.op("dve", lambda e: e.tensor_scalar(out=oml, in0=lbv, scalar1=-1.0, scalar2=1.0, op0=ALU.mult, op1=ALU.add), reads=constb, writes=constb)
        P.op("dve", lambda e: e.tensor_scalar(out=noml, in0=lbv, scalar1=-1.0, scalar2=None, op0=ALU.add), reads=constb, writes=constb)
        P.op("pool", lambda e: e.memset(onesf, 1.0), writes=onfb)
        P.op("pool", lambda e: e.memset(cmask, 1.0), writes=cmb)
        cm3 = cmask.rearrange("p (c i) -> p c i", i=CH)
        P.op("pool", lambda e: e.memset(cm3[:, :, 0:1], 0.0), writes=cmb)
        P.op("pool", lambda e: e.memset(m01, 1.0), writes=m01b)
        P.op("pool", lambda e: e.affine_select(out=m01, in_=m01, pattern=[[1, 128]], compare_op=ALU.is_ge, fill=0.0, base=0, channel_multiplier=-1),
             reads=m01b, writes=m01b)
        for g in range(4):
            P.op("pool", lambda e, g=g: e.affine_select(out=m01[32 * g:32 * g + 32, :], in_=m01[32 * g:32 * g + 32, :], pattern=[[-1, 128]],
                                                      compare_op=ALU.is_ge, fill=0.0, base=32 * (g + 1) - 1, channel_multiplier=0),
                 reads=m01b, writes=m01b)
        qs, qsb = X.view(0, T, F32)
        sg, sgb = X.view(4096, T, F32)
        lf, lfb = X.view(8192, T, F32)
        bb, bbb = X.view(12288, T, F32)
        Bc, Bcb = X.view(16384, T, F32)
        e1, e1b = X.view(20480, T, F32)
        bb3 = bb.rearrange("p (c i) -> p c i", i=CH)
        e13 = e1.rearrange("p (c i) -> p c i", i=CH)

        def hb(s, i):
            return A.view(s * 10240 + i * 2048, T, BF16)
        kec = [A.view(20480 + i * 2048, T, BF16, parts=32) for i in range(2)]
        vcl = [A.view(24576 + i * 2048, T, BF16, parts=32) for i in range(2)]
        o_out, oob = A.view(28672, T, F32)
        rb16 = rstd[:, :].bitcast(BF16)
        gs, gsb = rb16[:, 0:T], [rstdbuf]
        qcm, qcb = rb16[:, T:2 * T], [rstdbuf]
        scale = HD ** -0.5

        def stageA(h):
            s = h % 2
            q_in, qib = hb(s, 0); k_in, kib = hb(s, 1); vtm, vtb = hb(s, 2); keT, keTb = hb(s, 3); vT, vTb = hb(s, 4)
            dc, dcb = dec[s]

            def epi_act(dst, dstb, func):
                def f(hf, p, pb):
                    P.op("act", lambda e: e.activation(out=dst[:, hf * 512:(hf + 1) * 512], in_=p[:, :], func=func), reads=[pb], writes=dstb)
                return f
            def gate1():
                P.op("act", lambda e: e.activation(out=lf, in_=sg, func=AF.Ln, scale=oml[:, h:h + 1], bias=lbv[:, h:h + 1]), reads=sgb + constb, writes=lfb)
                yield
                P.op("dve", lambda e: e.tensor_scalar(out=sg, in0=sg, scalar1=noml[:, h:h + 1], scalar2=oml[:, h:h + 1], op0=ALU.mult, op1=ALU.add),
                     reads=sgb + constb, writes=sgb)
                yield
                P.op("dve", lambda e: e.tensor_tensor_scan(out=bb, data0=cmask, data1=lf, initial=0.0, op0=ALU.mult, op1=ALU.add), reads=lfb + cmb, writes=bbb)
                yield
                P.op("dve", lambda e: e.tensor_tensor_scan(out=Bc, data0=onesf, data1=lf, initial=0.0, op0=ALU.mult, op1=ALU.add), reads=lfb + onfb, writes=Bcb)
                yield
                P.op("act", lambda e: e.activation(out=lf, in_=bb, func=AF.Exp, scale=-1.0), reads=bbb, writes=lfb)
                yield
                P.op("dve", lambda e: e.tensor_tensor(out=k_in, in0=sg, in1=lf, op=ALU.mult), reads=sgb + lfb, writes=kib)
                yield
                P.op("act", lambda e: e.activation(out=dc, in_=bb3[:, :, CH - 1], func=AF.Exp), reads=bbb, writes=dcb)
                yield
                P.op("dve", lambda e: e.tensor_tensor(out=e13, in0=bb3[:, :, CH - 1:CH].broadcast_to([128, NCHUNK, CH]), in1=bb3, op=ALU.subtract),
                     reads=bbb, writes=e1b)
                yield
                P.op("act", lambda e: e.activation(out=e1, in_=e1, func=AF.Exp), reads=e1b, writes=e1b)
                yield
                P.op("dve", lambda e: e.tensor_tensor(out=keT, in0=sg, in1=e1, op=ALU.mult), reads=sgb + e1b, writes=keTb)
                yield
                P.op("act", lambda e: e.activation(out=e1, in_=bb, func=AF.Exp), reads=bbb, writes=e1b)
                yield
                P.op("act", lambda e: e.activation(out=lf, in_=Bc, func=AF.Exp), reads=Bcb, writes=lfb)
                yield

            def mix(ga, gb, ratio=2):
                alive_a = alive_b = True
                while alive_a or alive_b:
                    for _ in range(ratio):
                        if alive_a:
                            try:
                                next(ga)
                            except StopIteration:
                                alive_a = False
                        yield
                    if alive_b:
                        try:
                            next(gb)
                        except StopIteration:
                            alive_b = False

            def three_tiles():
                yield from dense_gen(rhs_u, 16, epi_act(qs, qsb, AF.Silu)); yield
                yield from dense_gen(rhs_u, 16, epi_act(vT, vTb, AF.Copy)); yield
                yield from dense_gen(rhs_u, 16, epi_act(gs, gsb, AF.Silu)); yield
            yield from dense_gen(rhs_u, 16, epi_act(sg, sgb, AF.Sigmoid)); yield
            yield from mix(three_tiles(), gate1(), ratio=2)
            P.dma("sp", lambda e: e.dma_start(out=G_d[h], in_=gs), chan="out_g", reads=gsb, writes=[dbuf(f"G{h}")])
            P.op("dve", lambda e: e.scalar_tensor_tensor(out=q_in, in0=qs, scalar=scale, in1=e1, op0=ALU.mult, op1=ALU.mult), reads=qsb + e1b, writes=qib)
            yield
            P.op("dve", lambda e: e.scalar_tensor_tensor(out=qcm, in0=qs, scalar=scale, in1=lf, op0=ALU.mult, op1=ALU.mult), reads=qsb + lfb, writes=qcb)
            yield
            P.dma("sp", lambda e: e.dma_start(out=QC_d[h], in_=qcm), chan="out_qc", reads=qcb, writes=[dbuf(f"QC{h}")])
            yield
            for (src, srcb, dst, dstb, bank) in ((vT, vTb, vtm, vtb, 4),):
                ptv = ps[bank][:, :].bitcast(BF16)
                for j in range(8):
                    P.op("pe", lambda e, j=j, src=src, ptv=ptv: e.transpose(out=ptv[:, j * 128:(j + 1) * 128], in_=src[:, j * 128:(j + 1) * 128], identity=ident_b[:, :]),
                         reads=srcb + [identbuf], writes=[psbuf[bank]])
                P.op("act", lambda e, dst=dst, ptv=ptv: e.activation(out=dst, in_=ptv, func=AF.Copy), reads=[psbuf[bank]], writes=dstb)
            yield

        def rec(h):
            s = h % 2
            q_in, qib = hb(s, 0); k_in, kib = hb(s, 1); vtm, vtb = hb(s, 2); keT, keTb = hb(s, 3); vT, vTb = hb(s, 4)
            dc, dcb = dec[s]

            def prep_batch(bt):
                for (src, srcb, (dst, dstb), bank) in ((keT, keTb, kec[bt % 2], 4), (vT, vTb, vcl[bt % 2], 4)):
                    ptv = ps[bank][:, :].bitcast(BF16)
                    for ci in range(8):
                        c = 8 * bt + ci
                        P.op("pe", lambda e: e.transpose(out=ptv[0:32, ci * 128:(ci + 1) * 128], in_=src[:, c * CH:(c + 1) * CH], identity=ident_b[:, :]),
                             reads=srcb + [identbuf], writes=[psbuf[bank]])
                    P.op("act", lambda e: e.activation(out=dst, in_=ptv[0:32, :], func=AF.Copy), reads=[psbuf[bank]], writes=dstb)
                    yield

            def kv_tile(jt):
                for cc in range(4):
                    c = 4 * jt + cc
                    bt, ci = c // 8, c % 8
                    kc, kcb = kec[bt % 2]; vc_, vcb = vcl[bt % 2]
                    kvbank = (2, 3, 7)[c % 3]
                    kvp = ps[kvbank][:, 0:128]; kvb = psbuf[kvbank]
                    so, sob = S32[c % 2]; sn32, sn32b = S32[(c + 1) % 2]
                    P.op("pe", lambda e: e.matmul(kvp, lhsT=kc[:, ci * 128:(ci + 1) * 128], rhs=vc_[:, ci * 128:(ci + 1) * 128], start=True, stop=True),
                         reads=kcb + vcb, writes=[kvb])
                    if c == 0:
                        P.op("dve", lambda e: e.tensor_copy(out=sn32, in_=kvp), reads=[kvb], writes=sn32b)
                    else:
                        P.op("dve", lambda e: e.scalar_tensor_tensor(out=sn32, in0=so, scalar=dc[:, c:c + 1], in1=kvp, op0=ALU.mult, op1=ALU.add),
                             reads=[kvb] + sob + dcb, writes=sn32b)
                    if c < NCHUNK - 1:
                        sn, snb = S16[(c + 1) % 12]
                        P.op("act", lambda e: e.activation(out=sn, in_=sn32, func=AF.Copy), reads=sn32b, writes=snb)
                    else:
                        P.dma("sp", lambda e: e.dma_start(out=SF_d[h], in_=sn32), chan="out_sf", reads=sn32b, writes=[dbuf(f"SF{h}")])
                    yield
            yield from prep_batch(0)
            yield from kv_tile(0)
            for jt in range(8):
                tsl = slice(jt * 128, (jt + 1) * 128)
                r = jt % 2
                ap_att = ps[5][:, r * 128:(r + 1) * 128]; ab = psbuf[5]
                ap_o = ps[6][:, r * 128:(r + 1) * 128]; ob = psbuf[6]
                am, amb = attm[r]
                P.op("pe", lambda e: e.matmul(ap_att, lhsT=k_in[:, tsl], rhs=q_in[:, tsl], start=True, stop=True), reads=kib + qib, writes=[ab])
                yield
                if jt % 2 == 0 and jt // 2 + 1 < 4:
                    yield from prep_batch(jt // 2 + 1)
                if jt < 7:
                    yield from kv_tile(jt + 1)
                P.op("dve", lambda e: e.tensor_tensor(out=am, in0=ap_att, in1=m01, op=ALU.mult), reads=[ab] + m01b, writes=amb)
                P.op("pe", lambda e: e.matmul(ap_o, lhsT=vtm[:, tsl], rhs=am, start=True, stop=(jt == 0), skip_group_check=True),
                     reads=vtb + amb, writes=[ob])
                for cc in range(4):
                    c = 4 * jt + cc
                    if c == 0:
                        continue
                    sc, scb = S16[c % 12]
                    last = cc == 3
                    P.op("pe", lambda e: e.matmul(ap_o[:, 32 * cc:32 * cc + 32], lhsT=sc, rhs=q_in[:, c * CH:(c + 1) * CH],
                                                  start=False, stop=last, skip_group_check=True),
                         reads=scb + qib, writes=[ob])
                P.op("act", lambda e: e.activation(out=o_out[:, tsl], in_=ap_o, func=AF.Copy), reads=[ob], writes=oob)
                yield
            P.dma("sp", lambda e: e.dma_start(out=OL_d[h], in_=o_out), chan="out_ol", reads=oob, writes=[dbuf(f"OL{h}")])

        H1LIM = getattr(build, "H1LIM", None)
        if H1LIM is not None:
            g_ = stageA(0)
            for _ in range(3):
                next(g_)
            if H1LIM >= 1:
                next(g_)
                dbgout("q_in", hb(0, 0)[0], hb(0, 0)[1], [128, T], BF16)
                dbgout("k_in", hb(0, 1)[0], hb(0, 1)[1], [128, T], BF16)
                dbgout("keT", hb(0, 3)[0], hb(0, 3)[1], [128, T], BF16)
                dbgout("vtm", hb(0, 2)[0], hb(0, 2)[1], [128, T], BF16)
                dbgout("bb", bb, bbb, [128, T], F32)
                dbgout("Bc", Bc, Bcb, [128, T], F32)
                dbgout("dec", dec[0][0], dec[0][1], [128, NCHUNK], F32)
                dbgout("m01", m01, m01b, [128, 128], F32)
            if H1LIM >= 2:
                r_ = rec(0)
                for _ in range(H1LIM - 1):
                    next(r_)
                dbgout("o_out", o_out, oob, [128, T], F32)
            return
        prev = None
        for h in range(NH + 1):
            gens = []
            if h < NH:
                gens.append(stageA(h))
            if prev is not None:
                gens.append(rec(prev))
            while gens:
                for g_ in list(gens):
                    try:
                        next(g_)
                    except StopIteration:
                        gens.remove(g_)
            prev = h if h < NH else None
        cnt["single"] = False
        cnt["group"] = 4
        if fused:
            fz["SFp"] = allgather("SF", SF_d.rearrange("h p e -> (h p) e"), NH * 128, 128, F32,
                                  [dbuf(f"SF{h}") for h in range(NH)]).rearrange("(h p) e -> h p e", h=NH)

    def hgrn2():
        og = [A.view(h * 2048, T, BF16) for h in range(NH)]
        o32 = [X.view(0, T, F32), X.view(4096, T, F32), X.view(18432, T, F32)]
        qc = [X.view(8192 + i * 2048, T, BF16) for i in range(2)]
        gsl = [X.view(12288, T, BF16), X.view(14336, T, BF16), X.view(22528, T, BF16)]
        s32 = [X.view(16384 + i * 512, 128, F32) for i in range(2)]
        s16 = [X.view(17408 + i * 256, 128, BF16) for i in range(2)]
        def load(h):
            s = h % 2
            o_ap, o_b = o32[h % 3]; q_ap, q_b = qc[s]; g_ap, g_b = gsl[h % 3]; sa, sab = s32[s]; sh, shb = s16[s]
            SFp = fz["SFp"] if fused else SFp_d
            P.dma("sp", lambda e: e.dma_start(out=o_ap, in_=OL_d[h]), chan=f"ld_o{h % 3}", reads=[dbuf(f"OL{h}")], writes=o_b)
            P.dma("sp", lambda e: e.dma_start(out=q_ap, in_=QC_d[h]), chan=f"ld_q{s}", reads=[dbuf(f"QC{h}")], writes=q_b)
            P.dma("sp", lambda e: e.dma_start(out=g_ap, in_=G_d[h]), chan=f"ld_g{h % 3}", reads=[dbuf(f"G{h}")], writes=g_b)
            P.dma("sp", lambda e: e.dma_start(out=sa, in_=SFp[h]), chan=f"ld_s{s}", reads=[dbuf("SF_all")], writes=sab)
            P.op("dve", lambda e: e.tensor_scalar(out=sh, in0=sa, scalar1=vcol("flag"), scalar2=None, op0=ALU.mult), reads=sab + [vecbuf], writes=shb)

        unit_state = {}

        def front(h, hf):
            s = h % 2
            o_ap, o_b = o32[h % 3]; q_ap, q_b = qc[s]; sh, shb = s16[s]
            sl = slice(hf * 512, (hf + 1) * 512)
            pc = ps[4 + hf]; pcb = psbuf[4 + hf]
            pm = ps[6 + hf]; pmb = psbuf[6 + hf]
            P.op("pe", lambda e: e.matmul(pc[:, :], lhsT=sh, rhs=q_ap[:, sl], start=True, stop=True), reads=shb + q_b, writes=[pcb])
            P.op("dve", lambda e: e.tensor_tensor(out=o_ap[:, sl], in0=o_ap[:, sl], in1=pc[:, :], op=ALU.add), reads=[pcb] + o_b, writes=o_b)
            i = cnt["sq"] % 3; cnt["sq"] += 1
            P.op("dve", lambda e: e.tensor_tensor(out=sq[i][:, :], in0=o_ap[:, sl], in1=o_ap[:, sl], op=ALU.mult), reads=o_b, writes=[sqbuf[i]])
            P.op("pe", lambda e: e.matmul(pm[:, :], lhsT=ones_b[:, :], rhs=sq[i][:, :], start=True, stop=True), reads=[sqbuf[i], onesbuf], writes=[pmb])

        def back_act(h, hf):
            pm = ps[6 + hf]; pmb = psbuf[6 + hf]
            t, tb = newtmp()
            unit_state[(h, hf)] = (t, tb)
            P.op("act", lambda e: e.activation(out=t[:, :], in_=pm[:, :], func=AF.Ln, scale=1.0 / HD, bias=epsc[:, 0:1]), reads=[pmb, epsb], writes=[tb])
            P.op("act", lambda e: e.activation(out=t[:, :], in_=t[:, :], func=AF.Exp, scale=-0.5), reads=[tb], writes=[tb])

        def back_dve(h, hf):
            o_ap, o_b = o32[h % 3]; g_ap, g_b = gsl[h % 3]
            sl = slice(hf * 512, (hf + 1) * 512)
            t, tb = unit_state[(h, hf)]
            P.op("dve", lambda e: e.scalar_tensor_tensor(out=t[:, :], in0=o_ap[:, sl], scalar=vcol("hg"), in1=t[:, :], op0=ALU.mult, op1=ALU.mult),
                 reads=o_b + [tb, vecbuf], writes=[tb])
            P.op("dve", lambda e: e.tensor_tensor(out=og[h][0][:, sl], in0=t[:, :], in1=g_ap[:, sl], op=ALU.mult), reads=[tb] + g_b, writes=og[h][1])
        units = [(h, hf) for h in range(NH) for hf in range(2)]
        load(0)
        load(1)
        for k, (h, hf) in enumerate(units):
            if k > 1:
                back_act(*units[k - 2])
            front(h, hf)
            if k > 1:
                back_dve(*units[k - 2])
            if hf == 1 and h + 2 < NH:
                load(h + 2)
        for u in units[-2:]:
            back_act(*u)
            back_dve(*u)
        cnt["single"] = False

        def rhs_og(k):
            return og[k][0], og[k][1]
        for o in range(16):
            dense(rhs_og, 16, epi_resid(o, final=True))

    for seg in segs:
        if seg == "S2b":
            rmsnorm("mlp0", stats="piped" if has("S2a") else "compute"); mlp(); rmsnorm("ple0", stats="piped"); ple(0)
        elif seg == "S3b":
            rmsnorm("mlp1", stats="piped" if has("S3a") else "compute"); mlp(); rmsnorm("ple1", stats="piped"); ple(1)
            rmsnorm("fin", out_f32_inplace=True, stats="piped")
        elif seg == "S2c":
            kv_phase()
        elif seg == "S3a":
            attn_phase()
        elif seg == "S1":
            hgrn1()
        elif seg == "S2a":
            hgrn2()

    for c in range(NCH):
        P.dma(("sp", "act", "pool")[c % 3], lambda e, c=c: e.dma_start(out=hout_d[c * 128:(c + 1) * 128, :], in_=hT[:, c, :]), chan="hout", reads=[hbuf[c]])
    finals = ["hout"] + [c for c in P.chan_count if c.startswith("out_")]
    assert widx["i"] == len(b.plan) or getattr(build, "H1LIM", None) is not None, (widx["i"], len(b.plan))
    P.emit_all(nc, finals)
    b.es.close()
    return b


NCORES = 8
_PROG_CACHE = {}


def _get_prog(segs, fused=False):
    key = (tuple(segs), fused)
    if key not in _PROG_CACHE:
        _PROG_CACHE[key] = build(list(segs), fused=fused)
    return _PROG_CACHE[key]


def _run(segs, inp, per_core, fused=False):
    b = _get_prog(segs, fused)
    slabs = [pack_slab(inp, s) for s in b.plan]
    walls = {f"wall{i}": np.stack(slabs[i * b.wch:(i + 1) * b.wch]) for i in range(len(b.walls))}
    wple = np.ascontiguousarray(np.stack([np.asarray(inp["w_ple_up"][l], np.float32).reshape(2, 128, 2048).transpose(1, 0, 2).reshape(128, 4096)
                                          for l in range(2)]))
    maps = []
    for r in range(NCORES):
        m = {"wple": wple, "vecs": pack_vecs(inp, r)}
        m.update(walls)
        m.update(per_core[r])
        maps.append(m)
    res = run_bass_kernel_spmd(b.nc, maps, core_ids=list(range(NCORES)))
    return res.results


ALL_SEGS = ["S1", "S2a", "S2b", "S2c", "S3a", "S3b"]


def kernel(**inputs):
    inp = {k: np.asarray(v) for k, v in inputs.items()}
    x = np.asarray(inp["x"], np.float32)
    p = np.asarray(inp["p"], np.float32)
    base = []
    for r in range(NCORES):
        bb, sl = r // 2, slice((r % 2) * T, (r % 2 + 1) * T)
        base.append({"hin": np.ascontiguousarray(x[bb, sl].T), "pT": np.ascontiguousarray(p[:, bb, sl].transpose(0, 2, 1))})
    o = _run(ALL_SEGS, inp, base, fused=True)
    out = np.empty((4, SEQ, D), np.float32)
    for r in range(NCORES):
        bb, sl = r // 2, slice((r % 2) * T, (r % 2 + 1) * T)
        out[bb, sl] = np.asarray(o[r]["hout"]).T
    return out
```

```python
import numpy as np
from contextlib import ExitStack
import concourse.bass as bass
import concourse.mybir as mybir
from concourse.bass_utils import run_bass_kernel_spmd

F32 = mybir.dt.float32
BF16 = mybir.dt.bfloat16
AF = mybir.ActivationFunctionType
ALU = mybir.AluOpType

D = 2048
NCH = 16
T = 1024
SEQ = 2048
NH = 16
HD = 128
PLE = 256
DFF = 8192
EPS = 1e-6
CH = 32
NCHUNK = T // CH
NSLOT = 6
NEG = -30000.0


class Buf:
    __slots__ = ("name", "w", "rs", "excl")

    def __init__(self, name, excl=False):
        self.name = name
        self.w = None
        self.rs = {}
        self.excl = excl


class Op:
    __slots__ = ("eng", "emit", "deps", "sig", "val", "chan", "idx", "inc")

    def __init__(self, eng, emit, chan=None):
        self.eng = eng
        self.emit = emit
        self.deps = []
        self.sig = False
        self.val = None
        self.chan = chan
        self.idx = None
        self.inc = 16


class _Rec:
    def __init__(self):
        self.calls = []

    def __getattr__(self, name):
        def f(*a, **k):
            self.calls.append((name, a, k))
            return self
        return f


def _freeze(emit):
    r = _Rec()
    emit(r)
    assert len(r.calls) == 1, r.calls
    name, a, k = r.calls[0]
    return lambda eng: getattr(eng, name)(*a, **k)


class Prog:
    ENGS = ("pe", "act", "dve", "pool", "sp")

    def __init__(self, same_engine_sync=True):
        self.ops = {e: [] for e in self.ENGS}
        self.chan_count = {}
        self.same_engine_sync = same_engine_sync
        self.nops = 0

    @staticmethod
    def _flat(x):
        out = []
        for b in x:
            if isinstance(b, (list, tuple)):
                out.extend(Prog._flat(b))
            else:
                out.append(b)
        return out

    def _track(self, op, reads, writes):
        reads = self._flat(reads)
        writes = self._flat(writes)
        writes = writes + [b for b in reads if b.excl]
        reads = [b for b in reads if not b.excl]
        deps = []
        for b in reads:
            if b.w is not None:
                deps.append(b.w)
        for b in writes:
            if b.w is not None:
                deps.append(b.w)
            deps.extend(b.rs.values())
        for b in writes:
            b.w = op
            b.rs = {}
        for b in reads:
            key = op.chan if op.chan is not None else op.eng
            b.rs[key] = op
        seen = set()
        for d in deps:
            if d is op or id(d) in seen:
                continue
            seen.add(id(d))
            if d.chan is None and d.eng == op.eng:
                if op.eng == "pe" or not self.same_engine_sync:
                    continue
                if op.chan is not None:
                    pass
            d.sig = True
            op.deps.append(d)

    def op(self, eng, emit, reads=(), writes=()):
        o = Op(eng, _freeze(emit))
        o.idx = self.nops
        self.nops += 1
        self._track(o, reads, writes)
        self.ops[eng].append(o)
        return o

    def dma(self, queue, emit, chan, reads=(), writes=()):
        o = Op(queue, _freeze(emit), chan=chan)
        o.idx = self.nops
        self.nops += 1
        self._track(o, reads, writes)
        c = self.chan_count.get(chan, 0) + 16
        self.chan_count[chan] = c
        o.val = c
        o.sig = True
        self.ops[queue].append(o)
        return o

    def coll(self, emit, chan, reads=(), writes=()):
        o = Op("pool", _freeze(emit), chan=chan)
        o.idx = self.nops
        self.nops += 1
        self._track(o, reads, writes)
        c = self.chan_count.get(chan, 0) + 1
        self.chan_count[chan] = c
        o.val = c
        o.sig = True
        o.inc = 1
        self.ops["pool"].append(o)
        return o

    def emit_all(self, nc, final_chans):
        with ExitStack() as es:
            esem = {e: es.enter_context(nc.semaphore("s_" + e)) for e in self.ENGS}
            csem = {c: es.enter_context(nc.semaphore("c_" + c)) for c in self.chan_count}
            for e in self.ENGS:
                cnt = 0
                for o in self.ops[e]:
                    if o.chan is None and o.sig:
                        cnt += 1
                        o.val = cnt

            def run(eng_name, eng):
                seen = {}
                for o in self.ops[eng_name]:
                    for d in o.deps:
                        if d.chan is not None:
                            key, sem = ("c", d.chan), csem[d.chan]
                        else:
                            key, sem = ("e", d.eng), esem[d.eng]
                        if seen.get(key, 0) >= d.val:
                            continue
                        seen[key] = d.val
                        eng.wait_ge(sem, d.val)
                    ins = o.emit(eng)
                    if o.chan is not None:
                        ins.then_inc(csem[o.chan], o.inc)
                    elif o.sig:
                        ins.then_inc(esem[eng_name], 1)
                if eng_name == "sp":
                    for c in final_chans:
                        eng.wait_ge(csem[c], self.chan_count[c])

            with nc.Block() as block:
                @block.tensor
                def _(e):
                    run("pe", e)

                @block.scalar
                def _(e):
                    run("act", e)

                @block.vector
                def _(e):
                    run("dve", e)

                @block.gpsimd
                def _(e):
                    run("pool", e)

                @block.sync
                def _(e):
                    run("sp", e)


VEC_COLS = {}
_c = 0
for _name, _n in [("mix0", 16), ("mix1", 16), ("mlp0", 16), ("mlp1", 16), ("ple0", 16), ("ple1", 16),
                  ("kvn", 16), ("fin", 16), ("lb0", 16), ("lb1", 16), ("hg", 1), ("bf", 1),
                  ("flag", 1), ("prevbias", 1)]:
    VEC_COLS[_name] = (_c, _n)
    _c += _n
NVEC = _c


def pack_vecs(inp, rank):
    v = np.zeros((128, NVEC), np.float32)

    def put(name, arr):
        c0, n = VEC_COLS[name]
        v[:, c0:c0 + n] = np.asarray(arr, np.float32).reshape(n, 128).T

    put("mix0", inp["mix_norm"][0]); put("mix1", inp["mix_norm"][1])
    put("mlp0", inp["mlp_norm"][0]); put("mlp1", inp["mlp_norm"][1])
    put("ple0", inp["ple_norm"][0]); put("ple1", inp["ple_norm"][1])
    put("kvn", inp["kv_norm"]); put("fin", inp["final_norm"])
    put("lb0", inp["a_lb_logits"][0]); put("lb1", inp["a_lb_logits"][1])
    put("hg", inp["a_head_gain"][0])
    c0, _ = VEC_COLS["bf"]
    v[:16, c0] = np.asarray(inp["b_f"], np.float32)
    odd = rank % 2
    v[:, VEC_COLS["flag"][0]] = 1.0 if odd else 0.0
    v[:, VEC_COLS["prevbias"][0]] = 0.0 if odd else NEG
    return v


def pack_slab(inp, spec):
    kind = spec[0]
    if kind == "std":
        _, name, layer, r0, c0 = spec
        W = inp[name] if layer is None else inp[name][layer]
        blk = np.asarray(W[r0:r0 + 2048, c0:c0 + 128], np.float32)
        return blk.reshape(16, 128, 128).transpose(1, 0, 2).reshape(128, 2048)
    if kind == "ple":
        _, layer, o8 = spec
        W = np.asarray(inp["w_ple_up"][layer][:, o8 * 1024:(o8 + 1) * 1024], np.float32)
        blk = W.reshape(2, 128, 8, 128).transpose(1, 2, 0, 3)
        return blk.reshape(128, 2048)
    if kind == "flog":
        W = np.asarray(inp["w_kvf"][:, 4096:4112], np.float32)
        out = np.zeros((128, 2048), np.float32)
        out[:, :256] = W.reshape(16, 128, 16).transpose(1, 0, 2).reshape(128, 256)
        return out
    raise ValueError(kind)


class Builder:
    def __init__(self, segs, fused):
        self.segs = segs
        self.fused = fused
        self.plan = []
        self.nc = bass.Bass("TRN2", target_bir_lowering=False)
        self.P = Prog()
        self.es = ExitStack()
        self.final_chans = []
        self._n = 0

    def sb(self, name, shape, dt):
        return self.es.enter_context(self.nc.sbuf_tensor(name, shape, dt))

    def dram(self, name, shape, dt, kind):
        if kind == "Internal":
            return self.nc.dram_tensor(name, shape, dt).ap()
        return self.nc.dram_tensor(name, shape, dt, kind=kind).ap()

    def uniq(self, s):
        self._n += 1
        return f"{s}{self._n}"

    def wplan(self, specs):
        i0 = len(self.plan)
        self.plan.extend(specs)
        return i0

    def wacquire(self, idx):
        hi = min(len(self.plan), idx + NSLOT)
        while self.w_issued < hi:
            j = self.w_issued
            s = j % NSLOT
            src = self.walls[j // self.wch][j % self.wch]
            dst = self.wslot[s]
            self.P.dma("pool", lambda e, dst=dst, src=src: e.dma_start(out=dst[:, :], in_=src),
                       chan=f"w{s}", writes=[self.wbuf[s]])
            self.w_issued += 1
        s = idx % NSLOT
        return self.wslot[s], self.wbuf[s]


def make_plan(segs):
    plan = []

    def mlp(l):
        for g in range(4):
            for j in range(16):
                plan.append(("std", "w_mlp_up", l, 0, (g * 16 + j) * 128))
            for o in range(16):
                plan.append(("std", "w_mlp_down", l, g * 2048, o * 128))

    def ple(l):
        for o in range(16):
            plan.append(("std", "w_ple_gate", l, 0, o * 128))

    for s in segs:
        if s == "S1":
            for h in range(NH):
                for j in (1, 0, 2, 3):
                    plan.append(("std", "w_a_in", 0, 0, j * 2048 + h * 128))
        elif s == "S2a":
            for o in range(16):
                plan.append(("std", "w_a_out", 0, 0, o * 128))
        elif s == "S2b":
            mlp(0); ple(0)
        elif s == "S2c":
            plan.append(("flog",))
            for h in range(NH):
                plan.append(("std", "w_kvf", None, 0, h * 128))
            for h in range(NH):
                plan.append(("std", "w_kvf", None, 0, 2048 + h * 128))
        elif s == "S3a":
            for h in range(NH):
                plan.append(("std", "w_b_q", 0, 0, h * 128))
            for o in range(16):
                plan.append(("std", "w_b_out", 0, 0, o * 128))
        elif s == "S3b":
            mlp(1); ple(1)
    return plan


class Arena:
    PAGE = 256

    def __init__(self, b, name, nbytes):
        self.t = b.sb(name, [128, nbytes // 4], F32)
        self.pages = [Buf(f"{name}_pg{i}") for i in range(nbytes // self.PAGE)]
        self.nbytes = nbytes

    def view(self, off, nelem, dt, parts=128):
        esz = 4 if dt == F32 else 2
        nb = nelem * esz
        assert off % 4 == 0 and off + nb <= self.nbytes, (off, nb, self.nbytes)
        ap = self.t[0:parts, off // 4:(off + nb + 3) // 4]
        if dt != F32:
            ap = ap.bitcast(dt)
        pgs = self.pages[off // self.PAGE:(off + nb - 1) // self.PAGE + 1]
        return ap, pgs


def build(segs, fused=False):
    b = Builder(segs, fused)
    nc, P = b.nc, b.P
    b.plan = make_plan(segs)
    nslab = max(1, len(b.plan))
    WCH = 224
    b.walls = [b.dram(f"wall{i}", [min(WCH, nslab - i * WCH), 128, 2048], F32, "ExternalInput") for i in range((nslab + WCH - 1) // WCH)]
    b.wch = WCH
    vecs_d = b.dram("vecs", [128, NVEC], F32, "ExternalInput")
    hin_d = b.dram("hin", [D, T], F32, "ExternalInput")
    hout_d = b.dram("hout", [D, T], F32, "ExternalOutput")
    pT_d = b.dram("pT", [2, PLE, T], F32, "ExternalInput")
    wple_d = b.dram("wple", [2, 128, 2 * D], F32, "ExternalInput")
    io = {}

    def xio(name, shape, dt, produced):
        kind = "Internal" if fused else ("ExternalOutput" if produced else "ExternalInput")
        io[name] = b.dram(name, shape, dt, kind)
        return io[name]

    PAIRS = [[2 * i, 2 * i + 1] for i in range(4)]
    dbufs = {}

    def dbuf(name):
        if name not in dbufs:
            dbufs[name] = Buf("dram_" + name)
        return dbufs[name]

    def allgather(name, src2d, rows, cols, dt, src_bufs):
        dst = b.dram(name + "_all", [2 * rows, cols], dt, "Internal")
        P.coll(lambda e: e.collective_compute("AllGather", ALU.bypass, replica_groups=PAIRS, ins=[src2d.opt()], outs=[dst.opt()]),
               chan="cc_" + name, reads=src_bufs, writes=[dbuf(name + "_all")])
        return dst[0:rows, :]

    has = lambda s: s in segs
    DBG = getattr(build, "DBG", False)

    def dbgout(name, ap, reads, shape, dt):
        if not DBG:
            return
        dd = b.dram("dbg_" + name, shape, dt, "ExternalOutput")
        P.dma("sp", lambda e: e.dma_start(out=dd, in_=ap), chan="out_dbg", reads=reads)
    if has("S2c") or has("S3a"):
        Kown_d = xio("Kown", [NH, 128, T], BF16, has("S2c"))
        Vown_d = xio("Vown", [NH, 128, 8 * 128], BF16, has("S2c"))
        Down_d = xio("Dloc", [NH, T], F32, has("S2c"))
    if has("S3a") and not fused:
        Kprev_d = xio("Kprev", [NH, 128, T], BF16, False)
        Vprev_d = xio("Vprev", [NH, 128, 8 * 128], BF16, False)
        Dprev_d = xio("Dprev", [NH, T], F32, False)
    fz = {}

    hT = b.sb("hT", [128, NCH, T], F32)
    hbuf = [Buf(f"h{c}") for c in range(NCH)]
    uT = b.sb("uT", [128, NCH, T], BF16)
    ubuf = [Buf(f"u{c}") for c in range(NCH)]
    A = Arena(b, "A", 32 * 1024)
    X = Arena(b, "X", 24 * 1024)
    b.wslot = [b.sb(f"w{s}", [128, 2048], BF16) for s in range(NSLOT)]
    b.wbuf = [Buf(f"w{s}") for s in range(NSLOT)]
    b.w_issued = 0
    vecs = b.sb("vecs_sb", [128, NVEC], F32); vecbuf = Buf("vecs")
    ones_b = b.sb("ones_b", [128, 128], BF16); onesbuf = Buf("ones")
    ident_b = b.sb("ident_b", [128, 128], BF16); identbuf = Buf("ident")
    ident_f = b.sb("ident_f", [128, 128], F32); identfbuf = Buf("identf")
    rstd = b.sb("rstd", [128, T], F32); rstdbuf = Buf("rstd")
    epsc = b.sb("epsc", [128, 2], F32); epsb = Buf("epsc")
    sq = [b.sb(f"sq{i}", [128, 512], BF16) for i in range(3)]
    sqbuf = [Buf(f"sq{i}") for i in range(3)]
    tmp = [b.sb(f"tmp{i}", [128, 512], F32) for i in range(4)]
    tmpbuf = [Buf(f"tmp{i}") for i in range(4)]
    ps = [b.es.enter_context(nc.psum_tensor(f"ps{i}", [128, 512], F32)) for i in range(8)]
    psbuf = [[Buf(f"ps{i}", excl=True)] for i in range(8)]
    Y = Arena(b, "Y", 14080)
    cnt = {"tmp": 0, "sq": 0, "par": 0}

    def vcol(name, i=0):
        c0, _ = VEC_COLS[name]
        return vecs[:, c0 + i:c0 + i + 1]

    P.dma("sp", lambda e: e.dma_start(out=vecs[:, :], in_=vecs_d), chan="misc", writes=[vecbuf])
    P.op("pool", lambda e: e.memset(ones_b[:, :], 1.0), writes=[onesbuf])
    P.op("pool", lambda e: e.memset(epsc[:, 0:1], EPS), writes=[epsb])
    P.op("pool", lambda e: e.memset(epsc[:, 1:2], 1.0), writes=[epsb])
    zf = tmp[0]
    P.op("pool", lambda e: e.memset(zf[:, 0:128], 0.0), writes=[tmpbuf[0]])
    P.op("pool", lambda e: e.affine_select(out=ident_f[:, :], in_=zf[:, 0:128], pattern=[[-1, 128]],
                                           compare_op=ALU.not_equal, fill=1.0, base=0, channel_multiplier=1),
         reads=[tmpbuf[0]], writes=[identfbuf])
    P.op("pool", lambda e: e.tensor_copy(out=ident_b[:, :], in_=ident_f[:, :]), reads=[identfbuf], writes=[identbuf])
    for c in range(NCH):
        P.dma(("sp", "act")[c % 2], lambda e, c=c: e.dma_start(out=hT[:, c, :], in_=hin_d[c * 128:(c + 1) * 128, :]),
              chan=f"hin{c}", writes=[hbuf[c]])

    pending = []

    def flush_stats():
        while pending:
            stats_chunk(pending.pop(0))

    def stats_chunk(c, sb=6):
        for hf in range(2):
            i = cnt["sq"] % 3; cnt["sq"] += 1
            P.op("act", lambda e: e.activation(out=sq[i][:, :], in_=hT[:, c, hf * 512:(hf + 1) * 512], func=AF.Square),
                 reads=[hbuf[c]], writes=[sqbuf[i]])
            P.op("pe", lambda e: e.matmul(ps[sb + hf][:, :], lhsT=ones_b[:, :], rhs=sq[i][:, :], start=(c == 0), stop=(c == NCH - 1)),
                 reads=[sqbuf[i], onesbuf], writes=[psbuf[sb + hf]])

    def rmsnorm(gname, out_f32_inplace=False, stats="compute"):
        sb = 4
        if stats == "compute":
            for c in range(NCH):
                stats_chunk(c, sb=4)
        elif stats == "piped":
            flush_stats()
            sb = 6
        if stats != "reuse":
            for hf in range(2):
                P.op("act", lambda e, hf=hf: e.activation(out=rstd[:, hf * 512:(hf + 1) * 512], in_=ps[sb + hf][:, :], func=AF.Ln, scale=1.0 / D, bias=epsc[:, 0:1]),
                     reads=[psbuf[sb + hf], epsb], writes=[rstdbuf])
            P.op("act", lambda e: e.activation(out=rstd[:, :], in_=rstd[:, :], func=AF.Exp, scale=-0.5), reads=[rstdbuf], writes=[rstdbuf])
        for c in range(NCH):
            if out_f32_inplace:
                P.op("dve", lambda e, c=c: e.scalar_tensor_tensor(out=hT[:, c, :], in0=hT[:, c, :], scalar=vcol(gname, c), in1=rstd[:, :],
                                                                  op0=ALU.mult, op1=ALU.mult),
                     reads=[hbuf[c], rstdbuf, vecbuf], writes=[hbuf[c]])
            else:
                P.op("dve", lambda e, c=c: e.scalar_tensor_tensor(out=uT[:, c, :], in0=hT[:, c, :], scalar=vcol(gname, c), in1=rstd[:, :],
                                                                  op0=ALU.mult, op1=ALU.mult),
                     reads=[hbuf[c], rstdbuf, vecbuf], writes=[ubuf[c]])

    widx = {"i": 0}

    def dense_gen(rhs, kch, epi, mcols=128, group=None):
        idx = widx["i"]; widx["i"] += 1
        slot, wb = b.wacquire(idx)
        w3 = slot[:, :].rearrange("p (k n) -> p k n", k=16) if mcols == 128 else slot[:, 0:16 * mcols].rearrange("p (k n) -> p k n", k=16)
        par = 0 if cnt.get("single") else cnt["par"] % 2
        cnt["par"] += 1
        n = 0
        group = group or cnt.get("group", 4)
        for k in range(kch):
            rap, rb = rhs(k)
            for hf in range(2):
                pi = 2 * par + hf
                P.op("pe", lambda e: e.matmul(ps[pi][0:mcols, :], lhsT=w3[:, k, :], rhs=rap[:, hf * 512:(hf + 1) * 512],
                                              start=(k == 0), stop=(k == kch - 1)),
                     reads=[wb] + rb, writes=[psbuf[pi]])
                n += 1
                if n % group == 0 and n < 2 * kch:
                    yield
        flush_stats()
        for hf in range(2):
            pi = 2 * par + hf
            epi(hf, ps[pi], psbuf[pi])

    def dense(rhs, kch, epi, mcols=128):
        for _ in dense_gen(rhs, kch, epi, mcols=mcols):
            pass

    def rhs_u(k):
        return uT[:, k, :], [ubuf[k]]

    def newtmp():
        i = cnt["tmp"] % 4; cnt["tmp"] += 1
        return tmp[i], tmpbuf[i]

    def epi_resid(o, final=False):
        def f(hf, p, pb):
            P.op("dve", lambda e: e.tensor_tensor(out=hT[:, o, hf * 512:(hf + 1) * 512], in0=hT[:, o, hf * 512:(hf + 1) * 512], in1=p[:, :], op=ALU.add),
                 reads=[pb, hbuf[o]], writes=[hbuf[o]])
            if final and hf == 1:
                pending.append(o)
        return f

    def mlp():
        hid = [A.view(j * 2048, T, BF16) for j in range(16)]

        def rhs_a(k):
            return hid[k][0], hid[k][1]
        for g in range(4):
            for j in range(16):
                def epi(hf, p, pb, j=j):
                    t, tb = newtmp()
                    P.op("act", lambda e: e.activation(out=t[:, :], in_=p[:, :], func=AF.Relu), reads=[pb], writes=[tb])
                    P.op("dve", lambda e: e.tensor_tensor(out=hid[j][0][:, hf * 512:(hf + 1) * 512], in0=t[:, :], in1=t[:, :], op=ALU.mult),
                         reads=[tb], writes=hid[j][1])
                dense(rhs_u, 16, epi)
            for o in range(16):
                dense(rhs_a, 16, epi_resid(o, final=(g == 3)))

    def ple(l):
        pT_sb, pTb = X.view(0, 2 * T, BF16)
        pT3 = pT_sb.rearrange("p (k t) -> p k t", k=2)
        wp_sb, wpb = X.view(4096, 2 * D, BF16)
        wp3 = wp_sb.rearrange("p (k n) -> p k n", k=2)
        P.dma("pool", lambda e: e.dma_start(out=pT3, in_=pT_d[l].rearrange("(k p) t -> p k t", p=128)), chan="pT", writes=pTb)
        for k in range(2):
            P.dma("pool", lambda e, k=k: e.dma_start(out=wp3[:, k, :], in_=wple_d[l][:, k * D:(k + 1) * D]), chan="wple", writes=wpb)
        for o in range(16):
            def epi(hf, p, pb, o=o):
                pp = ps[4 + hf]; ppb = psbuf[4 + hf]
                for k in range(2):
                    P.op("pe", lambda e, k=k: e.matmul(pp[:, :], lhsT=wp3[:, k, o * 128:(o + 1) * 128], rhs=pT3[:, k, hf * 512:(hf + 1) * 512],
                                                       start=(k == 0), stop=(k == 1)), reads=wpb + pTb, writes=[ppb])
                t, tb = newtmp()
                P.op("act", lambda e: e.activation(out=t[:, :], in_=p[:, :], func=AF.Sigmoid), reads=[pb], writes=[tb])
                t2, tb2 = newtmp()
                P.op("dve", lambda e: e.tensor_tensor(out=t2[:, :], in0=t[:, :], in1=pp[:, :], op=ALU.mult), reads=[tb, ppb], writes=[tb2])
                P.op("dve", lambda e: e.tensor_tensor(out=hT[:, o, hf * 512:(hf + 1) * 512], in0=hT[:, o, hf * 512:(hf + 1) * 512], in1=t2[:, :], op=ALU.add),
                     reads=[tb2, hbuf[o]], writes=[hbuf[o]])
                if hf == 1:
                    pending.append(o)
            dense(rhs_u, 16, epi)


    negD_, negDb = Y.view(0, 16 * NH, F32)
    negD = negD_.rearrange("p (j h) -> p j h", j=16)
    Dsb, Dsbb = Y.view(2048, 2 * T, F32, parts=16)
    dadd, daddb = Y.view(10240, 1, F32, parts=16)
    Dbf, Dbfb = Y.view(10496, T, BF16)
    sel_, selb = X.view(20480, NH * 128, BF16)
    sel = sel_.rearrange("p (h m) -> p h m", h=NH)
    dprep_done = []

    def d_prep():
        dprep_done.append(1)
        Dp = fz["Dprev"] if fused else Dprev_d
        P.dma("sp", lambda e: e.dma_start(out=Dsb[:, 0:T], in_=Dp), chan="dld", reads=[dbuf("D_all")], writes=[Dsbb])
        P.dma("sp", lambda e: e.dma_start(out=Dsb[:, T:2 * T], in_=Down_d), chan="dld", reads=[dbuf("D")], writes=[Dsbb])
        P.op("dve", lambda e: e.tensor_tensor(out=dadd[:, :], in0=Dsb[:, T - 1:T], in1=vcol("flag")[0:16, :], op=ALU.mult), reads=[Dsbb, vecbuf], writes=[daddb])
        P.op("dve", lambda e: e.tensor_scalar(out=Dsb[:, T:2 * T], in0=Dsb[:, T:2 * T], scalar1=dadd[:, 0:1], scalar2=None, op0=ALU.add),
             reads=[Dsbb, daddb], writes=[Dsbb])
        P.op("pool", lambda e: e.memset(Dbf, 0.0), writes=Dbfb)
        P.op("dve", lambda e: e.tensor_copy(out=Dbf[0:16, :], in_=Dsb[:, T:2 * T]), reads=[Dsbb] + Dbfb, writes=Dbfb)
        P.op("dve", lambda e: e.tensor_copy(out=sel, in_=ident_b[:, 0:16].unsqueeze(2).broadcast_to([128, NH, 128])), reads=[identbuf], writes=selb)
        for j in range(16):
            pj = ps[6 + j % 2]
            P.op("pe", lambda e, j=j, pj=pj: e.transpose(out=pj[:, 0:16], in_=Dsb[:, j * 128:(j + 1) * 128], identity=ident_f[0:16, 0:16]),
                 reads=[Dsbb, identfbuf], writes=[psbuf[6 + j % 2]])
            if j < 8:
                P.op("dve", lambda e, j=j, pj=pj: e.tensor_scalar(out=negD[:, j, :], in0=pj[:, 0:16], scalar1=-1.0, scalar2=vcol("prevbias"),
                                                                  op0=ALU.mult, op1=ALU.add), reads=[psbuf[6 + j % 2], vecbuf], writes=[negDb])
            else:
                P.op("dve", lambda e, j=j, pj=pj: e.tensor_scalar(out=negD[:, j, :], in0=pj[:, 0:16], scalar1=-1.0, scalar2=None, op0=ALU.mult),
                     reads=[psbuf[6 + j % 2]], writes=[negDb])

    def kv_phase():
        rmsnorm("kvn", stats="piped" if has("S2b") else "compute")
        kst = [X.view(i * 2048, T, BF16) for i in range(2)]
        vst = [X.view(4096 + i * 2048, T, BF16) for i in range(2)]
        vtm = [X.view(8192 + i * 2048, T, BF16) for i in range(2)]
        ones_f, onesfb = X.view(12288, T, F32)
        sp1, sp1b = X.view(16384, T, F32)
        dl, dlb = X.view(20480, T, F32)
        nbf, nbfb = tmp[3][:, 0:1], tmpbuf[3]
        P.op("pool", lambda e: e.memset(ones_f[:, :], 1.0), writes=onesfb)
        P.op("dve", lambda e: e.tensor_scalar(out=nbf[0:16, :], in0=vcol("bf")[0:16, :], scalar1=-1.0, scalar2=None, op0=ALU.mult),
             reads=[vecbuf], writes=[nbfb])

        def epi(hf, p, pb):
            sl = slice(hf * 512, (hf + 1) * 512)
            P.op("act", lambda e: e.activation(out=sp1[0:16, sl], in_=p[0:16, :], func=AF.Exp, scale=-1.0, bias=nbf[0:16, :]),
                 reads=[pb, nbfb], writes=sp1b)
            P.op("act", lambda e: e.activation(out=sp1[0:16, sl], in_=sp1[0:16, sl], func=AF.Ln, bias=epsc[0:16, 1:2]), reads=sp1b + [epsb], writes=sp1b)
            if hf == 1:
                P.op("dve", lambda e: e.tensor_tensor_scan(out=dl[0:16, :], data0=ones_f[0:16, :], data1=sp1[0:16, :], initial=0.0,
                                                           op0=ALU.mult, op1=ALU.add), reads=sp1b + onesfb, writes=dlb)
                P.op("dve", lambda e: e.tensor_scalar(out=dl[0:16, :], in0=dl[0:16, :], scalar1=-1.0, scalar2=None, op0=ALU.mult), reads=dlb, writes=dlb)
                P.dma("sp", lambda e: e.dma_start(out=Down_d, in_=dl[0:16, :]), chan="out_d", reads=dlb, writes=[dbuf("D")])
        dense(rhs_u, 16, epi, mcols=16)
        if fused:
            fz["Dprev"] = allgather("D", Down_d, NH, T, F32, [dbuf("D")])
        if fused:
            d_prep()
        for h in range(NH):
            def epi(hf, p, pb, h=h):
                k_ap, kb = kst[h % 2]
                P.op("act", lambda e: e.activation(out=k_ap[:, hf * 512:(hf + 1) * 512], in_=p[:, :], func=AF.Copy), reads=[pb], writes=kb)
                if hf == 1:
                    P.dma("sp", lambda e: e.dma_start(out=Kown_d[h], in_=k_ap), chan=f"out_k{h % 2}", reads=kb, writes=[dbuf(f"K{h}")])
            dense(rhs_u, 16, epi)
            if fused and h % 4 == 3:
                g = h // 4
                fz[f"Kprev{g}"] = allgather(f"K{g}_", Kown_d[4 * g:4 * g + 4].rearrange("h p t -> (h p) t"), 512, T, BF16,
                                            [dbuf(f"K{hh}") for hh in range(4 * g, 4 * g + 4)]).rearrange("(h p) t -> h p t", h=4)
        for h in range(NH):
            def epi(hf, p, pb, h=h):
                v_ap, vb = vst[h % 2]
                t_ap, tb = vtm[h % 2]
                P.op("act", lambda e: e.activation(out=v_ap[:, hf * 512:(hf + 1) * 512], in_=p[:, :], func=AF.Copy), reads=[pb], writes=vb)
                pt = ps[6 + hf][:, :].bitcast(BF16)
                for j in range(4):
                    jj = hf * 4 + j
                    P.op("pe", lambda e, j=j, jj=jj: e.transpose(out=pt[:, j * 128:(j + 1) * 128], in_=v_ap[:, jj * 128:(jj + 1) * 128], identity=ident_b[:, :]),
                         reads=vb + [identbuf], writes=[psbuf[6 + hf]])
                P.op("dve", lambda e: e.tensor_copy(out=t_ap[:, hf * 512:(hf + 1) * 512], in_=pt[:, 0:512]), reads=[psbuf[6 + hf]], writes=tb)
                if hf == 1:
                    P.dma("sp", lambda e: e.dma_start(out=Vown_d[h], in_=t_ap), chan=f"out_v{h % 2}", reads=tb, writes=[dbuf(f"V{h}")])
            dense(rhs_u, 16, epi)
            if fused and h % 4 == 3:
                g = h // 4
                fz[f"Vprev{g}"] = allgather(f"V{g}_", Vown_d[4 * g:4 * g + 4].rearrange("h p t -> (h p) t"), 512, T, BF16,
                                            [dbuf(f"V{hh}") for hh in range(4 * g, 4 * g + 4)]).rearrange("(h p) t -> h p t", h=4)

    def attn_phase():
        rmsnorm("mix1", stats="reuse" if fused else "compute")
        kh = [X.view(i * 4096, 2 * T, BF16) for i in range(2)]
        vh = [X.view(8192 + i * 4096, 2 * T, BF16) for i in range(2)]
        qh = [X.view(16384 + i * 2048, T, BF16) for i in range(2)]
        masks = rstd[:, :].bitcast(BF16).rearrange("p (r t) -> p r t", r=4)
        maskb = [rstdbuf]
        pt, ptb = sq, sqbuf
        rden, rdenb = tmp[2], tmpbuf[2]
        og = [A.view(h * 2048, T, BF16) for h in range(NH)]
        P.op("pool", lambda e: e.memset(masks[:, :, :], 0.0), writes=maskb)
        for r in range(4):
            P.op("pool", lambda e, r=r: e.affine_select(out=masks[:, r, :], in_=masks[:, r, :], pattern=[[1, 512]], compare_op=ALU.is_ge,
                                                       fill=NEG, base=-128 * r, channel_multiplier=-1), reads=maskb, writes=maskb)
        if not dprep_done:
            d_prep()
        dbgout("negD", negD_, negDb, [128, 16 * NH], F32)
        dbgout("sel", sel_, selb, [128, NH * 128], BF16)
        dbgout("Dbf", Dbf, Dbfb, [128, T], BF16)
        dbgout("Dsb", Dsb, Dsbb, [16, 2 * T], F32)
        dbgout("masks", rstd[:, :], maskb, [128, T], F32)
        npt = [0]
        cnt["single"] = True
        SB = (6, 7, 2, 3)
        NHL = getattr(build, "NHL", NH)
        for h in range(NH):
            s = h % 2
            k_ap, kb = kh[s]; v_ap, vb = vh[s]; q_ap, qb_ = qh[s]
            if h >= NHL:
                def epi0(hf, p, pb, h=h):
                    P.op("act", lambda e: e.activation(out=og[h][0][:, hf * 512:(hf + 1) * 512], in_=p[:, :], func=AF.Copy), reads=[pb], writes=og[h][1])
                dense(rhs_u, 16, epi0)
                continue
            v3 = v_ap.rearrange("p (j e) -> p j e", j=16)
            Kph = fz[f"Kprev{h // 4}"][h % 4] if fused else Kprev_d[h]
            Vph = fz[f"Vprev{h // 4}"][h % 4] if fused else Vprev_d[h]
            P.dma("sp", lambda e, h=h, k_ap=k_ap: e.dma_start(out=k_ap[:, 0:T], in_=Kph), chan=f"kld{s}", reads=[dbuf(f"K{h // 4}__all")], writes=kb)
            P.dma("sp", lambda e, h=h, k_ap=k_ap: e.dma_start(out=k_ap[:, T:2 * T], in_=Kown_d[h]), chan=f"kld{s}", reads=[dbuf(f"K{h}")], writes=kb)
            P.dma("sp", lambda e, h=h, v_ap=v_ap: e.dma_start(out=v_ap[:, 0:T], in_=Vph), chan=f"vld{s}", reads=[dbuf(f"V{h // 4}__all")], writes=vb)
            P.dma("sp", lambda e, h=h, v_ap=v_ap: e.dma_start(out=v_ap[:, T:2 * T], in_=Vown_d[h]), chan=f"vld{s}", reads=[dbuf(f"V{h}")], writes=vb)

            def epi(hf, p, pb, q_ap=q_ap, qb_=qb_):
                P.op("act", lambda e: e.activation(out=q_ap[:, hf * 512:(hf + 1) * 512], in_=p[:, :], func=AF.Copy, scale=HD ** -0.5), reads=[pb], writes=qb_)
            dense(rhs_u, 16, epi)
            if h == 0:
                dbgout("kh", k_ap, kb, [128, 2 * T], BF16)
                dbgout("vh", v_ap, vb, [128, 2 * T], BF16)
                dbgout("qh", q_ap, qb_, [128, T], BF16)
            for qb in range(2):
                qs = slice(qb * 512, (qb + 1) * 512)
                tiles = [(j, None) for j in range(8)]
                for jj in range(4 * qb + 4):
                    tiles.append((8 + jj, jj - 4 * qb if jj >= 4 * qb else None))
                po, pd = ps[4], ps[5]
                ptidx = {}

                def qk(ti):
                    j, mr = tiles[ti]
                    pS = ps[SB[ti % 4]]; pSb = psbuf[SB[ti % 4]]
                    P.op("pe", lambda e: e.matmul(pS[:, :], lhsT=k_ap[:, j * 128:(j + 1) * 128], rhs=q_ap[:, qs], start=True, stop=False),
                         reads=kb + qb_, writes=[pSb])
                    P.op("pe", lambda e: e.matmul(pS[:, :], lhsT=sel[:, h, :], rhs=Dbf[:, qs], start=False, stop=True),
                         reads=selb + Dbfb, writes=[pSb])
                    if mr is not None:
                        P.op("dve", lambda e: e.tensor_tensor(out=pS[:, :], in0=pS[:, :], in1=masks[:, mr, :], op=ALU.add),
                             reads=[pSb] + maskb, writes=[pSb])

                def ex(ti):
                    j, mr = tiles[ti]
                    pS = ps[SB[ti % 4]]; pSb = psbuf[SB[ti % 4]]
                    i = npt[0] % 3; npt[0] += 1
                    ptidx[ti] = i
                    P.op("act", lambda e: e.activation(out=pt[i][:, :], in_=pS[:, :], func=AF.Exp, bias=negD[:, j, h:h + 1], scale=1.0),
                         reads=[pSb, negDb], writes=[ptb[i]])

                def pv(ti):
                    j, mr = tiles[ti]
                    i = ptidx[ti]
                    first, last = ti == 0, ti == len(tiles) - 1
                    P.op("pe", lambda e: e.matmul(po[:, :], lhsT=v3[:, j, :], rhs=pt[i][:, :], start=first, stop=last),
                         reads=vb + [ptb[i]], writes=[psbuf[4]])
                    P.op("pe", lambda e: e.matmul(pd[:, :], lhsT=ones_b[:, :], rhs=pt[i][:, :], start=first, stop=last),
                         reads=[onesbuf, ptb[i]], writes=[psbuf[5]])
                qk(0)
                if len(tiles) > 1:
                    qk(1)
                for ti in range(len(tiles)):
                    if ti + 2 < len(tiles):
                        qk(ti + 2)
                    ex(ti)
                    pv(ti)
                P.op("dve", lambda e: e.reciprocal(out=rden[:, :], in_=pd[:, :]), reads=[psbuf[5]], writes=[rdenb])
                P.op("dve", lambda e, h=h: e.tensor_tensor(out=og[h][0][:, qs], in0=po[:, :], in1=rden[:, :], op=ALU.mult),
                     reads=[psbuf[4], rdenb], writes=og[h][1])
                if h == 0 and qb == 0:
                    dbgout("rden", rden[:, :], [rdenb], [128, 512], F32)
                if h == 0 and qb == 1:
                    dbgout("og0", og[0][0], og[0][1], [128, T], BF16)

        cnt["single"] = False

        def rhs_og(k):
            return og[k][0], og[k][1]
        for o in range(16):
            dense(rhs_og, 16, epi_resid(o, final=True))


    if has("S1") or has("S2a"):
        OL_d = xio("OL", [NH, 128, T], F32, has("S1"))
        QC_d = xio("QC", [NH, 128, T], BF16, has("S1"))
        G_d = xio("G", [NH, 128, T], BF16, has("S1"))
    if has("S1"):
        SF_d = xio("SF", [NH, 128, 128], F32, True)
    if has("S2a") and not fused:
        SFp_d = xio("SFprev", [NH, 128, 128], F32, False)

    def hgrn1():
        rmsnorm("mix0")
        cnt["single"] = True
        cnt["group"] = 2
        lbv, lbb = Y.view(0, 16, F32)
        oml, omlb = Y.view(64, 16, F32)
        noml, nomlb = Y.view(128, 16, F32)
        dec = [Y.view(256 + i * 256, NCHUNK, F32) for i in range(2)]
        attm = [Y.view(768 + i * 256, 128, BF16) for i in range(2)]
        S32 = [Y.view(1280, 128, F32), Y.view(13568, 128, F32)]
        m01, m01b = Y.view(1792, 128, F32)
        cmask, cmb = Y.view(2304, T, F32)
        onesf, onfb = Y.view(6400, T, F32)
        S16 = [Y.view(10496 + i * 256, 128, BF16) for i in range(12)]
        constb = lbb
        P.op("dve", lambda e: e.tensor_tensor(out=lbv, in0=vecs[:, VEC_COLS["lb0"][0]:VEC_COLS["lb0"][0] + 16],
                                              in1=vecs[:, VEC_COLS["lb1"][0]:VEC_COLS["lb1"][0] + 16], op=ALU.subtract), reads=[vecbuf], writes=constb)
        P.op("act", lambda e: e.activation(out=lbv, in_=lbv, func=AF.Sigmoid), reads=constb, writes=constb)
        P.op("dve", lambda e: e.tensor_scalar(out=oml, in0=lbv, scalar1=-1.0, scalar2=1.0, op0=ALU.mult, op1=ALU.add), reads=constb, writes=constb)
        P.op("dve", lambda e: e.tensor_scalar(out=noml, in0=lbv, scalar1=-1.0, scalar2=None, op0=ALU.add), reads=constb, writes=constb)
        P.op("pool", lambda e: e.memset(onesf, 1.0), writes=onfb)
        P.op("pool", lambda e: e.memset(cmask, 1.0), writes=cmb)
        cm3 = cmask.rearrange("p (c i) -> p c i", i=CH)
        P.op("pool", lambda e: e.memset(cm3[:, :, 0:1], 0.0), writes=cmb)
        P.op("pool", lambda e: e.memset(m01, 1.0), writes=m01b)
        P.op("pool", lambda e: e.affine_select(out=m01, in_=m01, pattern=[[1, 128]], compare_op=ALU.is_ge, fill=0.0, base=0, channel_multiplier=-1),
             reads=m01b, writes=m01b)
        for g in range(4):
            P.op("pool", lambda e, g=g: e.affine_select(out=m01[32 * g:32 * g + 32, :], in_=m01[32 * g:32 * g + 32, :], pattern=[[-1, 128]],
                                                      compare_op=ALU.is_ge, fill=0.0, base=32 * (g + 1) - 1, channel_multiplier=0),
                 reads=m01b, writes=m01b)
        qs, qsb = X.view(0, T, F32)
        sg, sgb = X.view(4096, T, F32)
        lf, lfb = X.view(8192, T, F32)
        bb, bbb = X.view(12288, T, F32)
        Bc, Bcb = X.view(16384, T, F32)
        e1, e1b = X.view(20480, T, F32)
        bb3 = bb.rearrange("p (c i) -> p c i", i=CH)
        e13 = e1.rearrange("p (c i) -> p c i", i=CH)

        def hb(s, i):
            return A.view(s * 10240 + i * 2048, T, BF16)
        kec = [A.view(20480 + i * 2048, T, BF16, parts=32) for i in range(2)]
        vcl = [A.view(24576 + i * 2048, T, BF16, parts=32) for i in range(2)]
        o_out, oob = A.view(28672, T, F32)
        rb16 = rstd[:, :].bitcast(BF16)
        gs, gsb = rb16[:, 0:T], [rstdbuf]
        qcm, qcb = rb16[:, T:2 * T], [rstdbuf]
        scale = HD ** -0.5

        def stageA(h):
            s = h % 2
            q_in, qib = hb(s, 0); k_in, kib = hb(s, 1); vtm, vtb = hb(s, 2); keT, keTb = hb(s, 3); vT, vTb = hb(s, 4)
            dc, dcb = dec[s]

            def epi_act(dst, dstb, func):
                def f(hf, p, pb):
                    P.op("act", lambda e: e.activation(out=dst[:, hf * 512:(hf + 1) * 512], in_=p[:, :], func=func), reads=[pb], writes=dstb)
                return f
            def gate1():
                P.op("act", lambda e: e.activation(out=lf, in_=sg, func=AF.Ln, scale=oml[:, h:h + 1], bias=lbv[:, h:h + 1]), reads=sgb + constb, writes=lfb)
                yield
                P.op("dve", lambda e: e.tensor_scalar(out=sg, in0=sg, scalar1=noml[:, h:h + 1], scalar2=oml[:, h:h + 1], op0=ALU.mult, op1=ALU.add),
                     reads=sgb + constb, writes=sgb)
                yield
                P.op("dve", lambda e: e.tensor_tensor_scan(out=bb, data0=cmask, data1=lf, initial=0.0, op0=ALU.mult, op1=ALU.add), reads=lfb + cmb, writes=bbb)
                yield
                P.op("dve", lambda e: e.tensor_tensor_scan(out=Bc, data0=onesf, data1=lf, initial=0.0, op0=ALU.mult, op1=ALU.add), reads=lfb + onfb, writes=Bcb)
                yield
                P.op("act", lambda e: e.activation(out=lf, in_=bb, func=AF.Exp, scale=-1.0), reads=bbb, writes=lfb)
                yield
                P.op("dve", lambda e: e.tensor_tensor(out=k_in, in0=sg, in1=lf, op=ALU.mult), reads=sgb + lfb, writes=kib)
                yield
                P.op("act", lambda e: e.activation(out=dc, in_=bb3[:, :, CH - 1], func=AF.Exp), reads=bbb, writes=dcb)
                yield
                P.op("dve", lambda e: e.tensor_tensor(out=e13, in0=bb3[:, :, CH - 1:CH].broadcast_to([128, NCHUNK, CH]), in1=bb3, op=ALU.subtract),
                     reads=bbb, writes=e1b)
                yield
                P.op("act", lambda e: e.activation(out=e1, in_=e1, func=AF.Exp), reads=e1b, writes=e1b)
                yield
                P.op("dve", lambda e: e.tensor_tensor(out=keT, in0=sg, in1=e1, op=ALU.mult), reads=sgb + e1b, writes=keTb)
                yield
                P.op("act", lambda e: e.activation(out=e1, in_=bb, func=AF.Exp), reads=bbb, writes=e1b)
                yield
                P.op("act", lambda e: e.activation(out=lf, in_=Bc, func=AF.Exp), reads=Bcb, writes=lfb)
                yield

            def mix(ga, gb, ratio=2):
                alive_a = alive_b = True
                while alive_a or alive_b:
                    for _ in range(ratio):
                        if alive_a:
                            try:
                                next(ga)
                            except StopIteration:
                                alive_a = False
                        yield
                    if alive_b:
                        try:
                            next(gb)
                        except StopIteration:
                            alive_b = False

            def three_tiles():
                yield from dense_gen(rhs_u, 16, epi_act(qs, qsb, AF.Silu)); yield
                yield from dense_gen(rhs_u, 16, epi_act(vT, vTb, AF.Copy)); yield
                yield from dense_gen(rhs_u, 16, epi_act(gs, gsb, AF.Silu)); yield
            yield from dense_gen(rhs_u, 16, epi_act(sg, sgb, AF.Sigmoid)); yield
            yield from mix(three_tiles(), gate1(), ratio=2)
            P.dma("sp", lambda e: e.dma_start(out=G_d[h], in_=gs), chan="out_g", reads=gsb, writes=[dbuf(f"G{h}")])
            P.op("dve", lambda e: e.scalar_tensor_tensor(out=q_in, in0=qs, scalar=scale, in1=e1, op0=ALU.mult, op1=ALU.mult), reads=qsb + e1b, writes=qib)
            yield
            P.op("dve", lambda e: e.scalar_tensor_tensor(out=qcm, in0=qs, scalar=scale, in1=lf, op0=ALU.mult, op1=ALU.mult), reads=qsb + lfb, writes=qcb)
            yield
            P.dma("sp", lambda e: e.dma_start(out=QC_d[h], in_=qcm), chan="out_qc", reads=qcb, writes=[dbuf(f"QC{h}")])
            yield
            for (src, srcb, dst, dstb, bank) in ((vT, vTb, vtm, vtb, 4),):
                ptv = ps[bank][:, :].bitcast(BF16)
                for j in range(8):
                    P.op("pe", lambda e, j=j, src=src, ptv=ptv: e.transpose(out=ptv[:, j * 128:(j + 1) * 128], in_=src[:, j * 128:(j + 1) * 128], identity=ident_b[:, :]),
                         reads=srcb + [identbuf], writes=[psbuf[bank]])
                P.op("act", lambda e, dst=dst, ptv=ptv: e.activation(out=dst, in_=ptv, func=AF.Copy), reads=[psbuf[bank]], writes=dstb)
            yield

        def rec(h):
            s = h % 2
            q_in, qib = hb(s, 0); k_in, kib = hb(s, 1); vtm, vtb = hb(s, 2); keT, keTb = hb(s, 3); vT, vTb = hb(s, 4)
            dc, dcb = dec[s]

            def prep_batch(bt):
                for (src, srcb, (dst, dstb), bank) in ((keT, keTb, kec[bt % 2], 4), (vT, vTb, vcl[bt % 2], 4)):
                    ptv = ps[bank][:, :].bitcast(BF16)
                    for ci in range(8):
                        c = 8 * bt + ci
                        P.op("pe", lambda e: e.transpose(out=ptv[0:32, ci * 128:(ci + 1) * 128], in_=src[:, c * CH:(c + 1) * CH], identity=ident_b[:, :]),
                             reads=srcb + [identbuf], writes=[psbuf[bank]])
                    P.op("act", lambda e: e.activation(out=dst, in_=ptv[0:32, :], func=AF.Copy), reads=[psbuf[bank]], writes=dstb)
                    yield

            def kv_tile(jt):
                for cc in range(4):
                    c = 4 * jt + cc
                    bt, ci = c // 8, c % 8
                    kc, kcb = kec[bt % 2]; vc_, vcb = vcl[bt % 2]
                    kvbank = (2, 3, 7)[c % 3]
                    kvp = ps[kvbank][:, 0:128]; kvb = psbuf[kvbank]
                    so, sob = S32[c % 2]; sn32, sn32b = S32[(c + 1) % 2]
                    P.op("pe", lambda e: e.matmul(kvp, lhsT=kc[:, ci * 128:(ci + 1) * 128], rhs=vc_[:, ci * 128:(ci + 1) * 128], start=True, stop=True),
                         reads=kcb + vcb, writes=[kvb])
                    if c == 0:
                        P.op("dve", lambda e: e.tensor_copy(out=sn32, in_=kvp), reads=[kvb], writes=sn32b)
                    else:
                        P.op("dve", lambda e: e.scalar_tensor_tensor(out=sn32, in0=so, scalar=dc[:, c:c + 1], in1=kvp, op0=ALU.mult, op1=ALU.add),
                             reads=[kvb] + sob + dcb, writes=sn32b)
                    if c < NCHUNK - 1:
                        sn, snb = S16[(c + 1) % 12]
                        P.op("act", lambda e: e.activation(out=sn, in_=sn32, func=AF.Copy), reads=sn32b, writes=snb)
                    else:
                        P.dma("sp", lambda e: e.dma_start(out=SF_d[h], in_=sn32), chan="out_sf", reads=sn32b, writes=[dbuf(f"SF{h}")])
                    yield
            yield from prep_batch(0)
            yield from kv_tile(0)
            for jt in range(8):
                tsl = slice(jt * 128, (jt + 1) * 128)
                r = jt % 2
                ap_att = ps[5][:, r * 128:(r + 1) * 128]; ab = psbuf[5]
                ap_o = ps[6][:, r * 128:(r + 1) * 128]; ob = psbuf[6]
                am, amb = attm[r]
                P.op("pe", lambda e: e.matmul(ap_att, lhsT=k_in[:, tsl], rhs=q_in[:, tsl], start=True, stop=True), reads=kib + qib, writes=[ab])
                yield
                if jt % 2 == 0 and jt // 2 + 1 < 4:
                    yield from prep_batch(jt // 2 + 1)
                if jt < 7:
                    yield from kv_tile(jt + 1)
                P.op("dve", lambda e: e.tensor_tensor(out=am, in0=ap_att, in1=m01, op=ALU.mult), reads=[ab] + m01b, writes=amb)
                P.op("pe", lambda e: e.matmul(ap_o, lhsT=vtm[:, tsl], rhs=am, start=True, stop=(jt == 0), skip_group_check=True),
                     reads=vtb + amb, writes=[ob])
                for cc in range(4):
                    c = 4 * jt + cc
                    if c == 0:
                        continue
                    sc, scb = S16[c % 12]
                    last = cc == 3
                    P.op("pe", lambda e: e.matmul(ap_o[:, 32 * cc:32 * cc + 32], lhsT=sc, rhs=q_in[:, c * CH:(c + 1) * CH],
                                                  start=False, stop=last, skip_group_check=True),
                         reads=scb + qib, writes=[ob])
                P.op("act", lambda e: e.activation(out=o_out[:, tsl], in_=ap_o, func=AF.Copy), reads=[ob], writes=oob)
                yield
            P.dma("sp", lambda e: e.dma_start(out=OL_d[h], in_=o_out), chan="out_ol", reads=oob, writes=[dbuf(f"OL{h}")])

        H1LIM = getattr(build, "H1LIM", None)
        if H1LIM is not None:
            g_ = stageA(0)
            for _ in range(3):
                next(g_)
            if H1LIM >= 1:
                next(g_)
                dbgout("q_in", hb(0, 0)[0], hb(0, 0)[1], [128, T], BF16)
                dbgout("k_in", hb(0, 1)[0], hb(0, 1)[1], [128, T], BF16)
                dbgout("keT", hb(0, 3)[0], hb(0, 3)[1], [128, T], BF16)
                dbgout("vtm", hb(0, 2)[0], hb(0, 2)[1], [128, T], BF16)
                dbgout("bb", bb, bbb, [128, T], F32)
                dbgout("Bc", Bc, Bcb, [128, T], F32)
                dbgout("dec", dec[0][0], dec[0][1], [128, NCHUNK], F32)
                dbgout("m01", m01, m01b, [128, 128], F32)
            if H1LIM >= 2:
                r_ = rec(0)
                for _ in range(H1LIM - 1):
                    next(r_)
                dbgout("o_out", o_out, oob, [128, T], F32)
            return
        prev = None
        for h in range(NH + 1):
            gens = []
            if h < NH:
                gens.append(stageA(h))
            if prev is not None:
                gens.append(rec(prev))
            while gens:
                for g_ in list(gens):
                    try:
                        next(g_)
                    except StopIteration:
                        gens.remove(g_)
            prev = h if h < NH else None
        cnt["single"] = False
        cnt["group"] = 4
        if fused:
            fz["SFp"] = allgather("SF", SF_d.rearrange("h p e -> (h p) e"), NH * 128, 128, F32,
                                  [dbuf(f"SF{h}") for h in range(NH)]).rearrange("(h p) e -> h p e", h=NH)

    def hgrn2():
        og = [A.view(h * 2048, T, BF16) for h in range(NH)]
        o32 = [X.view(0, T, F32), X.view(4096, T, F32), X.view(18432, T, F32)]
        qc = [X.view(8192 + i * 2048, T, BF16) for i in range(2)]
        gsl = [X.view(12288, T, BF16), X.view(14336, T, BF16), X.view(22528, T, BF16)]
        s32 = [X.view(16384 + i * 512, 128, F32) for i in range(2)]
        s16 = [X.view(17408 + i * 256, 128, BF16) for i in range(2)]
        def load(h):
            s = h % 2
            o_ap, o_b = o32[h % 3]; q_ap, q_b = qc[s]; g_ap, g_b = gsl[h % 3]; sa, sab = s32[s]; sh, shb = s16[s]
            SFp = fz["SFp"] if fused else SFp_d
            P.dma("pool", lambda e: e.dma_start(out=o_ap, in_=OL_d[h]), chan=f"ld_o{h % 3}", reads=[dbuf(f"OL{h}")], writes=o_b)
            P.dma("sp", lambda e: e.dma_start(out=q_ap, in_=QC_d[h]), chan=f"ld_q{s}", reads=[dbuf(f"QC{h}")], writes=q_b)
            P.dma("sp", lambda e: e.dma_start(out=g_ap, in_=G_d[h]), chan=f"ld_g{h % 3}", reads=[dbuf(f"G{h}")], writes=g_b)
            P.dma("sp", lambda e: e.dma_start(out=sa, in_=SFp[h]), chan=f"ld_s{s}", reads=[dbuf("SF_all")], writes=sab)
            P.op("dve", lambda e: e.tensor_scalar(out=sh, in0=sa, scalar1=vcol("flag"), scalar2=None, op0=ALU.mult), reads=sab + [vecbuf], writes=shb)

        unit_state = {}

        def front(h, hf):
            s = h % 2
            o_ap, o_b = o32[h % 3]; q_ap, q_b = qc[s]; sh, shb = s16[s]
            sl = slice(hf * 512, (hf + 1) * 512)
            pc = ps[4 + hf]; pcb = psbuf[4 + hf]
            pm = ps[6 + hf]; pmb = psbuf[6 + hf]
            P.op("pe", lambda e: e.matmul(pc[:, :], lhsT=sh, rhs=q_ap[:, sl], start=True, stop=True), reads=shb + q_b, writes=[pcb])
            P.op("dve", lambda e: e.tensor_tensor(out=o_ap[:, sl], in0=o_ap[:, sl], in1=pc[:, :], op=ALU.add), reads=[pcb] + o_b, writes=o_b)
            i = cnt["sq"] % 3; cnt["sq"] += 1
            P.op("dve", lambda e: e.tensor_tensor(out=sq[i][:, :], in0=o_ap[:, sl], in1=o_ap[:, sl], op=ALU.mult), reads=o_b, writes=[sqbuf[i]])
            P.op("pe", lambda e: e.matmul(pm[:, :], lhsT=ones_b[:, :], rhs=sq[i][:, :], start=True, stop=True), reads=[sqbuf[i], onesbuf], writes=[pmb])

        def back_act(h, hf):
            pm = ps[6 + hf]; pmb = psbuf[6 + hf]
            t, tb = newtmp()
            unit_state[(h, hf)] = (t, tb)
            P.op("act", lambda e: e.activation(out=t[:, :], in_=pm[:, :], func=AF.Ln, scale=1.0 / HD, bias=epsc[:, 0:1]), reads=[pmb, epsb], writes=[tb])
            P.op("act", lambda e: e.activation(out=t[:, :], in_=t[:, :], func=AF.Exp, scale=-0.5), reads=[tb], writes=[tb])

        def back_dve(h, hf):
            o_ap, o_b = o32[h % 3]; g_ap, g_b = gsl[h % 3]
            sl = slice(hf * 512, (hf + 1) * 512)
            t, tb = unit_state[(h, hf)]
            P.op("dve", lambda e: e.scalar_tensor_tensor(out=t[:, :], in0=o_ap[:, sl], scalar=vcol("hg"), in1=t[:, :], op0=ALU.mult, op1=ALU.mult),
                 reads=o_b + [tb, vecbuf], writes=[tb])
            P.op("dve", lambda e: e.tensor_tensor(out=og[h][0][:, sl], in0=t[:, :], in1=g_ap[:, sl], op=ALU.mult), reads=[tb] + g_b, writes=og[h][1])
        units = [(h, hf) for h in range(NH) for hf in range(2)]
        load(0)
        load(1)
        for k, (h, hf) in enumerate(units):
            if k > 1:
                back_act(*units[k - 2])
            front(h, hf)
            if k > 1:
                back_dve(*units[k - 2])
            if hf == 1 and h + 2 < NH:
                load(h + 2)
        for u in units[-2:]:
            back_act(*u)
            back_dve(*u)
        cnt["single"] = False

        def rhs_og(k):
            return og[k][0], og[k][1]
        for o in range(16):
            dense(rhs_og, 16, epi_resid(o, final=True))

    for seg in segs:
        if seg == "S2b":
            rmsnorm("mlp0", stats="piped" if has("S2a") else "compute"); mlp(); rmsnorm("ple0", stats="piped"); ple(0)
        elif seg == "S3b":
            rmsnorm("mlp1", stats="piped" if has("S3a") else "compute"); mlp(); rmsnorm("ple1", stats="piped"); ple(1)
            rmsnorm("fin", out_f32_inplace=True, stats="piped")
        elif seg == "S2c":
            kv_phase()
        elif seg == "S3a":
            attn_phase()
        elif seg == "S1":
            hgrn1()
        elif seg == "S2a":
            hgrn2()

    for c in range(NCH):
        P.dma(("sp", "act", "pool")[c % 3], lambda e, c=c: e.dma_start(out=hout_d[c * 128:(c + 1) * 128, :], in_=hT[:, c, :]), chan="hout", reads=[hbuf[c]])
    finals = ["hout"] + [c for c in P.chan_count if c.startswith("out_")]
    assert widx["i"] == len(b.plan) or getattr(build, "H1LIM", None) is not None, (widx["i"], len(b.plan))
    P.emit_all(nc, finals)
    b.es.close()
    return b


NCORES = 8
_PROG_CACHE = {}


def _get_prog(segs, fused=False):
    key = (tuple(segs), fused)
    if key not in _PROG_CACHE:
        _PROG_CACHE[key] = build(list(segs), fused=fused)
    return _PROG_CACHE[key]


def _run(segs, inp, per_core, fused=False):
    b = _get_prog(segs, fused)
    slabs = [pack_slab(inp, s) for s in b.plan]
    walls = {f"wall{i}": np.stack(slabs[i * b.wch:(i + 1) * b.wch]) for i in range(len(b.walls))}
    wple = np.ascontiguousarray(np.stack([np.asarray(inp["w_ple_up"][l], np.float32).reshape(2, 128, 2048).transpose(1, 0, 2).reshape(128, 4096)
                                          for l in range(2)]))
    maps = []
    for r in range(NCORES):
        m = {"wple": wple, "vecs": pack_vecs(inp, r)}
        m.update(walls)
        m.update(per_core[r])
        maps.append(m)
    res = run_bass_kernel_spmd(b.nc, maps, core_ids=list(range(NCORES)))
    return res.results


ALL_SEGS = ["S1", "S2a", "S2b", "S2c", "S3a", "S3b"]


def kernel(**inputs):
    inp = {k: np.asarray(v) for k, v in inputs.items()}
    x = np.asarray(inp["x"], np.float32)
    p = np.asarray(inp["p"], np.float32)
    base = []
    for r in range(NCORES):
        bb, sl = r // 2, slice((r % 2) * T, (r % 2 + 1) * T)
        base.append({"hin": np.ascontiguousarray(x[bb, sl].T), "pT": np.ascontiguousarray(p[:, bb, sl].transpose(0, 2, 1))})
    o = _run(ALL_SEGS, inp, base, fused=True)
    out = np.empty((4, SEQ, D), np.float32)
    for r in range(NCORES):
        bb, sl = r // 2, slice((r % 2) * T, (r % 2 + 1) * T)
        out[bb, sl] = np.asarray(o[r]["hout"]).T
    return out
```

```python
import numpy as np
from contextlib import ExitStack
import concourse.bass as bass
import concourse.mybir as mybir
from concourse.bass_utils import run_bass_kernel_spmd

F32 = mybir.dt.float32
BF16 = mybir.dt.bfloat16
AF = mybir.ActivationFunctionType
ALU = mybir.AluOpType

D = 2048
NCH = 16
T = 1024
SEQ = 2048
NH = 16
HD = 128
PLE = 256
DFF = 8192
EPS = 1e-6
CH = 32
NCHUNK = T // CH
NSLOT = 6
NEG = -30000.0


class Buf:
    __slots__ = ("name", "w", "rs", "excl")

    def __init__(self, name, excl=False):
        self.name = name
        self.w = None
        self.rs = {}
        self.excl = excl


class Op:
    __slots__ = ("eng", "emit", "deps", "sig", "val", "chan", "idx", "inc")

    def __init__(self, eng, emit, chan=None):
        self.eng = eng
        self.emit = emit
        self.deps = []
        self.sig = False
        self.val = None
        self.chan = chan
        self.idx = None
        self.inc = 16


class _Rec:
    def __init__(self):
        self.calls = []

    def __getattr__(self, name):
        def f(*a, **k):
            self.calls.append((name, a, k))
            return self
        return f


def _freeze(emit):
    r = _Rec()
    emit(r)
    assert len(r.calls) == 1, r.calls
    name, a, k = r.calls[0]
    return lambda eng: getattr(eng, name)(*a, **k)


class Prog:
    ENGS = ("pe", "act", "dve", "pool", "sp")

    def __init__(self, same_engine_sync=True):
        self.ops = {e: [] for e in self.ENGS}
        self.chan_count = {}
        self.same_engine_sync = same_engine_sync
        self.nops = 0

    @staticmethod
    def _flat(x):
        out = []
        for b in x:
            if isinstance(b, (list, tuple)):
                out.extend(Prog._flat(b))
            else:
                out.append(b)
        return out

    def _track(self, op, reads, writes):
        reads = self._flat(reads)
        writes = self._flat(writes)
        writes = writes + [b for b in reads if b.excl]
        reads = [b for b in reads if not b.excl]
        deps = []
        for b in reads:
            if b.w is not None:
                deps.append(b.w)
        for b in writes:
            if b.w is not None:
                deps.append(b.w)
            deps.extend(b.rs.values())
        for b in writes:
            b.w = op
            b.rs = {}
        for b in reads:
            key = op.chan if op.chan is not None else op.eng
            b.rs[key] = op
        seen = set()
        for d in deps:
            if d is op or id(d) in seen:
                continue
            seen.add(id(d))
            if d.chan is None and d.eng == op.eng:
                if op.eng == "pe" or not self.same_engine_sync:
                    continue
                if op.chan is not None:
                    pass
            d.sig = True
            op.deps.append(d)

    def op(self, eng, emit, reads=(), writes=()):
        o = Op(eng, _freeze(emit))
        o.idx = self.nops
        self.nops += 1
        self._track(o, reads, writes)
        self.ops[eng].append(o)
        return o

    def dma(self, queue, emit, chan, reads=(), writes=()):
        o = Op(queue, _freeze(emit), chan=chan)
        o.idx = self.nops
        self.nops += 1
        self._track(o, reads, writes)
        c = self.chan_count.get(chan, 0) + 16
        self.chan_count[chan] = c
        o.val = c
        o.sig = True
        self.ops[queue].append(o)
        return o

    def coll(self, emit, chan, reads=(), writes=()):
        o = Op("pool", _freeze(emit), chan=chan)
        o.idx = self.nops
        self.nops += 1
        self._track(o, reads, writes)
        c = self.chan_count.get(chan, 0) + 1
        self.chan_count[chan] = c
        o.val = c
        o.sig = True
        o.inc = 1
        self.ops["pool"].append(o)
        return o

    def emit_all(self, nc, final_chans):
        with ExitStack() as es:
            esem = {e: es.enter_context(nc.semaphore("s_" + e)) for e in self.ENGS}
            csem = {c: es.enter_context(nc.semaphore("c_" + c)) for c in self.chan_count}
            for e in self.ENGS:
                cnt = 0
                for o in self.ops[e]:
                    if o.chan is None and o.sig:
                        cnt += 1
                        o.val = cnt

            def run(eng_name, eng):
                seen = {}
                for o in self.ops[eng_name]:
                    for d in o.deps:
                        if d.chan is not None:
                            key, sem = ("c", d.chan), csem[d.chan]
                        else:
                            key, sem = ("e", d.eng), esem[d.eng]
                        if seen.get(key, 0) >= d.val:
                            continue
                        seen[key] = d.val
                        eng.wait_ge(sem, d.val)
                    ins = o.emit(eng)
                    if o.chan is not None:
                        ins.then_inc(csem[o.chan], o.inc)
                    elif o.sig:
                        ins.then_inc(esem[eng_name], 1)
                if eng_name == "sp":
                    for c in final_chans:
                        eng.wait_ge(csem[c], self.chan_count[c])

            with nc.Block() as block:
                @block.tensor
                def _(e):
                    run("pe", e)

                @block.scalar
                def _(e):
                    run("act", e)

                @block.vector
                def _(e):
                    run("dve", e)

                @block.gpsimd
                def _(e):
                    run("pool", e)

                @block.sync
                def _(e):
                    run("sp", e)


VEC_COLS = {}
_c = 0
for _name, _n in [("mix0", 16), ("mix1", 16), ("mlp0", 16), ("mlp1", 16), ("ple0", 16), ("ple1", 16),
                  ("kvn", 16), ("fin", 16), ("lb0", 16), ("lb1", 16), ("hg", 1), ("bf", 1),
                  ("flag", 1), ("prevbias", 1)]:
    VEC_COLS[_name] = (_c, _n)
    _c += _n
NVEC = _c


def pack_vecs(inp, rank):
    v = np.zeros((128, NVEC), np.float32)

    def put(name, arr):
        c0, n = VEC_COLS[name]
        v[:, c0:c0 + n] = np.asarray(arr, np.float32).reshape(n, 128).T

    put("mix0", inp["mix_norm"][0]); put("mix1", inp["mix_norm"][1])
    put("mlp0", inp["mlp_norm"][0]); put("mlp1", inp["mlp_norm"][1])
    put("ple0", inp["ple_norm"][0]); put("ple1", inp["ple_norm"][1])
    put("kvn", inp["kv_norm"]); put("fin", inp["final_norm"])
    put("lb0", inp["a_lb_logits"][0]); put("lb1", inp["a_lb_logits"][1])
    put("hg", inp["a_head_gain"][0])
    c0, _ = VEC_COLS["bf"]
    v[:16, c0] = np.asarray(inp["b_f"], np.float32)
    odd = rank % 2
    v[:, VEC_COLS["flag"][0]] = 1.0 if odd else 0.0
    v[:, VEC_COLS["prevbias"][0]] = 0.0 if odd else NEG
    return v


def pack_slab(inp, spec):
    kind = spec[0]
    if kind == "std":
        _, name, layer, r0, c0 = spec
        W = inp[name] if layer is None else inp[name][layer]
        blk = np.asarray(W[r0:r0 + 2048, c0:c0 + 128], np.float32)
        return blk.reshape(16, 128, 128).transpose(1, 0, 2).reshape(128, 2048)
    if kind == "ple":
        _, layer, o8 = spec
        W = np.asarray(inp["w_ple_up"][layer][:, o8 * 1024:(o8 + 1) * 1024], np.float32)
        blk = W.reshape(2, 128, 8, 128).transpose(1, 2, 0, 3)
        return blk.reshape(128, 2048)
    if kind == "flog":
        W = np.asarray(inp["w_kvf"][:, 4096:4112], np.float32)
        out = np.zeros((128, 2048), np.float32)
        out[:, :256] = W.reshape(16, 128, 16).transpose(1, 0, 2).reshape(128, 256)
        return out
    raise ValueError(kind)


class Builder:
    def __init__(self, segs, fused):
        self.segs = segs
        self.fused = fused
        self.plan = []
        self.nc = bass.Bass("TRN2", target_bir_lowering=False)
        self.P = Prog()
        self.es = ExitStack()
        self.final_chans = []
        self._n = 0

    def sb(self, name, shape, dt):
        return self.es.enter_context(self.nc.sbuf_tensor(name, shape, dt))

    def dram(self, name, shape, dt, kind):
        if kind == "Internal":
            return self.nc.dram_tensor(name, shape, dt).ap()
        return self.nc.dram_tensor(name, shape, dt, kind=kind).ap()

    def uniq(self, s):
        self._n += 1
        return f"{s}{self._n}"

    def wplan(self, specs):
        i0 = len(self.plan)
        self.plan.extend(specs)
        return i0

    def wacquire(self, idx):
        hi = min(len(self.plan), idx + NSLOT)
        while self.w_issued < hi:
            j = self.w_issued
            s = j % NSLOT
            src = self.walls[j // self.wch][j % self.wch]
            dst = self.wslot[s]
            self.P.dma("pool", lambda e, dst=dst, src=src: e.dma_start(out=dst[:, :], in_=src),
                       chan=f"w{s}", reads=(self.w_first_reads if 0 < j < NSLOT else ()), writes=[self.wbuf[s]])
            self.w_issued += 1
        s = idx % NSLOT
        return self.wslot[s], self.wbuf[s]


def make_plan(segs):
    plan = []

    def mlp(l):
        for g in range(4):
            for j in range(16):
                plan.append(("std", "w_mlp_up", l, 0, (g * 16 + j) * 128))
            for o in range(16):
                plan.append(("std", "w_mlp_down", l, g * 2048, o * 128))

    def ple(l):
        for o in range(16):
            plan.append(("std", "w_ple_gate", l, 0, o * 128))

    for s in segs:
        if s == "S1":
            for h in range(NH):
                for j in (1, 0, 2, 3):
                    plan.append(("std", "w_a_in", 0, 0, j * 2048 + h * 128))
        elif s == "S2a":
            for o in range(16):
                plan.append(("std", "w_a_out", 0, 0, o * 128))
        elif s == "S2b":
            mlp(0); ple(0)
        elif s == "S2c":
            plan.append(("flog",))
            for h in range(NH):
                plan.append(("std", "w_kvf", None, 0, h * 128))
            for h in range(NH):
                plan.append(("std", "w_kvf", None, 0, 2048 + h * 128))
        elif s == "S3a":
            for h in range(NH):
                plan.append(("std", "w_b_q", 0, 0, h * 128))
            for o in range(16):
                plan.append(("std", "w_b_out", 0, 0, o * 128))
        elif s == "S3b":
            mlp(1); ple(1)
    return plan


class Arena:
    PAGE = 256

    def __init__(self, b, name, nbytes):
        self.t = b.sb(name, [128, nbytes // 4], F32)
        self.pages = [Buf(f"{name}_pg{i}") for i in range(nbytes // self.PAGE)]
        self.nbytes = nbytes

    def view(self, off, nelem, dt, parts=128):
        esz = 4 if dt == F32 else 2
        nb = nelem * esz
        assert off % 4 == 0 and off + nb <= self.nbytes, (off, nb, self.nbytes)
        ap = self.t[0:parts, off // 4:(off + nb + 3) // 4]
        if dt != F32:
            ap = ap.bitcast(dt)
        pgs = self.pages[off // self.PAGE:(off + nb - 1) // self.PAGE + 1]
        return ap, pgs


def build(segs, fused=False):
    b = Builder(segs, fused)
    nc, P = b.nc, b.P
    b.plan = make_plan(segs)
    nslab = max(1, len(b.plan))
    WCH = 224
    b.walls = [b.dram(f"wall{i}", [min(WCH, nslab - i * WCH), 128, 2048], F32, "ExternalInput") for i in range((nslab + WCH - 1) // WCH)]
    b.wch = WCH
    vecs_d = b.dram("vecs", [128, NVEC], F32, "ExternalInput")
    hin_d = b.dram("hin", [D, T], F32, "ExternalInput")
    hout_d = b.dram("hout", [D, T], F32, "ExternalOutput")
    pT_d = b.dram("pT", [2, PLE, T], F32, "ExternalInput")
    wple_d = b.dram("wple", [2, 128, 2 * D], F32, "ExternalInput")
    io = {}

    def xio(name, shape, dt, produced):
        kind = "Internal" if fused else ("ExternalOutput" if produced else "ExternalInput")
        io[name] = b.dram(name, shape, dt, kind)
        return io[name]

    PAIRS = [[2 * i, 2 * i + 1] for i in range(4)]
    dbufs = {}

    def dbuf(name):
        if name not in dbufs:
            dbufs[name] = Buf("dram_" + name)
        return dbufs[name]

    def allgather(name, src2d, rows, cols, dt, src_bufs):
        dst = b.dram(name + "_all", [2 * rows, cols], dt, "Internal")
        P.coll(lambda e: e.collective_compute("AllGather", ALU.bypass, replica_groups=PAIRS, ins=[src2d.opt()], outs=[dst.opt()]),
               chan="cc_" + name, reads=src_bufs, writes=[dbuf(name + "_all")])
        return dst[0:rows, :]

    has = lambda s: s in segs
    DBG = getattr(build, "DBG", False)

    def dbgout(name, ap, reads, shape, dt):
        if not DBG:
            return
        dd = b.dram("dbg_" + name, shape, dt, "ExternalOutput")
        P.dma("sp", lambda e: e.dma_start(out=dd, in_=ap), chan="out_dbg", reads=reads)
    if has("S2c") or has("S3a"):
        Kown_d = xio("Kown", [NH, 128, T], BF16, has("S2c"))
        Vown_d = xio("Vown", [NH, 128, 8 * 128], BF16, has("S2c"))
        Down_d = xio("Dloc", [NH, T], F32, has("S2c"))
    if has("S3a") and not fused:
        Kprev_d = xio("Kprev", [NH, 128, T], BF16, False)
        Vprev_d = xio("Vprev", [NH, 128, 8 * 128], BF16, False)
        Dprev_d = xio("Dprev", [NH, T], F32, False)
    fz = {}

    hT = b.sb("hT", [128, NCH, T], F32)
    hbuf = [Buf(f"h{c}") for c in range(NCH)]
    b.w_first_reads = [hbuf[NCH - 2], hbuf[NCH - 1]]
    uT = b.sb("uT", [128, NCH, T], BF16)
    ubuf = [Buf(f"u{c}") for c in range(NCH)]
    A = Arena(b, "A", 32 * 1024)
    X = Arena(b, "X", 24 * 1024)
    b.wslot = [b.sb(f"w{s}", [128, 2048], BF16) for s in range(NSLOT)]
    b.wbuf = [Buf(f"w{s}") for s in range(NSLOT)]
    b.w_issued = 0
    vecs = b.sb("vecs_sb", [128, NVEC], F32); vecbuf = Buf("vecs")
    ones_b = b.sb("ones_b", [128, 128], BF16); onesbuf = Buf("ones")
    ident_b = b.sb("ident_b", [128, 128], BF16); identbuf = Buf("ident")
    ident_f = b.sb("ident_f", [128, 128], F32); identfbuf = Buf("identf")
    rstd = b.sb("rstd", [128, T], F32); rstdbuf = Buf("rstd")
    epsc = b.sb("epsc", [128, 2], F32); epsb = Buf("epsc")
    sq = [b.sb(f"sq{i}", [128, 512], BF16) for i in range(3)]
    sqbuf = [Buf(f"sq{i}") for i in range(3)]
    tmp = [b.sb(f"tmp{i}", [128, 512], F32) for i in range(4)]
    tmpbuf = [Buf(f"tmp{i}") for i in range(4)]
    ps = [b.es.enter_context(nc.psum_tensor(f"ps{i}", [128, 512], F32)) for i in range(8)]
    psbuf = [[Buf(f"ps{i}", excl=True)] for i in range(8)]
    Y = Arena(b, "Y", 14080)
    cnt = {"tmp": 0, "sq": 0, "par": 0}

    def vcol(name, i=0):
        c0, _ = VEC_COLS[name]
        return vecs[:, c0 + i:c0 + i + 1]

    P.dma("sp", lambda e: e.dma_start(out=vecs[:, :], in_=vecs_d), chan="misc", writes=[vecbuf])
    P.op("pool", lambda e: e.memset(ones_b[:, :], 1.0), writes=[onesbuf])
    P.op("pool", lambda e: e.memset(epsc[:, 0:1], EPS), writes=[epsb])
    P.op("pool", lambda e: e.memset(epsc[:, 1:2], 1.0), writes=[epsb])
    zf = tmp[0]
    P.op("pool", lambda e: e.memset(zf[:, 0:128], 0.0), writes=[tmpbuf[0]])
    P.op("pool", lambda e: e.affine_select(out=ident_f[:, :], in_=zf[:, 0:128], pattern=[[-1, 128]],
                                           compare_op=ALU.not_equal, fill=1.0, base=0, channel_multiplier=1),
         reads=[tmpbuf[0]], writes=[identfbuf])
    P.op("pool", lambda e: e.tensor_copy(out=ident_b[:, :], in_=ident_f[:, :]), reads=[identfbuf], writes=[identbuf])
    for c in range(NCH):
        P.dma(("sp", "act")[c % 2], lambda e, c=c: e.dma_start(out=hT[:, c, :], in_=hin_d[c * 128:(c + 1) * 128, :]),
              chan=f"hin{c}", writes=[hbuf[c]])

    pending = []

    def flush_stats():
        while pending:
            stats_chunk(pending.pop(0))

    def stats_chunk(c, sb=6):
        for hf in range(2):
            i = cnt["sq"] % 3; cnt["sq"] += 1
            P.op("act", lambda e: e.activation(out=sq[i][:, :], in_=hT[:, c, hf * 512:(hf + 1) * 512], func=AF.Square),
                 reads=[hbuf[c]], writes=[sqbuf[i]])
            P.op("pe", lambda e: e.matmul(ps[sb + hf][:, :], lhsT=ones_b[:, :], rhs=sq[i][:, :], start=(c == 0), stop=(c == NCH - 1)),
                 reads=[sqbuf[i], onesbuf], writes=[psbuf[sb + hf]])

    def rmsnorm(gname, out_f32_inplace=False, stats="compute"):
        sb = 4
        if stats == "compute":
            for c in range(NCH):
                stats_chunk(c, sb=4)
        elif stats == "piped":
            flush_stats()
            sb = 6
        if stats != "reuse":
            for hf in range(2):
                P.op("act", lambda e, hf=hf: e.activation(out=rstd[:, hf * 512:(hf + 1) * 512], in_=ps[sb + hf][:, :], func=AF.Ln, scale=1.0 / D, bias=epsc[:, 0:1]),
                     reads=[psbuf[sb + hf], epsb], writes=[rstdbuf])
            P.op("act", lambda e: e.activation(out=rstd[:, :], in_=rstd[:, :], func=AF.Exp, scale=-0.5), reads=[rstdbuf], writes=[rstdbuf])
        for c in range(NCH):
            if out_f32_inplace:
                P.op("dve", lambda e, c=c: e.scalar_tensor_tensor(out=hT[:, c, :], in0=hT[:, c, :], scalar=vcol(gname, c), in1=rstd[:, :],
                                                                  op0=ALU.mult, op1=ALU.mult),
                     reads=[hbuf[c], rstdbuf, vecbuf], writes=[hbuf[c]])
            else:
                P.op("dve", lambda e, c=c: e.scalar_tensor_tensor(out=uT[:, c, :], in0=hT[:, c, :], scalar=vcol(gname, c), in1=rstd[:, :],
                                                                  op0=ALU.mult, op1=ALU.mult),
                     reads=[hbuf[c], rstdbuf, vecbuf], writes=[ubuf[c]])

    widx = {"i": 0}

    def dense_gen(rhs, kch, epi, mcols=128, group=None):
        idx = widx["i"]; widx["i"] += 1
        slot, wb = b.wacquire(idx)
        w3 = slot[:, :].rearrange("p (k n) -> p k n", k=16) if mcols == 128 else slot[:, 0:16 * mcols].rearrange("p (k n) -> p k n", k=16)
        par = 0 if cnt.get("single") else cnt["par"] % 2
        cnt["par"] += 1
        n = 0
        group = group or cnt.get("group", 4)
        for k in range(kch):
            rap, rb = rhs(k)
            for hf in range(2):
                pi = 2 * par + hf
                P.op("pe", lambda e: e.matmul(ps[pi][0:mcols, :], lhsT=w3[:, k, :], rhs=rap[:, hf * 512:(hf + 1) * 512],
                                              start=(k == 0), stop=(k == kch - 1)),
                     reads=[wb] + rb, writes=[psbuf[pi]])
                n += 1
                if n % group == 0 and n < 2 * kch:
                    yield
        flush_stats()
        for hf in range(2):
            pi = 2 * par + hf
            epi(hf, ps[pi], psbuf[pi])

    def dense(rhs, kch, epi, mcols=128):
        for _ in dense_gen(rhs, kch, epi, mcols=mcols):
            pass

    def rhs_u(k):
        return uT[:, k, :], [ubuf[k]]

    def newtmp():
        i = cnt["tmp"] % 4; cnt["tmp"] += 1
        return tmp[i], tmpbuf[i]

    def epi_resid(o, final=False):
        def f(hf, p, pb):
            P.op("dve", lambda e: e.tensor_tensor(out=hT[:, o, hf * 512:(hf + 1) * 512], in0=hT[:, o, hf * 512:(hf + 1) * 512], in1=p[:, :], op=ALU.add),
                 reads=[pb, hbuf[o]], writes=[hbuf[o]])
            if final and hf == 1:
                pending.append(o)
        return f

    def mlp():
        hid = [A.view(j * 2048, T, BF16) for j in range(16)]

        def rhs_a(k):
            return hid[k][0], hid[k][1]
        for g in range(4):
            for j in range(16):
                def epi(hf, p, pb, j=j):
                    t, tb = newtmp()
                    P.op("act", lambda e: e.activation(out=t[:, :], in_=p[:, :], func=AF.Relu), reads=[pb], writes=[tb])
                    P.op("dve", lambda e: e.tensor_tensor(out=hid[j][0][:, hf * 512:(hf + 1) * 512], in0=t[:, :], in1=t[:, :], op=ALU.mult),
                         reads=[tb], writes=hid[j][1])
                dense(rhs_u, 16, epi)
            for o in range(16):
                dense(rhs_a, 16, epi_resid(o, final=(g == 3)))

    def ple(l):
        pT_sb, pTb = X.view(0, 2 * T, BF16)
        pT3 = pT_sb.rearrange("p (k t) -> p k t", k=2)
        wp_sb, wpb = X.view(4096, 2 * D, BF16)
        wp3 = wp_sb.rearrange("p (k n) -> p k n", k=2)
        P.dma("pool", lambda e: e.dma_start(out=pT3, in_=pT_d[l].rearrange("(k p) t -> p k t", p=128)), chan="pT", writes=pTb)
        for k in range(2):
            P.dma("pool", lambda e, k=k: e.dma_start(out=wp3[:, k, :], in_=wple_d[l][:, k * D:(k + 1) * D]), chan="wple", writes=wpb)
        for o in range(16):
            def epi(hf, p, pb, o=o):
                pp = ps[4 + hf]; ppb = psbuf[4 + hf]
                for k in range(2):
                    P.op("pe", lambda e, k=k: e.matmul(pp[:, :], lhsT=wp3[:, k, o * 128:(o + 1) * 128], rhs=pT3[:, k, hf * 512:(hf + 1) * 512],
                                                       start=(k == 0), stop=(k == 1)), reads=wpb + pTb, writes=[ppb])
                t, tb = newtmp()
                P.op("act", lambda e: e.activation(out=t[:, :], in_=p[:, :], func=AF.Sigmoid), reads=[pb], writes=[tb])
                t2, tb2 = newtmp()
                P.op("dve", lambda e: e.tensor_tensor(out=t2[:, :], in0=t[:, :], in1=pp[:, :], op=ALU.mult), reads=[tb, ppb], writes=[tb2])
                P.op("dve", lambda e: e.tensor_tensor(out=hT[:, o, hf * 512:(hf + 1) * 512], in0=hT[:, o, hf * 512:(hf + 1) * 512], in1=t2[:, :], op=ALU.add),
                     reads=[tb2, hbuf[o]], writes=[hbuf[o]])
                if hf == 1:
                    pending.append(o)
            dense(rhs_u, 16, epi)


    negD_, negDb = Y.view(0, 16 * NH, F32)
    negD = negD_.rearrange("p (j h) -> p j h", j=16)
    Dsb, Dsbb = Y.view(2048, 2 * T, F32, parts=16)
    dadd, daddb = Y.view(10240, 1, F32, parts=16)
    Dbf, Dbfb = Y.view(10496, T, BF16)
    sel_, selb = X.view(20480, NH * 128, BF16)
    sel = sel_.rearrange("p (h m) -> p h m", h=NH)
    dprep_done = []

    def d_prep():
        dprep_done.append(1)
        Dp = fz["Dprev"] if fused else Dprev_d
        P.dma("sp", lambda e: e.dma_start(out=Dsb[:, 0:T], in_=Dp), chan="dld", reads=[dbuf("D_all")], writes=[Dsbb])
        P.dma("sp", lambda e: e.dma_start(out=Dsb[:, T:2 * T], in_=Down_d), chan="dld", reads=[dbuf("D")], writes=[Dsbb])
        P.op("dve", lambda e: e.tensor_tensor(out=dadd[:, :], in0=Dsb[:, T - 1:T], in1=vcol("flag")[0:16, :], op=ALU.mult), reads=[Dsbb, vecbuf], writes=[daddb])
        P.op("dve", lambda e: e.tensor_scalar(out=Dsb[:, T:2 * T], in0=Dsb[:, T:2 * T], scalar1=dadd[:, 0:1], scalar2=None, op0=ALU.add),
             reads=[Dsbb, daddb], writes=[Dsbb])
        P.op("pool", lambda e: e.memset(Dbf, 0.0), writes=Dbfb)
        P.op("dve", lambda e: e.tensor_copy(out=Dbf[0:16, :], in_=Dsb[:, T:2 * T]), reads=[Dsbb] + Dbfb, writes=Dbfb)
        P.op("dve", lambda e: e.tensor_copy(out=sel, in_=ident_b[:, 0:16].unsqueeze(2).broadcast_to([128, NH, 128])), reads=[identbuf], writes=selb)
        for j in range(16):
            pj = ps[6 + j % 2]
            P.op("pe", lambda e, j=j, pj=pj: e.transpose(out=pj[:, 0:16], in_=Dsb[:, j * 128:(j + 1) * 128], identity=ident_f[0:16, 0:16]),
                 reads=[Dsbb, identfbuf], writes=[psbuf[6 + j % 2]])
            if j < 8:
                P.op("dve", lambda e, j=j, pj=pj: e.tensor_scalar(out=negD[:, j, :], in0=pj[:, 0:16], scalar1=-1.0, scalar2=vcol("prevbias"),
                                                                  op0=ALU.mult, op1=ALU.add), reads=[psbuf[6 + j % 2], vecbuf], writes=[negDb])
            else:
                P.op("dve", lambda e, j=j, pj=pj: e.tensor_scalar(out=negD[:, j, :], in0=pj[:, 0:16], scalar1=-1.0, scalar2=None, op0=ALU.mult),
                     reads=[psbuf[6 + j % 2]], writes=[negDb])

    def kv_phase():
        rmsnorm("kvn", stats="piped" if has("S2b") else "compute")
        kst = [X.view(i * 2048, T, BF16) for i in range(2)]
        vst = [X.view(4096 + i * 2048, T, BF16) for i in range(2)]
        vtm = [X.view(8192 + i * 2048, T, BF16) for i in range(2)]
        ones_f, onesfb = X.view(12288, T, F32)
        sp1, sp1b = X.view(16384, T, F32)
        dl, dlb = X.view(20480, T, F32)
        nbf, nbfb = tmp[3][:, 0:1], tmpbuf[3]
        P.op("pool", lambda e: e.memset(ones_f[:, :], 1.0), writes=onesfb)
        P.op("dve", lambda e: e.tensor_scalar(out=nbf[0:16, :], in0=vcol("bf")[0:16, :], scalar1=-1.0, scalar2=None, op0=ALU.mult),
             reads=[vecbuf], writes=[nbfb])

        def epi(hf, p, pb):
            sl = slice(hf * 512, (hf + 1) * 512)
            P.op("act", lambda e: e.activation(out=sp1[0:16, sl], in_=p[0:16, :], func=AF.Exp, scale=-1.0, bias=nbf[0:16, :]),
                 reads=[pb, nbfb], writes=sp1b)
            P.op("act", lambda e: e.activation(out=sp1[0:16, sl], in_=sp1[0:16, sl], func=AF.Ln, bias=epsc[0:16, 1:2]), reads=sp1b + [epsb], writes=sp1b)
            if hf == 1:
                P.op("dve", lambda e: e.tensor_tensor_scan(out=dl[0:16, :], data0=ones_f[0:16, :], data1=sp1[0:16, :], initial=0.0,
                                                           op0=ALU.mult, op1=ALU.add), reads=sp1b + onesfb, writes=dlb)
                P.op("dve", lambda e: e.tensor_scalar(out=dl[0:16, :], in0=dl[0:16, :], scalar1=-1.0, scalar2=None, op0=ALU.mult), reads=dlb, writes=dlb)
                P.dma("sp", lambda e: e.dma_start(out=Down_d, in_=dl[0:16, :]), chan="out_d", reads=dlb, writes=[dbuf("D")])
        dense(rhs_u, 16, epi, mcols=16)
        if fused:
            fz["Dprev"] = allgather("D", Down_d, NH, T, F32, [dbuf("D")])
        if fused:
            d_prep()
        for h in range(NH):
            def epi(hf, p, pb, h=h):
                k_ap, kb = kst[h % 2]
                P.op("act", lambda e: e.activation(out=k_ap[:, hf * 512:(hf + 1) * 512], in_=p[:, :], func=AF.Copy), reads=[pb], writes=kb)
                if hf == 1:
                    P.dma("sp", lambda e: e.dma_start(out=Kown_d[h], in_=k_ap), chan=f"out_k{h % 2}", reads=kb, writes=[dbuf(f"K{h}")])
            dense(rhs_u, 16, epi)
            if fused and h % 4 == 3:
                g = h // 4
                fz[f"Kprev{g}"] = allgather(f"K{g}_", Kown_d[4 * g:4 * g + 4].rearrange("h p t -> (h p) t"), 512, T, BF16,
                                            [dbuf(f"K{hh}") for hh in range(4 * g, 4 * g + 4)]).rearrange("(h p) t -> h p t", h=4)
        for h in range(NH):
            def epi(hf, p, pb, h=h):
                v_ap, vb = vst[h % 2]
                t_ap, tb = vtm[h % 2]
                P.op("act", lambda e: e.activation(out=v_ap[:, hf * 512:(hf + 1) * 512], in_=p[:, :], func=AF.Copy), reads=[pb], writes=vb)
                pt = ps[6 + hf][:, :].bitcast(BF16)
                for j in range(4):
                    jj = hf * 4 + j
                    P.op("pe", lambda e, j=j, jj=jj: e.transpose(out=pt[:, j * 128:(j + 1) * 128], in_=v_ap[:, jj * 128:(jj + 1) * 128], identity=ident_b[:, :]),
                         reads=vb + [identbuf], writes=[psbuf[6 + hf]])
                P.op("dve", lambda e: e.tensor_copy(out=t_ap[:, hf * 512:(hf + 1) * 512], in_=pt[:, 0:512]), reads=[psbuf[6 + hf]], writes=tb)
                if hf == 1:
                    P.dma("sp", lambda e: e.dma_start(out=Vown_d[h], in_=t_ap), chan=f"out_v{h % 2}", reads=tb, writes=[dbuf(f"V{h}")])
            dense(rhs_u, 16, epi)
            if fused and h % 4 == 3:
                g = h // 4
                fz[f"Vprev{g}"] = allgather(f"V{g}_", Vown_d[4 * g:4 * g + 4].rearrange("h p t -> (h p) t"), 512, T, BF16,
                                            [dbuf(f"V{hh}") for hh in range(4 * g, 4 * g + 4)]).rearrange("(h p) t -> h p t", h=4)

    def attn_phase():
        rmsnorm("mix1", stats="reuse" if fused else "compute")
        kh = [X.view(i * 4096, 2 * T, BF16) for i in range(2)]
        vh = [X.view(8192 + i * 4096, 2 * T, BF16) for i in range(2)]
        qh = [X.view(16384 + i * 2048, T, BF16) for i in range(2)]
        masks = rstd[:, :].bitcast(BF16).rearrange("p (r t) -> p r t", r=4)
        maskb = [rstdbuf]
        pt, ptb = sq, sqbuf
        rden, rdenb = tmp[2], tmpbuf[2]
        og = [A.view(h * 2048, T, BF16) for h in range(NH)]
        P.op("pool", lambda e: e.memset(masks[:, :, :], 0.0), writes=maskb)
        for r in range(4):
            P.op("pool", lambda e, r=r: e.affine_select(out=masks[:, r, :], in_=masks[:, r, :], pattern=[[1, 512]], compare_op=ALU.is_ge,
                                                       fill=NEG, base=-128 * r, channel_multiplier=-1), reads=maskb, writes=maskb)
        if not dprep_done:
            d_prep()
        dbgout("negD", negD_, negDb, [128, 16 * NH], F32)
        dbgout("sel", sel_, selb, [128, NH * 128], BF16)
        dbgout("Dbf", Dbf, Dbfb, [128, T], BF16)
        dbgout("Dsb", Dsb, Dsbb, [16, 2 * T], F32)
        dbgout("masks", rstd[:, :], maskb, [128, T], F32)
        npt = [0]
        cnt["single"] = True
        SB = (6, 7, 2, 3)
        NHL = getattr(build, "NHL", NH)
        for h in range(NH):
            s = h % 2
            k_ap, kb = kh[s]; v_ap, vb = vh[s]; q_ap, qb_ = qh[s]
            if h >= NHL:
                def epi0(hf, p, pb, h=h):
                    P.op("act", lambda e: e.activation(out=og[h][0][:, hf * 512:(hf + 1) * 512], in_=p[:, :], func=AF.Copy), reads=[pb], writes=og[h][1])
                dense(rhs_u, 16, epi0)
                continue
            v3 = v_ap.rearrange("p (j e) -> p j e", j=16)
            Kph = fz[f"Kprev{h // 4}"][h % 4] if fused else Kprev_d[h]
            Vph = fz[f"Vprev{h // 4}"][h % 4] if fused else Vprev_d[h]
            P.dma("sp", lambda e, h=h, k_ap=k_ap: e.dma_start(out=k_ap[:, 0:T], in_=Kph), chan=f"kld{s}", reads=[dbuf(f"K{h // 4}__all")], writes=kb)
            P.dma("sp", lambda e, h=h, k_ap=k_ap: e.dma_start(out=k_ap[:, T:2 * T], in_=Kown_d[h]), chan=f"kld{s}", reads=[dbuf(f"K{h}")], writes=kb)
            P.dma("sp", lambda e, h=h, v_ap=v_ap: e.dma_start(out=v_ap[:, 0:T], in_=Vph), chan=f"vld{s}", reads=[dbuf(f"V{h // 4}__all")], writes=vb)
            P.dma("sp", lambda e, h=h, v_ap=v_ap: e.dma_start(out=v_ap[:, T:2 * T], in_=Vown_d[h]), chan=f"vld{s}", reads=[dbuf(f"V{h}")], writes=vb)

            def epi(hf, p, pb, q_ap=q_ap, qb_=qb_):
                P.op("act", lambda e: e.activation(out=q_ap[:, hf * 512:(hf + 1) * 512], in_=p[:, :], func=AF.Copy, scale=HD ** -0.5), reads=[pb], writes=qb_)
            dense(rhs_u, 16, epi)
            if h == 0:
                dbgout("kh", k_ap, kb, [128, 2 * T], BF16)
                dbgout("vh", v_ap, vb, [128, 2 * T], BF16)
                dbgout("qh", q_ap, qb_, [128, T], BF16)
            for qb in range(2):
                qs = slice(qb * 512, (qb + 1) * 512)
                tiles = [(j, None) for j in range(8)]
                for jj in range(4 * qb + 4):
                    tiles.append((8 + jj, jj - 4 * qb if jj >= 4 * qb else None))
                po, pd = ps[4], ps[5]
                ptidx = {}

                def qk(ti):
                    j, mr = tiles[ti]
                    pS = ps[SB[ti % 4]]; pSb = psbuf[SB[ti % 4]]
                    P.op("pe", lambda e: e.matmul(pS[:, :], lhsT=k_ap[:, j * 128:(j + 1) * 128], rhs=q_ap[:, qs], start=True, stop=False),
                         reads=kb + qb_, writes=[pSb])
                    P.op("pe", lambda e: e.matmul(pS[:, :], lhsT=sel[:, h, :], rhs=Dbf[:, qs], start=False, stop=True),
                         reads=selb + Dbfb, writes=[pSb])
                    if mr is not None:
                        P.op("dve", lambda e: e.tensor_tensor(out=pS[:, :], in0=pS[:, :], in1=masks[:, mr, :], op=ALU.add),
                             reads=[pSb] + maskb, writes=[pSb])

                def ex(ti):
                    j, mr = tiles[ti]
                    pS = ps[SB[ti % 4]]; pSb = psbuf[SB[ti % 4]]
                    i = npt[0] % 3; npt[0] += 1
                    ptidx[ti] = i
                    P.op("act", lambda e: e.activation(out=pt[i][:, :], in_=pS[:, :], func=AF.Exp, bias=negD[:, j, h:h + 1], scale=1.0),
                         reads=[pSb, negDb], writes=[ptb[i]])

                def pv(ti):
                    j, mr = tiles[ti]
                    i = ptidx[ti]
                    first, last = ti == 0, ti == len(tiles) - 1
                    P.op("pe", lambda e: e.matmul(po[:, :], lhsT=v3[:, j, :], rhs=pt[i][:, :], start=first, stop=last),
                         reads=vb + [ptb[i]], writes=[psbuf[4]])
                    P.op("pe", lambda e: e.matmul(pd[:, :], lhsT=ones_b[:, :], rhs=pt[i][:, :], start=first, stop=last),
                         reads=[onesbuf, ptb[i]], writes=[psbuf[5]])
                qk(0)
                if len(tiles) > 1:
                    qk(1)
                for ti in range(len(tiles)):
                    if ti + 2 < len(tiles):
                        qk(ti + 2)
                    ex(ti)
                    pv(ti)
                P.op("dve", lambda e: e.reciprocal(out=rden[:, :], in_=pd[:, :]), reads=[psbuf[5]], writes=[rdenb])
                P.op("dve", lambda e, h=h: e.tensor_tensor(out=og[h][0][:, qs], in0=po[:, :], in1=rden[:, :], op=ALU.mult),
                     reads=[psbuf[4], rdenb], writes=og[h][1])
                if h == 0 and qb == 0:
                    dbgout("rden", rden[:, :], [rdenb], [128, 512], F32)
                if h == 0 and qb == 1:
                    dbgout("og0", og[0][0], og[0][1], [128, T], BF16)

        cnt["single"] = False

        def rhs_og(k):
            return og[k][0], og[k][1]
        for o in range(16):
            dense(rhs_og, 16, epi_resid(o, final=True))


    if has("S1") or has("S2a"):
        OL_d = xio("OL", [NH, 128, T], F32, has("S1"))
        QC_d = xio("QC", [NH, 128, T], BF16, has("S1"))
        G_d = xio("G", [NH, 128, T], BF16, has("S1"))
    if has("S1"):
        SF_d = xio("SF", [NH, 128, 128], F32, True)
    if has("S2a") and not fused:
        SFp_d = xio("SFprev", [NH, 128, 128], F32, False)

    def hgrn1():
        rmsnorm("mix0")
        cnt["single"] = True
        cnt["group"] = 2
        lbv, lbb = Y.view(0, 16, F32)
        oml, omlb = Y.view(64, 16, F32)
        noml, nomlb = Y.view(128, 16, F32)
        dec = [Y.view(256 + i * 256, NCHUNK, F32) for i in range(2)]
        attm = [Y.view(768 + i * 256, 128, BF16) for i in range(2)]
        S32 = [Y.view(1280, 128, F32), Y.view(13568, 128, F32)]
        m01, m01b = Y.view(1792, 128, F32)
        cmask, cmb = Y.view(2304, T, F32)
        onesf, onfb = Y.view(6400, T, F32)
        S16 = [Y.view(10496 + i * 256, 128, BF16) for i in range(12)]
        constb = lbb
        P.op("dve", lambda e: e.tensor_tensor(out=lbv, in0=vecs[:, VEC_COLS["lb0"][0]:VEC_COLS["lb0"][0] + 16],
                                              in1=vecs[:, VEC_COLS["lb1"][0]:VEC_COLS["lb1"][0] + 16], op=ALU.subtract), reads=[vecbuf], writes=constb)
        P.op("act", lambda e: e.activation(out=lbv, in_=lbv, func=AF.Sigmoid), reads=constb, writes=constb)
        P.op("dve", lambda e: e.tensor_scalar(out=oml, in0=lbv, scalar1=-1.0, scalar2=1.0, op0=ALU.mult, op1=ALU.add), reads=constb, writes=constb)
        P.op("dve", lambda e: e.tensor_scalar(out=noml, in0=lbv, scalar1=-1.0, scalar2=None, op0=ALU.add), reads=constb, writes=constb)
        P.op("pool", lambda e: e.memset(onesf, 1.0), writes=onfb)
        P.op("pool", lambda e: e.memset(cmask, 1.0), writes=cmb)
        cm3 = cmask.rearrange("p (c i) -> p c i", i=CH)
        P.op("pool", lambda e: e.memset(cm3[:, :, 0:1], 0.0), writes=cmb)
        P.op("pool", lambda e: e.memset(m01, 1.0), writes=m01b)
        P.op("pool", lambda e: e.affine_select(out=m01, in_=m01, pattern=[[1, 128]], compare_op=ALU.is_ge, fill=0.0, base=0, channel_multiplier=-1),
             reads=m01b, writes=m01b)
        for g in range(4):
            P.op("pool", lambda e, g=g: e.affine_select(out=m01[32 * g:32 * g + 32, :], in_=m01[32 * g:32 * g + 32, :], pattern=[[-1, 128]],
                                                      compare_op=ALU.is_ge, fill=0.0, base=32 * (g + 1) - 1, channel_multiplier=0),
                 reads=m01b, writes=m01b)
        qs, qsb = X.view(0, T, F32)
        sg, sgb = X.view(4096, T, F32)
        lf, lfb = X.view(8192, T, F32)
        bb, bbb = X.view(12288, T, F32)
        Bc, Bcb = X.view(16384, T, F32)
        e1, e1b = X.view(20480, T, F32)
        bb3 = bb.rearrange("p (c i) -> p c i", i=CH)
        e13 = e1.rearrange("p (c i) -> p c i", i=CH)

        def hb(s, i):
            return A.view(s * 10240 + i * 2048, T, BF16)
        kec = [A.view(20480 + i * 2048, T, BF16, parts=32) for i in range(2)]
        vcl = [A.view(24576 + i * 2048, T, BF16, parts=32) for i in range(2)]
        o_out, oob = A.view(28672, T, F32)
        rb16 = rstd[:, :].bitcast(BF16)
        gs, gsb = rb16[:, 0:T], [rstdbuf]
        qcm, qcb = rb16[:, T:2 * T], [rstdbuf]
        scale = HD ** -0.5

        def stageA(h):
            s = h % 2
            q_in, qib = hb(s, 0); k_in, kib = hb(s, 1); vtm, vtb = hb(s, 2); keT, keTb = hb(s, 3); vT, vTb = hb(s, 4)
            dc, dcb = dec[s]

            def epi_act(dst, dstb, func):
                def f(hf, p, pb):
                    P.op("act", lambda e: e.activation(out=dst[:, hf * 512:(hf + 1) * 512], in_=p[:, :], func=func), reads=[pb], writes=dstb)
                return f
            def gate1():
                P.op("act", lambda e: e.activation(out=lf, in_=sg, func=AF.Ln, scale=oml[:, h:h + 1], bias=lbv[:, h:h + 1]), reads=sgb + constb, writes=lfb)
                yield
                P.op("dve", lambda e: e.tensor_scalar(out=sg, in0=sg, scalar1=noml[:, h:h + 1], scalar2=oml[:, h:h + 1], op0=ALU.mult, op1=ALU.add),
                     reads=sgb + constb, writes=sgb)
                yield
                P.op("dve", lambda e: e.tensor_tensor_scan(out=bb, data0=cmask, data1=lf, initial=0.0, op0=ALU.mult, op1=ALU.add), reads=lfb + cmb, writes=bbb)
                yield
                P.op("dve", lambda e: e.tensor_tensor_scan(out=Bc, data0=onesf, data1=lf, initial=0.0, op0=ALU.mult, op1=ALU.add), reads=lfb + onfb, writes=Bcb)
                yield
                P.op("act", lambda e: e.activation(out=lf, in_=bb, func=AF.Exp, scale=-1.0), reads=bbb, writes=lfb)
                yield
                P.op("dve", lambda e: e.tensor_tensor(out=k_in, in0=sg, in1=lf, op=ALU.mult), reads=sgb + lfb, writes=kib)
                yield
                P.op("act", lambda e: e.activation(out=dc, in_=bb3[:, :, CH - 1], func=AF.Exp), reads=bbb, writes=dcb)
                yield
                P.op("dve", lambda e: e.tensor_tensor(out=e13, in0=bb3[:, :, CH - 1:CH].broadcast_to([128, NCHUNK, CH]), in1=bb3, op=ALU.subtract),
                     reads=bbb, writes=e1b)
                yield
                P.op("act", lambda e: e.activation(out=e1, in_=e1, func=AF.Exp), reads=e1b, writes=e1b)
                yield
                P.op("dve", lambda e: e.tensor_tensor(out=keT, in0=sg, in1=e1, op=ALU.mult), reads=sgb + e1b, writes=keTb)
                yield
                P.op("act", lambda e: e.activation(out=e1, in_=bb, func=AF.Exp), reads=bbb, writes=e1b)
                yield
                P.op("act", lambda e: e.activation(out=lf, in_=Bc, func=AF.Exp), reads=Bcb, writes=lfb)
                yield

            def mix(ga, gb, ratio=2):
                alive_a = alive_b = True
                while alive_a or alive_b:
                    for _ in range(ratio):
                        if alive_a:
                            try:
                                next(ga)
                            except StopIteration:
                                alive_a = False
                        yield
                    if alive_b:
                        try:
                            next(gb)
                        except StopIteration:
                            alive_b = False

            def three_tiles():
                yield from dense_gen(rhs_u, 16, epi_act(qs, qsb, AF.Silu)); yield
                yield from dense_gen(rhs_u, 16, epi_act(vT, vTb, AF.Copy)); yield
                yield from dense_gen(rhs_u, 16, epi_act(gs, gsb, AF.Silu)); yield
            yield from dense_gen(rhs_u, 16, epi_act(sg, sgb, AF.Sigmoid)); yield
            yield from mix(three_tiles(), gate1(), ratio=2)
            P.dma("sp", lambda e: e.dma_start(out=G_d[h], in_=gs), chan="out_g", reads=gsb, writes=[dbuf(f"G{h}")])
            P.op("dve", lambda e: e.scalar_tensor_tensor(out=q_in, in0=qs, scalar=scale, in1=e1, op0=ALU.mult, op1=ALU.mult), reads=qsb + e1b, writes=qib)
            yield
            P.op("dve", lambda e: e.scalar_tensor_tensor(out=qcm, in0=qs, scalar=scale, in1=lf, op0=ALU.mult, op1=ALU.mult), reads=qsb + lfb, writes=qcb)
            yield
            P.dma("sp", lambda e: e.dma_start(out=QC_d[h], in_=qcm), chan="out_qc", reads=qcb, writes=[dbuf(f"QC{h}")])
            yield
            for (src, srcb, dst, dstb, bank) in ((vT, vTb, vtm, vtb, 4),):
                ptv = ps[bank][:, :].bitcast(BF16)
                for j in range(8):
                    P.op("pe", lambda e, j=j, src=src, ptv=ptv: e.transpose(out=ptv[:, j * 128:(j + 1) * 128], in_=src[:, j * 128:(j + 1) * 128], identity=ident_b[:, :]),
                         reads=srcb + [identbuf], writes=[psbuf[bank]])
                P.op("act", lambda e, dst=dst, ptv=ptv: e.activation(out=dst, in_=ptv, func=AF.Copy), reads=[psbuf[bank]], writes=dstb)
            yield

        def rec(h):
            s = h % 2
            q_in, qib = hb(s, 0); k_in, kib = hb(s, 1); vtm, vtb = hb(s, 2); keT, keTb = hb(s, 3); vT, vTb = hb(s, 4)
            dc, dcb = dec[s]

            def prep_batch(bt):
                for (src, srcb, (dst, dstb), bank) in ((keT, keTb, kec[bt % 2], 4), (vT, vTb, vcl[bt % 2], 4)):
                    ptv = ps[bank][:, :].bitcast(BF16)
                    for ci in range(8):
                        c = 8 * bt + ci
                        P.op("pe", lambda e: e.transpose(out=ptv[0:32, ci * 128:(ci + 1) * 128], in_=src[:, c * CH:(c + 1) * CH], identity=ident_b[:, :]),
                             reads=srcb + [identbuf], writes=[psbuf[bank]])
                    P.op("act", lambda e: e.activation(out=dst, in_=ptv[0:32, :], func=AF.Copy), reads=[psbuf[bank]], writes=dstb)
                    yield

            def kv_tile(jt):
                for cc in range(4):
                    c = 4 * jt + cc
                    bt, ci = c // 8, c % 8
                    kc, kcb = kec[bt % 2]; vc_, vcb = vcl[bt % 2]
                    kvbank = (2, 3, 7)[c % 3]
                    kvp = ps[kvbank][:, 0:128]; kvb = psbuf[kvbank]
                    so, sob = S32[c % 2]; sn32, sn32b = S32[(c + 1) % 2]
                    P.op("pe", lambda e: e.matmul(kvp, lhsT=kc[:, ci * 128:(ci + 1) * 128], rhs=vc_[:, ci * 128:(ci + 1) * 128], start=True, stop=True),
                         reads=kcb + vcb, writes=[kvb])
                    if c == 0:
                        P.op("dve", lambda e: e.tensor_copy(out=sn32, in_=kvp), reads=[kvb], writes=sn32b)
                    else:
                        P.op("dve", lambda e: e.scalar_tensor_tensor(out=sn32, in0=so, scalar=dc[:, c:c + 1], in1=kvp, op0=ALU.mult, op1=ALU.add),
                             reads=[kvb] + sob + dcb, writes=sn32b)
                    if c < NCHUNK - 1:
                        sn, snb = S16[(c + 1) % 12]
                        P.op("act", lambda e: e.activation(out=sn, in_=sn32, func=AF.Copy), reads=sn32b, writes=snb)
                    else:
                        P.dma("sp", lambda e: e.dma_start(out=SF_d[h], in_=sn32), chan="out_sf", reads=sn32b, writes=[dbuf(f"SF{h}")])
                    yield
            yield from prep_batch(0)
            yield from kv_tile(0)
            for jt in range(8):
                tsl = slice(jt * 128, (jt + 1) * 128)
                r = jt % 2
                ap_att = ps[5][:, r * 128:(r + 1) * 128]; ab = psbuf[5]
                ap_o = ps[6][:, r * 128:(r + 1) * 128]; ob = psbuf[6]
                am, amb = attm[r]
                P.op("pe", lambda e: e.matmul(ap_att, lhsT=k_in[:, tsl], rhs=q_in[:, tsl], start=True, stop=True), reads=kib + qib, writes=[ab])
                yield
                if jt % 2 == 0 and jt // 2 + 1 < 4:
                    yield from prep_batch(jt // 2 + 1)
                if jt < 7:
                    yield from kv_tile(jt + 1)
                P.op("dve", lambda e: e.tensor_tensor(out=am, in0=ap_att, in1=m01, op=ALU.mult), reads=[ab] + m01b, writes=amb)
                P.op("pe", lambda e: e.matmul(ap_o, lhsT=vtm[:, tsl], rhs=am, start=True, stop=(jt == 0), skip_group_check=True),
                     reads=vtb + amb, writes=[ob])
                for cc in range(4):
                    c = 4 * jt + cc
                    if c == 0:
                        continue
                    sc, scb = S16[c % 12]
                    last = cc == 3
                    P.op("pe", lambda e: e.matmul(ap_o[:, 32 * cc:32 * cc + 32], lhsT=sc, rhs=q_in[:, c * CH:(c + 1) * CH],
                                                  start=False, stop=last, skip_group_check=True),
                         reads=scb + qib, writes=[ob])
                P.op("act", lambda e: e.activation(out=o_out[:, tsl], in_=ap_o, func=AF.Copy), reads=[ob], writes=oob)
                yield
            P.dma("sp", lambda e: e.dma_start(out=OL_d[h], in_=o_out), chan="out_ol", reads=oob, writes=[dbuf(f"OL{h}")])

        H1LIM = getattr(build, "H1LIM", None)
        if H1LIM is not None:
            g_ = stageA(0)
            for _ in range(3):
                next(g_)
            if H1LIM >= 1:
                next(g_)
                dbgout("q_in", hb(0, 0)[0], hb(0, 0)[1], [128, T], BF16)
                dbgout("k_in", hb(0, 1)[0], hb(0, 1)[1], [128, T], BF16)
                dbgout("keT", hb(0, 3)[0], hb(0, 3)[1], [128, T], BF16)
                dbgout("vtm", hb(0, 2)[0], hb(0, 2)[1], [128, T], BF16)
                dbgout("bb", bb, bbb, [128, T], F32)
                dbgout("Bc", Bc, Bcb, [128, T], F32)
                dbgout("dec", dec[0][0], dec[0][1], [128, NCHUNK], F32)
                dbgout("m01", m01, m01b, [128, 128], F32)
            if H1LIM >= 2:
                r_ = rec(0)
                for _ in range(H1LIM - 1):
                    next(r_)
                dbgout("o_out", o_out, oob, [128, T], F32)
            return
        prev = None
        for h in range(NH + 1):
            gens = []
            if h < NH:
                gens.append(stageA(h))
            if prev is not None:
                gens.append(rec(prev))
            while gens:
                for g_ in list(gens):
                    try:
                        next(g_)
                    except StopIteration:
                        gens.remove(g_)
            prev = h if h < NH else None
        cnt["single"] = False
        cnt["group"] = 4
        if fused:
            fz["SFp"] = allgather("SF", SF_d.rearrange("h p e -> (h p) e"), NH * 128, 128, F32,
                                  [dbuf(f"SF{h}") for h in range(NH)]).rearrange("(h p) e -> h p e", h=NH)

    def hgrn2():
        og = [A.view(h * 2048, T, BF16) for h in range(NH)]
        o32 = [X.view(0, T, F32), X.view(4096, T, F32), X.view(18432, T, F32)]
        qc = [X.view(8192 + i * 2048, T, BF16) for i in range(2)]
        gsl = [X.view(12288, T, BF16), X.view(14336, T, BF16), X.view(22528, T, BF16)]
        s32 = [X.view(16384 + i * 512, 128, F32) for i in range(2)]
        s16 = [X.view(17408 + i * 256, 128, BF16) for i in range(2)]
        def load(h):
            s = h % 2
            o_ap, o_b = o32[h % 3]; q_ap, q_b = qc[s]; g_ap, g_b = gsl[h % 3]; sa, sab = s32[s]; sh, shb = s16[s]
            SFp = fz["SFp"] if fused else SFp_d
            P.dma("sp", lambda e: e.dma_start(out=o_ap, in_=OL_d[h]), chan=f"ld_o{h % 3}", reads=[dbuf(f"OL{h}")], writes=o_b)
            P.dma("sp", lambda e: e.dma_start(out=q_ap, in_=QC_d[h]), chan=f"ld_q{s}", reads=[dbuf(f"QC{h}")], writes=q_b)
            P.dma("sp", lambda e: e.dma_start(out=g_ap, in_=G_d[h]), chan=f"ld_g{h % 3}", reads=[dbuf(f"G{h}")], writes=g_b)
            P.dma("sp", lambda e: e.dma_start(out=sa, in_=SFp[h]), chan=f"ld_s{s}", reads=[dbuf("SF_all")], writes=sab)
            P.op("dve", lambda e: e.tensor_scalar(out=sh, in0=sa, scalar1=vcol("flag"), scalar2=None, op0=ALU.mult), reads=sab + [vecbuf], writes=shb)

        unit_state = {}

        def front(h, hf):
            s = h % 2
            o_ap, o_b = o32[h % 3]; q_ap, q_b = qc[s]; sh, shb = s16[s]
            sl = slice(hf * 512, (hf + 1) * 512)
            pc = ps[4 + hf]; pcb = psbuf[4 + hf]
            pm = ps[6 + hf]; pmb = psbuf[6 + hf]
            P.op("pe", lambda e: e.matmul(pc[:, :], lhsT=sh, rhs=q_ap[:, sl], start=True, stop=True), reads=shb + q_b, writes=[pcb])
            P.op("dve", lambda e: e.tensor_tensor(out=o_ap[:, sl], in0=o_ap[:, sl], in1=pc[:, :], op=ALU.add), reads=[pcb] + o_b, writes=o_b)
            i = cnt["sq"] % 3; cnt["sq"] += 1
            P.op("dve", lambda e: e.tensor_tensor(out=sq[i][:, :], in0=o_ap[:, sl], in1=o_ap[:, sl], op=ALU.mult), reads=o_b, writes=[sqbuf[i]])
            P.op("pe", lambda e: e.matmul(pm[:, :], lhsT=ones_b[:, :], rhs=sq[i][:, :], start=True, stop=True), reads=[sqbuf[i], onesbuf], writes=[pmb])

        def back_act(h, hf):
            pm = ps[6 + hf]; pmb = psbuf[6 + hf]
            t, tb = newtmp()
            unit_state[(h, hf)] = (t, tb)
            P.op("act", lambda e: e.activation(out=t[:, :], in_=pm[:, :], func=AF.Ln, scale=1.0 / HD, bias=epsc[:, 0:1]), reads=[pmb, epsb], writes=[tb])
            P.op("act", lambda e: e.activation(out=t[:, :], in_=t[:, :], func=AF.Exp, scale=-0.5), reads=[tb], writes=[tb])

        def back_dve(h, hf):
            o_ap, o_b = o32[h % 3]; g_ap, g_b = gsl[h % 3]
            sl = slice(hf * 512, (hf + 1) * 512)
            t, tb = unit_state[(h, hf)]
            P.op("dve", lambda e: e.scalar_tensor_tensor(out=t[:, :], in0=o_ap[:, sl], scalar=vcol("hg"), in1=t[:, :], op0=ALU.mult, op1=ALU.mult),
                 reads=o_b + [tb, vecbuf], writes=[tb])
            P.op("dve", lambda e: e.tensor_tensor(out=og[h][0][:, sl], in0=t[:, :], in1=g_ap[:, sl], op=ALU.mult), reads=[tb] + g_b, writes=og[h][1])
        units = [(h, hf) for h in range(NH) for hf in range(2)]
        load(0)
        load(1)
        for k, (h, hf) in enumerate(units):
            if k > 1:
                back_act(*units[k - 2])
            front(h, hf)
            if k > 1:
                back_dve(*units[k - 2])
            if hf == 1 and h + 2 < NH:
                load(h + 2)
        for u in units[-2:]:
            back_act(*u)
            back_dve(*u)
        cnt["single"] = False

        def rhs_og(k):
            return og[k][0], og[k][1]
        for o in range(16):
            dense(rhs_og, 16, epi_resid(o, final=True))

    for seg in segs:
        if seg == "S2b":
            rmsnorm("mlp0", stats="piped" if has("S2a") else "compute"); mlp(); rmsnorm("ple0", stats="piped"); ple(0)
        elif seg == "S3b":
            rmsnorm("mlp1", stats="piped" if has("S3a") else "compute"); mlp(); rmsnorm("ple1", stats="piped"); ple(1)
            rmsnorm("fin", out_f32_inplace=True, stats="piped")
        elif seg == "S2c":
            kv_phase()
        elif seg == "S3a":
            attn_phase()
        elif seg == "S1":
            hgrn1()
        elif seg == "S2a":
            hgrn2()

    for c in range(NCH):
        P.dma(("sp", "act", "pool")[c % 3], lambda e, c=c: e.dma_start(out=hout_d[c * 128:(c + 1) * 128, :], in_=hT[:, c, :]), chan="hout", reads=[hbuf[c]])
    finals = ["hout"] + [c for c in P.chan_count if c.startswith("out_")]
    assert widx["i"] == len(b.plan) or getattr(build, "H1LIM", None) is not None, (widx["i"], len(b.plan))
    P.emit_all(nc, finals)
    b.es.close()
    return b


NCORES = 8
_PROG_CACHE = {}


def _get_prog(segs, fused=False):
    key = (tuple(segs), fused)
    if key not in _PROG_CACHE:
        _PROG_CACHE[key] = build(list(segs), fused=fused)
    return _PROG_CACHE[key]


def _run(segs, inp, per_core, fused=False):
    b = _get_prog(segs, fused)
    slabs = [pack_slab(inp, s) for s in b.plan]
    walls = {f"wall{i}": np.stack(slabs[i * b.wch:(i + 1) * b.wch]) for i in range(len(b.walls))}
    wple = np.ascontiguousarray(np.stack([np.asarray(inp["w_ple_up"][l], np.float32).reshape(2, 128, 2048).transpose(1, 0, 2).reshape(128, 4096)
                                          for l in range(2)]))
    maps = []
    for r in range(NCORES):
        m = {"wple": wple, "vecs": pack_vecs(inp, r)}
        m.update(walls)
        m.update(per_core[r])
        maps.append(m)
    res = run_bass_kernel_spmd(b.nc, maps, core_ids=list(range(NCORES)))
    return res.results


ALL_SEGS = ["S1", "S2a", "S2b", "S2c", "S3a", "S3b"]


def kernel(**inputs):
    inp = {k: np.asarray(v) for k, v in inputs.items()}
    x = np.asarray(inp["x"], np.float32)
    p = np.asarray(inp["p"], np.float32)
    base = []
    for r in range(NCORES):
        bb, sl = r // 2, slice((r % 2) * T, (r % 2 + 1) * T)
        base.append({"hin": np.ascontiguousarray(x[bb, sl].T), "pT": np.ascontiguousarray(p[:, bb, sl].transpose(0, 2, 1))})
    o = _run(ALL_SEGS, inp, base, fused=True)
    out = np.empty((4, SEQ, D), np.float32)
    for r in range(NCORES):
        bb, sl = r // 2, slice((r % 2) * T, (r % 2 + 1) * T)
        out[bb, sl] = np.asarray(o[r]["hout"]).T
    return out
```

```python
import numpy as np
from contextlib import ExitStack
import concourse.bass as bass
import concourse.mybir as mybir
from concourse.bass_utils import run_bass_kernel_spmd

F32 = mybir.dt.float32
BF16 = mybir.dt.bfloat16
AF = mybir.ActivationFunctionType
ALU = mybir.AluOpType

D = 2048
NCH = 16
T = 1024
SEQ = 2048
NH = 16
HD = 128
PLE = 256
DFF = 8192
EPS = 1e-6
CH = 32
NCHUNK = T // CH
NSLOT = 6
NEG = -30000.0


class Buf:
    __slots__ = ("name", "w", "rs", "excl")

    def __init__(self, name, excl=False):
        self.name = name
        self.w = None
        self.rs = {}
        self.excl = excl


class Op:
    __slots__ = ("eng", "emit", "deps", "sig", "val", "chan", "idx", "inc")

    def __init__(self, eng, emit, chan=None):
        self.eng = eng
        self.emit = emit
        self.deps = []
        self.sig = False
        self.val = None
        self.chan = chan
        self.idx = None
        self.inc = 16


class _Rec:
    def __init__(self):
        self.calls = []

    def __getattr__(self, name):
        def f(*a, **k):
            self.calls.append((name, a, k))
            return self
        return f


def _freeze(emit):
    r = _Rec()
    emit(r)
    assert len(r.calls) == 1, r.calls
    name, a, k = r.calls[0]
    return lambda eng: getattr(eng, name)(*a, **k)


class Prog:
    ENGS = ("pe", "act", "dve", "pool", "sp")

    def __init__(self, same_engine_sync=True):
        self.ops = {e: [] for e in self.ENGS}
        self.chan_count = {}
        self.same_engine_sync = same_engine_sync
        self.nops = 0

    @staticmethod
    def _flat(x):
        out = []
        for b in x:
            if isinstance(b, (list, tuple)):
                out.extend(Prog._flat(b))
            else:
                out.append(b)
        return out

    def _track(self, op, reads, writes):
        reads = self._flat(reads)
        writes = self._flat(writes)
        writes = writes + [b for b in reads if b.excl]
        reads = [b for b in reads if not b.excl]
        deps = []
        for b in reads:
            if b.w is not None:
                deps.append(b.w)
        for b in writes:
            if b.w is not None:
                deps.append(b.w)
            deps.extend(b.rs.values())
        for b in writes:
            b.w = op
            b.rs = {}
        for b in reads:
            key = op.chan if op.chan is not None else op.eng
            b.rs[key] = op
        seen = set()
        for d in deps:
            if d is op or id(d) in seen:
                continue
            seen.add(id(d))
            if d.chan is None and d.eng == op.eng:
                if op.eng == "pe" or not self.same_engine_sync:
                    continue
                if op.chan is not None:
                    pass
            d.sig = True
            op.deps.append(d)

    def op(self, eng, emit, reads=(), writes=()):
        o = Op(eng, _freeze(emit))
        o.idx = self.nops
        self.nops += 1
        self._track(o, reads, writes)
        self.ops[eng].append(o)
        return o

    def dma(self, queue, emit, chan, reads=(), writes=()):
        o = Op(queue, _freeze(emit), chan=chan)
        o.idx = self.nops
        self.nops += 1
        self._track(o, reads, writes)
        c = self.chan_count.get(chan, 0) + 16
        self.chan_count[chan] = c
        o.val = c
        o.sig = True
        self.ops[queue].append(o)
        return o

    def coll(self, emit, chan, reads=(), writes=()):
        o = Op("pool", _freeze(emit), chan=chan)
        o.idx = self.nops
        self.nops += 1
        self._track(o, reads, writes)
        c = self.chan_count.get(chan, 0) + 1
        self.chan_count[chan] = c
        o.val = c
        o.sig = True
        o.inc = 1
        self.ops["pool"].append(o)
        return o

    def emit_all(self, nc, final_chans):
        with ExitStack() as es:
            esem = {e: es.enter_context(nc.semaphore("s_" + e)) for e in self.ENGS}
            csem = {c: es.enter_context(nc.semaphore("c_" + c)) for c in self.chan_count}
            for e in self.ENGS:
                cnt = 0
                for o in self.ops[e]:
                    if o.chan is None and o.sig:
                        cnt += 1
                        o.val = cnt

            def run(eng_name, eng):
                seen = {}
                for o in self.ops[eng_name]:
                    for d in o.deps:
                        if d.chan is not None:
                            key, sem = ("c", d.chan), csem[d.chan]
                        else:
                            key, sem = ("e", d.eng), esem[d.eng]
                        if seen.get(key, 0) >= d.val:
                            continue
                        seen[key] = d.val
                        eng.wait_ge(sem, d.val)
                    ins = o.emit(eng)
                    if o.chan is not None:
                        ins.then_inc(csem[o.chan], o.inc)
                    elif o.sig:
                        ins.then_inc(esem[eng_name], 1)
                if eng_name == "sp":
                    for c in final_chans:
                        eng.wait_ge(csem[c], self.chan_count[c])

            with nc.Block() as block:
                @block.tensor
                def _(e):
                    run("pe", e)

                @block.scalar
                def _(e):
                    run("act", e)

                @block.vector
                def _(e):
                    run("dve", e)

                @block.gpsimd
                def _(e):
                    run("pool", e)

                @block.sync
                def _(e):
                    run("sp", e)


VEC_COLS = {}
_c = 0
for _name, _n in [("mix0", 16), ("mix1", 16), ("mlp0", 16), ("mlp1", 16), ("ple0", 16), ("ple1", 16),
                  ("kvn", 16), ("fin", 16), ("lb0", 16), ("lb1", 16), ("hg", 1), ("bf", 1),
                  ("flag", 1), ("prevbias", 1)]:
    VEC_COLS[_name] = (_c, _n)
    _c += _n
NVEC = _c


def pack_vecs(inp, rank):
    v = np.zeros((128, NVEC), np.float32)

    def put(name, arr):
        c0, n = VEC_COLS[name]
        v[:, c0:c0 + n] = np.asarray(arr, np.float32).reshape(n, 128).T

    put("mix0", inp["mix_norm"][0]); put("mix1", inp["mix_norm"][1])
    put("mlp0", inp["mlp_norm"][0]); put("mlp1", inp["mlp_norm"][1])
    put("ple0", inp["ple_norm"][0]); put("ple1", inp["ple_norm"][1])
    put("kvn", inp["kv_norm"]); put("fin", inp["final_norm"])
    put("lb0", inp["a_lb_logits"][0]); put("lb1", inp["a_lb_logits"][1])
    put("hg", inp["a_head_gain"][0])
    c0, _ = VEC_COLS["bf"]
    v[:16, c0] = np.asarray(inp["b_f"], np.float32)
    odd = rank % 2
    v[:, VEC_COLS["flag"][0]] = 1.0 if odd else 0.0
    v[:, VEC_COLS["prevbias"][0]] = 0.0 if odd else NEG
    return v


def pack_slab(inp, spec):
    kind = spec[0]
    if kind == "std":
        _, name, layer, r0, c0 = spec
        W = inp[name] if layer is None else inp[name][layer]
        blk = np.asarray(W[r0:r0 + 2048, c0:c0 + 128], np.float32)
        return blk.reshape(16, 128, 128).transpose(1, 0, 2).reshape(128, 2048)
    if kind == "ple":
        _, layer, o8 = spec
        W = np.asarray(inp["w_ple_up"][layer][:, o8 * 1024:(o8 + 1) * 1024], np.float32)
        blk = W.reshape(2, 128, 8, 128).transpose(1, 2, 0, 3)
        return blk.reshape(128, 2048)
    if kind == "flog":
        W = np.asarray(inp["w_kvf"][:, 4096:4112], np.float32)
        out = np.zeros((128, 2048), np.float32)
        out[:, :256] = W.reshape(16, 128, 16).transpose(1, 0, 2).reshape(128, 256)
        return out
    raise ValueError(kind)


class Builder:
    def __init__(self, segs, fused):
        self.segs = segs
        self.fused = fused
        self.plan = []
        self.nc = bass.Bass("TRN2", target_bir_lowering=False)
        self.P = Prog()
        self.es = ExitStack()
        self.final_chans = []
        self._n = 0

    def sb(self, name, shape, dt):
        return self.es.enter_context(self.nc.sbuf_tensor(name, shape, dt))

    def dram(self, name, shape, dt, kind):
        if kind == "Internal":
            return self.nc.dram_tensor(name, shape, dt).ap()
        return self.nc.dram_tensor(name, shape, dt, kind=kind).ap()

    def uniq(self, s):
        self._n += 1
        return f"{s}{self._n}"

    def wplan(self, specs):
        i0 = len(self.plan)
        self.plan.extend(specs)
        return i0

    def wacquire(self, idx):
        hi = min(len(self.plan), idx + NSLOT)
        while self.w_issued < hi:
            j = self.w_issued
            s = j % NSLOT
            src = self.walls[j // self.wch][j % self.wch]
            dst = self.wslot[s]
            self.P.dma("pool", lambda e, dst=dst, src=src: e.dma_start(out=dst[:, :], in_=src),
                       chan=f"w{s}", reads=(self.w_first_reads if 0 < j < NSLOT else ()), writes=[self.wbuf[s]])
            self.w_issued += 1
        s = idx % NSLOT
        return self.wslot[s], self.wbuf[s]


def make_plan(segs):
    plan = []

    def mlp(l):
        for g in range(4):
            for j in range(16):
                plan.append(("std", "w_mlp_up", l, 0, (g * 16 + j) * 128))
            for o in range(16):
                plan.append(("std", "w_mlp_down", l, g * 2048, o * 128))

    def ple(l):
        for o in range(16):
            plan.append(("std", "w_ple_gate", l, 0, o * 128))

    for s in segs:
        if s == "S1":
            for h in range(NH):
                for j in (1, 0, 2, 3):
                    plan.append(("std", "w_a_in", 0, 0, j * 2048 + h * 128))
        elif s == "S2a":
            for o in range(16):
                plan.append(("std", "w_a_out", 0, 0, o * 128))
        elif s == "S2b":
            mlp(0); ple(0)
        elif s == "S2c":
            plan.append(("flog",))
            for h in range(NH):
                plan.append(("std", "w_kvf", None, 0, h * 128))
            for h in range(NH):
                plan.append(("std", "w_kvf", None, 0, 2048 + h * 128))
        elif s == "S3a":
            for h in range(NH):
                plan.append(("std", "w_b_q", 0, 0, h * 128))
            for o in range(16):
                plan.append(("std", "w_b_out", 0, 0, o * 128))
        elif s == "S3b":
            mlp(1); ple(1)
    return plan


class Arena:
    PAGE = 256

    def __init__(self, b, name, nbytes):
        self.t = b.sb(name, [128, nbytes // 4], F32)
        self.pages = [Buf(f"{name}_pg{i}") for i in range(nbytes // self.PAGE)]
        self.nbytes = nbytes

    def view(self, off, nelem, dt, parts=128):
        esz = 4 if dt == F32 else 2
        nb = nelem * esz
        assert off % 4 == 0 and off + nb <= self.nbytes, (off, nb, self.nbytes)
        ap = self.t[0:parts, off // 4:(off + nb + 3) // 4]
        if dt != F32:
            ap = ap.bitcast(dt)
        pgs = self.pages[off // self.PAGE:(off + nb - 1) // self.PAGE + 1]
        return ap, pgs


def build(segs, fused=False):
    b = Builder(segs, fused)
    nc, P = b.nc, b.P
    b.plan = make_plan(segs)
    nslab = max(1, len(b.plan))
    WCH = 224
    b.walls = [b.dram(f"wall{i}", [min(WCH, nslab - i * WCH), 128, 2048], F32, "ExternalInput") for i in range((nslab + WCH - 1) // WCH)]
    b.wch = WCH
    vecs_d = b.dram("vecs", [128, NVEC], F32, "ExternalInput")
    hin_d = b.dram("hin", [D, T], F32, "ExternalInput")
    hout_d = b.dram("hout", [D, T], F32, "ExternalOutput")
    pT_d = b.dram("pT", [2, PLE, T], F32, "ExternalInput")
    wple_d = b.dram("wple", [2, 128, 2 * D], F32, "ExternalInput")
    io = {}

    def xio(name, shape, dt, produced):
        kind = "Internal" if fused else ("ExternalOutput" if produced else "ExternalInput")
        io[name] = b.dram(name, shape, dt, kind)
        return io[name]

    PAIRS = [[2 * i, 2 * i + 1] for i in range(4)]
    dbufs = {}

    def dbuf(name):
        if name not in dbufs:
            dbufs[name] = Buf("dram_" + name)
        return dbufs[name]

    def allgather(name, src2d, rows, cols, dt, src_bufs):
        dst = b.dram(name + "_all", [2 * rows, cols], dt, "Internal")
        P.coll(lambda e: e.collective_compute("AllGather", ALU.bypass, replica_groups=PAIRS, ins=[src2d.opt()], outs=[dst.opt()]),
               chan="cc_" + name, reads=src_bufs, writes=[dbuf(name + "_all")])
        return dst[0:rows, :]

    has = lambda s: s in segs
    DBG = getattr(build, "DBG", False)

    def dbgout(name, ap, reads, shape, dt):
        if not DBG:
            return
        dd = b.dram("dbg_" + name, shape, dt, "ExternalOutput")
        P.dma("sp", lambda e: e.dma_start(out=dd, in_=ap), chan="out_dbg", reads=reads)
    if has("S2c") or has("S3a"):
        Kown_d = xio("Kown", [NH, 128, T], BF16, has("S2c"))
        Vown_d = xio("Vown", [NH, 128, 8 * 128], BF16, has("S2c"))
        Down_d = xio("Dloc", [NH, T], F32, has("S2c"))
    if has("S3a") and not fused:
        Kprev_d = xio("Kprev", [NH, 128, T], BF16, False)
        Vprev_d = xio("Vprev", [NH, 128, 8 * 128], BF16, False)
        Dprev_d = xio("Dprev", [NH, T], F32, False)
    fz = {}

    hT = b.sb("hT", [128, NCH, T], F32)
    hbuf = [Buf(f"h{c}") for c in range(NCH)]
    b.w_first_reads = [hbuf[NCH - 2], hbuf[NCH - 1]]
    uT = b.sb("uT", [128, NCH, T], BF16)
    ubuf = [Buf(f"u{c}") for c in range(NCH)]
    A = Arena(b, "A", 32 * 1024)
    X = Arena(b, "X", 24 * 1024)
    b.wslot = [b.sb(f"w{s}", [128, 2048], BF16) for s in range(NSLOT)]
    b.wbuf = [Buf(f"w{s}") for s in range(NSLOT)]
    b.w_issued = 0
    vecs = b.sb("vecs_sb", [128, NVEC], F32); vecbuf = Buf("vecs")
    ones_b = b.sb("ones_b", [128, 128], BF16); onesbuf = Buf("ones")
    ident_b = b.sb("ident_b", [128, 128], BF16); identbuf = Buf("ident")
    ident_f = b.sb("ident_f", [128, 128], F32); identfbuf = Buf("identf")
    rstd = b.sb("rstd", [128, T], F32); rstdbuf = Buf("rstd")
    epsc = b.sb("epsc", [128, 2], F32); epsb = Buf("epsc")
    sq = [b.sb(f"sq{i}", [128, 512], BF16) for i in range(3)]
    sqbuf = [Buf(f"sq{i}") for i in range(3)]
    tmp = [b.sb(f"tmp{i}", [128, 512], F32) for i in range(4)]
    tmpbuf = [Buf(f"tmp{i}") for i in range(4)]
    ps = [b.es.enter_context(nc.psum_tensor(f"ps{i}", [128, 512], F32)) for i in range(8)]
    psbuf = [[Buf(f"ps{i}", excl=True)] for i in range(8)]
    Y = Arena(b, "Y", 14080)
    cnt = {"tmp": 0, "sq": 0, "par": 0}

    def vcol(name, i=0):
        c0, _ = VEC_COLS[name]
        return vecs[:, c0 + i:c0 + i + 1]

    P.dma("sp", lambda e: e.dma_start(out=vecs[:, :], in_=vecs_d), chan="misc", writes=[vecbuf])
    P.op("pool", lambda e: e.memset(ones_b[:, :], 1.0), writes=[onesbuf])
    P.op("pool", lambda e: e.memset(epsc[:, 0:1], EPS), writes=[epsb])
    P.op("pool", lambda e: e.memset(epsc[:, 1:2], 1.0), writes=[epsb])
    zf = tmp[0]
    P.op("pool", lambda e: e.memset(zf[:, 0:128], 0.0), writes=[tmpbuf[0]])
    P.op("pool", lambda e: e.affine_select(out=ident_f[:, :], in_=zf[:, 0:128], pattern=[[-1, 128]],
                                           compare_op=ALU.not_equal, fill=1.0, base=0, channel_multiplier=1),
         reads=[tmpbuf[0]], writes=[identfbuf])
    P.op("pool", lambda e: e.tensor_copy(out=ident_b[:, :], in_=ident_f[:, :]), reads=[identfbuf], writes=[identbuf])
    for c in range(NCH):
        P.dma(("sp", "act")[c % 2], lambda e, c=c: e.dma_start(out=hT[:, c, :], in_=hin_d[c * 128:(c + 1) * 128, :]),
              chan=f"hin{c}", writes=[hbuf[c]])

    pending = []

    def flush_stats():
        while pending:
            stats_chunk(pending.pop(0))

    def stats_chunk(c, sb=6):
        for hf in range(2):
            i = cnt["sq"] % 3; cnt["sq"] += 1
            P.op("act", lambda e: e.activation(out=sq[i][:, :], in_=hT[:, c, hf * 512:(hf + 1) * 512], func=AF.Square),
                 reads=[hbuf[c]], writes=[sqbuf[i]])
            P.op("pe", lambda e: e.matmul(ps[sb + hf][:, :], lhsT=ones_b[:, :], rhs=sq[i][:, :], start=(c == 0), stop=(c == NCH - 1)),
                 reads=[sqbuf[i], onesbuf], writes=[psbuf[sb + hf]])

    def rmsnorm(gname, out_f32_inplace=False, stats="compute"):
        sb = 4
        if stats == "compute":
            for c in range(NCH):
                stats_chunk(c, sb=4)
        elif stats == "piped":
            flush_stats()
            sb = 6
        if stats != "reuse":
            for hf in range(2):
                P.op("act", lambda e, hf=hf: e.activation(out=rstd[:, hf * 512:(hf + 1) * 512], in_=ps[sb + hf][:, :], func=AF.Ln, scale=1.0 / D, bias=epsc[:, 0:1]),
                     reads=[psbuf[sb + hf], epsb], writes=[rstdbuf])
            P.op("act", lambda e: e.activation(out=rstd[:, :], in_=rstd[:, :], func=AF.Exp, scale=-0.5), reads=[rstdbuf], writes=[rstdbuf])
        for c in range(NCH):
            if out_f32_inplace:
                P.op("dve", lambda e, c=c: e.scalar_tensor_tensor(out=hT[:, c, :], in0=hT[:, c, :], scalar=vcol(gname, c), in1=rstd[:, :],
                                                                  op0=ALU.mult, op1=ALU.mult),
                     reads=[hbuf[c], rstdbuf, vecbuf], writes=[hbuf[c]])
            else:
                P.op("dve", lambda e, c=c: e.scalar_tensor_tensor(out=uT[:, c, :], in0=hT[:, c, :], scalar=vcol(gname, c), in1=rstd[:, :],
                                                                  op0=ALU.mult, op1=ALU.mult),
                     reads=[hbuf[c], rstdbuf, vecbuf], writes=[ubuf[c]])

    widx = {"i": 0}

    def dense_gen(rhs, kch, epi, mcols=128, group=None):
        idx = widx["i"]; widx["i"] += 1
        slot, wb = b.wacquire(idx)
        w3 = slot[:, :].rearrange("p (k n) -> p k n", k=16) if mcols == 128 else slot[:, 0:16 * mcols].rearrange("p (k n) -> p k n", k=16)
        par = 0 if cnt.get("single") else cnt["par"] % 2
        cnt["par"] += 1
        n = 0
        group = group or cnt.get("group", 4)
        for k in range(kch):
            rap, rb = rhs(k)
            for hf in range(2):
                pi = 2 * par + hf
                P.op("pe", lambda e: e.matmul(ps[pi][0:mcols, :], lhsT=w3[:, k, :], rhs=rap[:, hf * 512:(hf + 1) * 512],
                                              start=(k == 0), stop=(k == kch - 1)),
                     reads=[wb] + rb, writes=[psbuf[pi]])
                n += 1
                if n % group == 0 and n < 2 * kch:
                    yield
        flush_stats()
        for hf in range(2):
            pi = 2 * par + hf
            epi(hf, ps[pi], psbuf[pi])

    def dense(rhs, kch, epi, mcols=128):
        for _ in dense_gen(rhs, kch, epi, mcols=mcols):
            pass

    def rhs_u(k):
        return uT[:, k, :], [ubuf[k]]

    def newtmp():
        i = cnt["tmp"] % 4; cnt["tmp"] += 1
        return tmp[i], tmpbuf[i]

    def epi_resid(o, final=False):
        def f(hf, p, pb):
            P.op("dve", lambda e: e.tensor_tensor(out=hT[:, o, hf * 512:(hf + 1) * 512], in0=hT[:, o, hf * 512:(hf + 1) * 512], in1=p[:, :], op=ALU.add),
                 reads=[pb, hbuf[o]], writes=[hbuf[o]])
            if final and hf == 1:
                pending.append(o)
        return f

    def mlp():
        hid = [A.view(j * 2048, T, BF16) for j in range(16)]

        def rhs_a(k):
            return hid[k][0], hid[k][1]
        for g in range(4):
            for j in range(16):
                def epi(hf, p, pb, j=j):
                    t, tb = newtmp()
                    P.op("act", lambda e: e.activation(out=t[:, :], in_=p[:, :], func=AF.Relu), reads=[pb], writes=[tb])
                    P.op("dve", lambda e: e.tensor_tensor(out=hid[j][0][:, hf * 512:(hf + 1) * 512], in0=t[:, :], in1=t[:, :], op=ALU.mult),
                         reads=[tb], writes=hid[j][1])
                dense(rhs_u, 16, epi)
            for o in range(16):
                dense(rhs_a, 16, epi_resid(o, final=(g == 3)))

    def ple(l):
        pT_sb, pTb = X.view(0, 2 * T, BF16)
        pT3 = pT_sb.rearrange("p (k t) -> p k t", k=2)
        wp_sb, wpb = X.view(4096, 2 * D, BF16)
        wp3 = wp_sb.rearrange("p (k n) -> p k n", k=2)
        P.dma("pool", lambda e: e.dma_start(out=pT3, in_=pT_d[l].rearrange("(k p) t -> p k t", p=128)), chan="pT", writes=pTb)
        for k in range(2):
            P.dma("pool", lambda e, k=k: e.dma_start(out=wp3[:, k, :], in_=wple_d[l][:, k * D:(k + 1) * D]), chan="wple", writes=wpb)
        for o in range(16):
            def epi(hf, p, pb, o=o):
                pp = ps[4 + hf]; ppb = psbuf[4 + hf]
                for k in range(2):
                    P.op("pe", lambda e, k=k: e.matmul(pp[:, :], lhsT=wp3[:, k, o * 128:(o + 1) * 128], rhs=pT3[:, k, hf * 512:(hf + 1) * 512],
                                                       start=(k == 0), stop=(k == 1)), reads=wpb + pTb, writes=[ppb])
                t, tb = newtmp()
                P.op("act", lambda e: e.activation(out=t[:, :], in_=p[:, :], func=AF.Sigmoid), reads=[pb], writes=[tb])
                t2, tb2 = newtmp()
                P.op("dve", lambda e: e.tensor_tensor(out=t2[:, :], in0=t[:, :], in1=pp[:, :], op=ALU.mult), reads=[tb, ppb], writes=[tb2])
                P.op("dve", lambda e: e.tensor_tensor(out=hT[:, o, hf * 512:(hf + 1) * 512], in0=hT[:, o, hf * 512:(hf + 1) * 512], in1=t2[:, :], op=ALU.add),
                     reads=[tb2, hbuf[o]], writes=[hbuf[o]])
                if hf == 1:
                    pending.append(o)
            dense(rhs_u, 16, epi)


    negD_, negDb = Y.view(0, 16 * NH, F32)
    negD = negD_.rearrange("p (j h) -> p j h", j=16)
    Dsb, Dsbb = Y.view(2048, 2 * T, F32, parts=16)
    dadd, daddb = Y.view(10240, 1, F32, parts=16)
    Dbf, Dbfb = Y.view(10496, T, BF16)
    sel_, selb = X.view(20480, NH * 128, BF16)
    sel = sel_.rearrange("p (h m) -> p h m", h=NH)
    dprep_done = []

    def d_prep():
        dprep_done.append(1)
        Dp = fz["Dprev"] if fused else Dprev_d
        P.dma("sp", lambda e: e.dma_start(out=Dsb[:, 0:T], in_=Dp), chan="dld", reads=[dbuf("D_all")], writes=[Dsbb])
        P.dma("sp", lambda e: e.dma_start(out=Dsb[:, T:2 * T], in_=Down_d), chan="dld", reads=[dbuf("D")], writes=[Dsbb])
        P.op("dve", lambda e: e.tensor_tensor(out=dadd[:, :], in0=Dsb[:, T - 1:T], in1=vcol("flag")[0:16, :], op=ALU.mult), reads=[Dsbb, vecbuf], writes=[daddb])
        P.op("dve", lambda e: e.tensor_scalar(out=Dsb[:, T:2 * T], in0=Dsb[:, T:2 * T], scalar1=dadd[:, 0:1], scalar2=None, op0=ALU.add),
             reads=[Dsbb, daddb], writes=[Dsbb])
        P.op("pool", lambda e: e.memset(Dbf, 0.0), writes=Dbfb)
        P.op("dve", lambda e: e.tensor_copy(out=Dbf[0:16, :], in_=Dsb[:, T:2 * T]), reads=[Dsbb] + Dbfb, writes=Dbfb)
        P.op("dve", lambda e: e.tensor_copy(out=sel, in_=ident_b[:, 0:16].unsqueeze(2).broadcast_to([128, NH, 128])), reads=[identbuf], writes=selb)
        for j in range(16):
            pj = ps[6 + j % 2]
            P.op("pe", lambda e, j=j, pj=pj: e.transpose(out=pj[:, 0:16], in_=Dsb[:, j * 128:(j + 1) * 128], identity=ident_f[0:16, 0:16]),
                 reads=[Dsbb, identfbuf], writes=[psbuf[6 + j % 2]])
            if j < 8:
                P.op("dve", lambda e, j=j, pj=pj: e.tensor_scalar(out=negD[:, j, :], in0=pj[:, 0:16], scalar1=-1.0, scalar2=vcol("prevbias"),
                                                                  op0=ALU.mult, op1=ALU.add), reads=[psbuf[6 + j % 2], vecbuf], writes=[negDb])
            else:
                P.op("dve", lambda e, j=j, pj=pj: e.tensor_scalar(out=negD[:, j, :], in0=pj[:, 0:16], scalar1=-1.0, scalar2=None, op0=ALU.mult),
                     reads=[psbuf[6 + j % 2]], writes=[negDb])

    def kv_phase():
        rmsnorm("kvn", stats="piped" if has("S2b") else "compute")
        kst = [X.view(i * 2048, T, BF16) for i in range(2)]
        vst = [X.view(4096 + i * 2048, T, BF16) for i in range(2)]
        vtm = [X.view(8192 + i * 2048, T, BF16) for i in range(2)]
        ones_f, onesfb = X.view(12288, T, F32)
        sp1, sp1b = X.view(16384, T, F32)
        dl, dlb = X.view(20480, T, F32)
        nbf, nbfb = tmp[3][:, 0:1], tmpbuf[3]
        P.op("pool", lambda e: e.memset(ones_f[:, :], 1.0), writes=onesfb)
        P.op("dve", lambda e: e.tensor_scalar(out=nbf[0:16, :], in0=vcol("bf")[0:16, :], scalar1=-1.0, scalar2=None, op0=ALU.mult),
             reads=[vecbuf], writes=[nbfb])

        def epi(hf, p, pb):
            sl = slice(hf * 512, (hf + 1) * 512)
            P.op("act", lambda e: e.activation(out=sp1[0:16, sl], in_=p[0:16, :], func=AF.Exp, scale=-1.0, bias=nbf[0:16, :]),
                 reads=[pb, nbfb], writes=sp1b)
            P.op("act", lambda e: e.activation(out=sp1[0:16, sl], in_=sp1[0:16, sl], func=AF.Ln, bias=epsc[0:16, 1:2]), reads=sp1b + [epsb], writes=sp1b)
            if hf == 1:
                P.op("dve", lambda e: e.tensor_tensor_scan(out=dl[0:16, :], data0=ones_f[0:16, :], data1=sp1[0:16, :], initial=0.0,
                                                           op0=ALU.mult, op1=ALU.add), reads=sp1b + onesfb, writes=dlb)
                P.op("dve", lambda e: e.tensor_scalar(out=dl[0:16, :], in0=dl[0:16, :], scalar1=-1.0, scalar2=None, op0=ALU.mult), reads=dlb, writes=dlb)
                P.dma("sp", lambda e: e.dma_start(out=Down_d, in_=dl[0:16, :]), chan="out_d", reads=dlb, writes=[dbuf("D")])
        dense(rhs_u, 16, epi, mcols=16)
        if fused:
            fz["Dprev"] = allgather("D", Down_d, NH, T, F32, [dbuf("D")])
        if fused:
            d_prep()
        for h in range(NH):
            def epi(hf, p, pb, h=h):
                k_ap, kb = kst[h % 2]
                P.op("act", lambda e: e.activation(out=k_ap[:, hf * 512:(hf + 1) * 512], in_=p[:, :], func=AF.Copy), reads=[pb], writes=kb)
                if hf == 1:
                    P.dma("sp", lambda e: e.dma_start(out=Kown_d[h], in_=k_ap), chan=f"out_k{h % 2}", reads=kb, writes=[dbuf(f"K{h}")])
            dense(rhs_u, 16, epi)
            if fused and h % 4 == 3:
                g = h // 4
                fz[f"Kprev{g}"] = allgather(f"K{g}_", Kown_d[4 * g:4 * g + 4].rearrange("h p t -> (h p) t"), 512, T, BF16,
                                            [dbuf(f"K{hh}") for hh in range(4 * g, 4 * g + 4)]).rearrange("(h p) t -> h p t", h=4)
        for h in range(NH):
            def epi(hf, p, pb, h=h):
                v_ap, vb = vst[h % 2]
                t_ap, tb = vtm[h % 2]
                P.op("act", lambda e: e.activation(out=v_ap[:, hf * 512:(hf + 1) * 512], in_=p[:, :], func=AF.Copy), reads=[pb], writes=vb)
                pt = ps[6 + hf][:, :].bitcast(BF16)
                for j in range(4):
                    jj = hf * 4 + j
                    P.op("pe", lambda e, j=j, jj=jj: e.transpose(out=pt[:, j * 128:(j + 1) * 128], in_=v_ap[:, jj * 128:(jj + 1) * 128], identity=ident_b[:, :]),
                         reads=vb + [identbuf], writes=[psbuf[6 + hf]])
                P.op("dve", lambda e: e.tensor_copy(out=t_ap[:, hf * 512:(hf + 1) * 512], in_=pt[:, 0:512]), reads=[psbuf[6 + hf]], writes=tb)
                if hf == 1:
                    P.dma("sp", lambda e: e.dma_start(out=Vown_d[h], in_=t_ap), chan=f"out_v{h % 2}", reads=tb, writes=[dbuf(f"V{h}")])
            dense(rhs_u, 16, epi)
            if fused and h % 4 == 3:
                g = h // 4
                fz[f"Vprev{g}"] = allgather(f"V{g}_", Vown_d[4 * g:4 * g + 4].rearrange("h p t -> (h p) t"), 512, T, BF16,
                                            [dbuf(f"V{hh}") for hh in range(4 * g, 4 * g + 4)]).rearrange("(h p) t -> h p t", h=4)

    def attn_phase():
        rmsnorm("mix1", stats="reuse" if fused else "compute")
        kh = [X.view(i * 4096, 2 * T, BF16) for i in range(2)]
        vh = [X.view(8192 + i * 4096, 2 * T, BF16) for i in range(2)]
        qh = [X.view(16384 + i * 2048, T, BF16) for i in range(2)]
        masks = rstd[:, :].bitcast(BF16).rearrange("p (r t) -> p r t", r=4)
        maskb = [rstdbuf]
        pt, ptb = sq, sqbuf
        rden, rdenb = tmp[2], tmpbuf[2]
        og = [A.view(h * 2048, T, BF16) for h in range(NH)]
        P.op("pool", lambda e: e.memset(masks[:, :, :], 0.0), writes=maskb)
        for r in range(4):
            P.op("pool", lambda e, r=r: e.affine_select(out=masks[:, r, :], in_=masks[:, r, :], pattern=[[1, 512]], compare_op=ALU.is_ge,
                                                       fill=NEG, base=-128 * r, channel_multiplier=-1), reads=maskb, writes=maskb)
        if not dprep_done:
            d_prep()
        dbgout("negD", negD_, negDb, [128, 16 * NH], F32)
        dbgout("sel", sel_, selb, [128, NH * 128], BF16)
        dbgout("Dbf", Dbf, Dbfb, [128, T], BF16)
        dbgout("Dsb", Dsb, Dsbb, [16, 2 * T], F32)
        dbgout("masks", rstd[:, :], maskb, [128, T], F32)
        npt = [0]
        cnt["single"] = True
        SB = (6, 7, 2, 3)
        NHL = getattr(build, "NHL", NH)
        for h in range(NH):
            s = h % 2
            k_ap, kb = kh[s]; v_ap, vb = vh[s]; q_ap, qb_ = qh[s]
            if h >= NHL:
                def epi0(hf, p, pb, h=h):
                    P.op("act", lambda e: e.activation(out=og[h][0][:, hf * 512:(hf + 1) * 512], in_=p[:, :], func=AF.Copy), reads=[pb], writes=og[h][1])
                dense(rhs_u, 16, epi0)
                continue
            v3 = v_ap.rearrange("p (j e) -> p j e", j=16)
            Kph = fz[f"Kprev{h // 4}"][h % 4] if fused else Kprev_d[h]
            Vph = fz[f"Vprev{h // 4}"][h % 4] if fused else Vprev_d[h]
            P.dma("sp", lambda e, h=h, k_ap=k_ap: e.dma_start(out=k_ap[:, 0:T], in_=Kph), chan=f"kld{s}", reads=[dbuf(f"K{h // 4}__all")], writes=kb)
            P.dma("sp", lambda e, h=h, k_ap=k_ap: e.dma_start(out=k_ap[:, T:2 * T], in_=Kown_d[h]), chan=f"kld{s}", reads=[dbuf(f"K{h}")], writes=kb)
            P.dma("sp", lambda e, h=h, v_ap=v_ap: e.dma_start(out=v_ap[:, 0:T], in_=Vph), chan=f"vld{s}", reads=[dbuf(f"V{h // 4}__all")], writes=vb)
            P.dma("sp", lambda e, h=h, v_ap=v_ap: e.dma_start(out=v_ap[:, T:2 * T], in_=Vown_d[h]), chan=f"vld{s}", reads=[dbuf(f"V{h}")], writes=vb)

            def epi(hf, p, pb, q_ap=q_ap, qb_=qb_):
                P.op("act", lambda e: e.activation(out=q_ap[:, hf * 512:(hf + 1) * 512], in_=p[:, :], func=AF.Copy, scale=HD ** -0.5), reads=[pb], writes=qb_)
            dense(rhs_u, 16, epi)
            if h == 0:
                dbgout("kh", k_ap, kb, [128, 2 * T], BF16)
                dbgout("vh", v_ap, vb, [128, 2 * T], BF16)
                dbgout("qh", q_ap, qb_, [128, T], BF16)
            for qb in range(2):
                qs = slice(qb * 512, (qb + 1) * 512)
                tiles = [(j, None) for j in range(8)]
                for jj in range(4 * qb + 4):
                    tiles.append((8 + jj, jj - 4 * qb if jj >= 4 * qb else None))
                po, pd = ps[4], ps[5]
                ptidx = {}

                def qk(ti):
                    j, mr = tiles[ti]
                    pS = ps[SB[ti % 4]]; pSb = psbuf[SB[ti % 4]]
                    P.op("pe", lambda e: e.matmul(pS[:, :], lhsT=k_ap[:, j * 128:(j + 1) * 128], rhs=q_ap[:, qs], start=True, stop=False),
                         reads=kb + qb_, writes=[pSb])
                    P.op("pe", lambda e: e.matmul(pS[:, :], lhsT=sel[:, h, :], rhs=Dbf[:, qs], start=False, stop=True),
                         reads=selb + Dbfb, writes=[pSb])
                    if mr is not None:
                        P.op("dve", lambda e: e.tensor_tensor(out=pS[:, :], in0=pS[:, :], in1=masks[:, mr, :], op=ALU.add),
                             reads=[pSb] + maskb, writes=[pSb])

                def ex(ti):
                    j, mr = tiles[ti]
                    pS = ps[SB[ti % 4]]; pSb = psbuf[SB[ti % 4]]
                    i = npt[0] % 3; npt[0] += 1
                    ptidx[ti] = i
                    P.op("act", lambda e: e.activation(out=pt[i][:, :], in_=pS[:, :], func=AF.Exp, bias=negD[:, j, h:h + 1], scale=1.0),
                         reads=[pSb, negDb], writes=[ptb[i]])

                def pv(ti):
                    j, mr = tiles[ti]
                    i = ptidx[ti]
                    first, last = ti == 0, ti == len(tiles) - 1
                    P.op("pe", lambda e: e.matmul(po[:, :], lhsT=v3[:, j, :], rhs=pt[i][:, :], start=first, stop=last),
                         reads=vb + [ptb[i]], writes=[psbuf[4]])
                    P.op("pe", lambda e: e.matmul(pd[:, :], lhsT=ones_b[:, :], rhs=pt[i][:, :], start=first, stop=last),
                         reads=[onesbuf, ptb[i]], writes=[psbuf[5]])
                qk(0)
                if len(tiles) > 1:
                    qk(1)
                for ti in range(len(tiles)):
                    if ti + 2 < len(tiles):
                        qk(ti + 2)
                    ex(ti)
                    pv(ti)
                P.op("dve", lambda e: e.reciprocal(out=rden[:, :], in_=pd[:, :]), reads=[psbuf[5]], writes=[rdenb])
                P.op("dve", lambda e, h=h: e.tensor_tensor(out=og[h][0][:, qs], in0=po[:, :], in1=rden[:, :], op=ALU.mult),
                     reads=[psbuf[4], rdenb], writes=og[h][1])
                if h == 0 and qb == 0:
                    dbgout("rden", rden[:, :], [rdenb], [128, 512], F32)
                if h == 0 and qb == 1:
                    dbgout("og0", og[0][0], og[0][1], [128, T], BF16)

        cnt["single"] = False

        def rhs_og(k):
            return og[k][0], og[k][1]
        for o in range(16):
            dense(rhs_og, 16, epi_resid(o, final=True))


    if has("S1") or has("S2a"):
        OL_d = xio("OL", [NH, 128, T], F32, has("S1"))
        QC_d = xio("QC", [NH, 128, T], BF16, has("S1"))
        G_d = xio("G", [NH, 128, T], BF16, has("S1"))
    if has("S1"):
        SF_d = xio("SF", [NH, 128, 128], F32, True)
    if has("S2a") and not fused:
        SFp_d = xio("SFprev", [NH, 128, 128], F32, False)

    def hgrn1():
        rmsnorm("mix0")
        cnt["single"] = True
        cnt["group"] = 2
        lbv, lbb = Y.view(0, 16, F32)
        oml, omlb = Y.view(64, 16, F32)
        noml, nomlb = Y.view(128, 16, F32)
        dec = [Y.view(256 + i * 256, NCHUNK, F32) for i in range(2)]
        attm = [Y.view(768 + i * 256, 128, BF16) for i in range(2)]
        S32 = [Y.view(1280, 128, F32), Y.view(13568, 128, F32)]
        m01, m01b = Y.view(1792, 128, F32)
        cmask, cmb = Y.view(2304, T, F32)
        onesf, onfb = Y.view(6400, T, F32)
        S16 = [Y.view(10496 + i * 256, 128, BF16) for i in range(12)]
        constb = lbb
        P.op("dve", lambda e: e.tensor_tensor(out=lbv, in0=vecs[:, VEC_COLS["lb0"][0]:VEC_COLS["lb0"][0] + 16],
                                              in1=vecs[:, VEC_COLS["lb1"][0]:VEC_COLS["lb1"][0] + 16], op=ALU.subtract), reads=[vecbuf], writes=constb)
        P.op("act", lambda e: e.activation(out=lbv, in_=lbv, func=AF.Sigmoid), reads=constb, writes=constb)
        P.op("dve", lambda e: e.tensor_scalar(out=oml, in0=lbv, scalar1=-1.0, scalar2=1.0, op0=ALU.mult, op1=ALU.add), reads=constb, writes=constb)
        P.op("dve", lambda e: e.tensor_scalar(out=noml, in0=lbv, scalar1=-1.0, scalar2=None, op0=ALU.add), reads=constb, writes=constb)
        P.op("pool", lambda e: e.memset(onesf, 1.0), writes=onfb)
        P.op("pool", lambda e: e.memset(cmask, 1.0), writes=cmb)
        cm3 = cmask.rearrange("p (c i) -> p c i", i=CH)
        P.op("pool", lambda e: e.memset(cm3[:, :, 0:1], 0.0), writes=cmb)
        P.op("pool", lambda e: e.memset(m01, 1.0), writes=m01b)
        P.op("pool", lambda e: e.affine_select(out=m01, in_=m01, pattern=[[1, 128]], compare_op=ALU.is_ge, fill=0.0, base=0, channel_multiplier=-1),
             reads=m01b, writes=m01b)
        for g in range(4):
            P.op("pool", lambda e, g=g: e.affine_select(out=m01[32 * g:32 * g + 32, :], in_=m01[32 * g:32 * g + 32, :], pattern=[[-1, 128]],
                                                      compare_op=ALU.is_ge, fill=0.0, base=32 * (g + 1) - 1, channel_multiplier=0),
                 reads=m01b, writes=m01b)
        qs, qsb = X.view(0, T, F32)
        sg, sgb = X.view(4096, T, F32)
        lf, lfb = X.view(8192, T, F32)
        bb, bbb = X.view(12288, T, F32)
        Bc, Bcb = X.view(16384, T, F32)
        e1, e1b = X.view(20480, T, F32)
        bb3 = bb.rearrange("p (c i) -> p c i", i=CH)
        e13 = e1.rearrange("p (c i) -> p c i", i=CH)

        def hb(s, i):
            return A.view(s * 10240 + i * 2048, T, BF16)
        kec = [A.view(20480 + i * 2048, T, BF16, parts=32) for i in range(2)]
        vcl = [A.view(24576 + i * 2048, T, BF16, parts=32) for i in range(2)]
        o_out, oob = A.view(28672, T, F32)
        rb16 = rstd[:, :].bitcast(BF16)
        gs, gsb = rb16[:, 0:T], [rstdbuf]
        qcm, qcb = rb16[:, T:2 * T], [rstdbuf]
        scale = HD ** -0.5

        def stageA(h):
            s = h % 2
            q_in, qib = hb(s, 0); k_in, kib = hb(s, 1); vtm, vtb = hb(s, 2); keT, keTb = hb(s, 3); vT, vTb = hb(s, 4)
            dc, dcb = dec[s]

            def epi_act(dst, dstb, func):
                def f(hf, p, pb):
                    P.op("act", lambda e: e.activation(out=dst[:, hf * 512:(hf + 1) * 512], in_=p[:, :], func=func), reads=[pb], writes=dstb)
                return f
            def gate1():
                P.op("act", lambda e: e.activation(out=lf, in_=sg, func=AF.Ln, scale=oml[:, h:h + 1], bias=lbv[:, h:h + 1]), reads=sgb + constb, writes=lfb)
                yield
                P.op("dve", lambda e: e.tensor_scalar(out=sg, in0=sg, scalar1=noml[:, h:h + 1], scalar2=oml[:, h:h + 1], op0=ALU.mult, op1=ALU.add),
                     reads=sgb + constb, writes=sgb)
                yield
                P.op("dve", lambda e: e.tensor_tensor_scan(out=bb, data0=cmask, data1=lf, initial=0.0, op0=ALU.mult, op1=ALU.add), reads=lfb + cmb, writes=bbb)
                yield
                P.op("dve", lambda e: e.tensor_tensor_scan(out=Bc, data0=onesf, data1=lf, initial=0.0, op0=ALU.mult, op1=ALU.add), reads=lfb + onfb, writes=Bcb)
                yield
                P.op("act", lambda e: e.activation(out=lf, in_=bb, func=AF.Exp, scale=-1.0), reads=bbb, writes=lfb)
                yield
                P.op("dve", lambda e: e.tensor_tensor(out=k_in, in0=sg, in1=lf, op=ALU.mult), reads=sgb + lfb, writes=kib)
                yield
                P.op("act", lambda e: e.activation(out=dc, in_=bb3[:, :, CH - 1], func=AF.Exp), reads=bbb, writes=dcb)
                yield
                P.op("dve", lambda e: e.tensor_tensor(out=e13, in0=bb3[:, :, CH - 1:CH].broadcast_to([128, NCHUNK, CH]), in1=bb3, op=ALU.subtract),
                     reads=bbb, writes=e1b)
                yield
                P.op("act", lambda e: e.activation(out=e1, in_=e1, func=AF.Exp), reads=e1b, writes=e1b)
                yield
                P.op("dve", lambda e: e.tensor_tensor(out=keT, in0=sg, in1=e1, op=ALU.mult), reads=sgb + e1b, writes=keTb)
                yield
                P.op("act", lambda e: e.activation(out=e1, in_=bb, func=AF.Exp), reads=bbb, writes=e1b)
                yield
                P.op("act", lambda e: e.activation(out=lf, in_=Bc, func=AF.Exp), reads=Bcb, writes=lfb)
                yield

            def mix(ga, gb, ratio=2):
                alive_a = alive_b = True
                while alive_a or alive_b:
                    for _ in range(ratio):
                        if alive_a:
                            try:
                                next(ga)
                            except StopIteration:
                                alive_a = False
                        yield
                    if alive_b:
                        try:
                            next(gb)
                        except StopIteration:
                            alive_b = False

            def three_tiles():
                yield from dense_gen(rhs_u, 16, epi_act(qs, qsb, AF.Silu)); yield
                yield from dense_gen(rhs_u, 16, epi_act(vT, vTb, AF.Copy)); yield
                yield from dense_gen(rhs_u, 16, epi_act(gs, gsb, AF.Silu)); yield
            yield from dense_gen(rhs_u, 16, epi_act(sg, sgb, AF.Sigmoid)); yield
            yield from mix(three_tiles(), gate1(), ratio=2)
            P.dma("sp", lambda e: e.dma_start(out=G_d[h], in_=gs), chan="out_g", reads=gsb, writes=[dbuf(f"G{h}")])
            P.op("dve", lambda e: e.scalar_tensor_tensor(out=q_in, in0=qs, scalar=scale, in1=e1, op0=ALU.mult, op1=ALU.mult), reads=qsb + e1b, writes=qib)
            yield
            P.op("dve", lambda e: e.scalar_tensor_tensor(out=qcm, in0=qs, scalar=scale, in1=lf, op0=ALU.mult, op1=ALU.mult), reads=qsb + lfb, writes=qcb)
            yield
            P.dma("sp", lambda e: e.dma_start(out=QC_d[h], in_=qcm), chan="out_qc", reads=qcb, writes=[dbuf(f"QC{h}")])
            yield
            for (src, srcb, dst, dstb, bank) in ((vT, vTb, vtm, vtb, 4),):
                ptv = ps[bank][:, :].bitcast(BF16)
                for j in range(8):
                    P.op("pe", lambda e, j=j, src=src, ptv=ptv: e.transpose(out=ptv[:, j * 128:(j + 1) * 128], in_=src[:, j * 128:(j + 1) * 128], identity=ident_b[:, :]),
                         reads=srcb + [identbuf], writes=[psbuf[bank]])
                P.op("act", lambda e, dst=dst, ptv=ptv: e.activation(out=dst, in_=ptv, func=AF.Copy), reads=[psbuf[bank]], writes=dstb)
            yield

        def rec(h):
            s = h % 2
            q_in, qib = hb(s, 0); k_in, kib = hb(s, 1); vtm, vtb = hb(s, 2); keT, keTb = hb(s, 3); vT, vTb = hb(s, 4)
            dc, dcb = dec[s]

            def prep_batch(bt):
                for (src, srcb, (dst, dstb), bank) in ((keT, keTb, kec[bt % 2], 4), (vT, vTb, vcl[bt % 2], 4)):
                    ptv = ps[bank][:, :].bitcast(BF16)
                    for ci in range(8):
                        c = 8 * bt + ci
                        P.op("pe", lambda e: e.transpose(out=ptv[0:32, ci * 128:(ci + 1) * 128], in_=src[:, c * CH:(c + 1) * CH], identity=ident_b[:, :]),
                             reads=srcb + [identbuf], writes=[psbuf[bank]])
                    P.op("act", lambda e: e.activation(out=dst, in_=ptv[0:32, :], func=AF.Copy), reads=[psbuf[bank]], writes=dstb)
                    yield

            def kv_tile(jt):
                for cc in range(4):
                    c = 4 * jt + cc
                    bt, ci = c // 8, c % 8
                    kc, kcb = kec[bt % 2]; vc_, vcb = vcl[bt % 2]
                    kvbank = (2, 3, 7)[c % 3]
                    kvp = ps[kvbank][:, 0:128]; kvb = psbuf[kvbank]
                    so, sob = S32[c % 2]; sn32, sn32b = S32[(c + 1) % 2]
                    P.op("pe", lambda e: e.matmul(kvp, lhsT=kc[:, ci * 128:(ci + 1) * 128], rhs=vc_[:, ci * 128:(ci + 1) * 128], start=True, stop=True),
                         reads=kcb + vcb, writes=[kvb])
                    if c == 0:
                        P.op("dve", lambda e: e.tensor_copy(out=sn32, in_=kvp), reads=[kvb], writes=sn32b)
                    else:
                        P.op("dve", lambda e: e.scalar_tensor_tensor(out=sn32, in0=so, scalar=dc[:, c:c + 1], in1=kvp, op0=ALU.mult, op1=ALU.add),
                             reads=[kvb] + sob + dcb, writes=sn32b)
                    if c < NCHUNK - 1:
                        sn, snb = S16[(c + 1) % 12]
                        P.op("act", lambda e: e.activation(out=sn, in_=sn32, func=AF.Copy), reads=sn32b, writes=snb)
                    else:
                        P.dma("sp", lambda e: e.dma_start(out=SF_d[h], in_=sn32), chan="out_sf", reads=sn32b, writes=[dbuf(f"SF{h}")])
                    yield
            yield from prep_batch(0)
            yield from kv_tile(0)
            for jt in range(8):
                tsl = slice(jt * 128, (jt + 1) * 128)
                r = jt % 2
                ap_att = ps[5][:, r * 128:(r + 1) * 128]; ab = psbuf[5]
                ap_o = ps[6][:, r * 128:(r + 1) * 128]; ob = psbuf[6]
                am, amb = attm[r]
                P.op("pe", lambda e: e.matmul(ap_att, lhsT=k_in[:, tsl], rhs=q_in[:, tsl], start=True, stop=True), reads=kib + qib, writes=[ab])
                yield
                if jt % 2 == 0 and jt // 2 + 1 < 4:
                    yield from prep_batch(jt // 2 + 1)
                if jt < 7:
                    yield from kv_tile(jt + 1)
                P.op("dve", lambda e: e.tensor_tensor(out=am, in0=ap_att, in1=m01, op=ALU.mult), reads=[ab] + m01b, writes=amb)
                P.op("pe", lambda e: e.matmul(ap_o, lhsT=vtm[:, tsl], rhs=am, start=True, stop=(jt == 0), skip_group_check=True),
                     reads=vtb + amb, writes=[ob])
                for cc in range(4):
                    c = 4 * jt + cc
                    if c == 0:
                        continue
                    sc, scb = S16[c % 12]
                    last = cc == 3
                    P.op("pe", lambda e: e.matmul(ap_o[:, 32 * cc:32 * cc + 32], lhsT=sc, rhs=q_in[:, c * CH:(c + 1) * CH],
                                                  start=False, stop=last, skip_group_check=True),
                         reads=scb + qib, writes=[ob])
                P.op("act", lambda e: e.activation(out=o_out[:, tsl], in_=ap_o, func=AF.Copy), reads=[ob], writes=oob)
                yield
            P.dma("sp", lambda e: e.dma_start(out=OL_d[h], in_=o_out), chan="out_ol", reads=oob, writes=[dbuf(f"OL{h}")])

        H1LIM = getattr(build, "H1LIM", None)
        if H1LIM is not None:
            g_ = stageA(0)
            for _ in range(3):
                next(g_)
            if H1LIM >= 1:
                next(g_)
                dbgout("q_in", hb(0, 0)[0], hb(0, 0)[1], [128, T], BF16)
                dbgout("k_in", hb(0, 1)[0], hb(0, 1)[1], [128, T], BF16)
                dbgout("keT", hb(0, 3)[0], hb(0, 3)[1], [128, T], BF16)
                dbgout("vtm", hb(0, 2)[0], hb(0, 2)[1], [128, T], BF16)
                dbgout("bb", bb, bbb, [128, T], F32)
                dbgout("Bc", Bc, Bcb, [128, T], F32)
                dbgout("dec", dec[0][0], dec[0][1], [128, NCHUNK], F32)
                dbgout("m01", m01, m01b, [128, 128], F32)
            if H1LIM >= 2:
                r_ = rec(0)
                for _ in range(H1LIM - 1):
                    next(r_)
                dbgout("o_out", o_out, oob, [128, T], F32)
            return
        prev = None
        for h in range(NH + 1):
            gens = []
            if h < NH:
                gens.append(stageA(h))
            if prev is not None:
                gens.append(rec(prev))
            while gens:
                for g_ in list(gens):
                    try:
                        next(g_)
                    except StopIteration:
                        gens.remove(g_)
            prev = h if h < NH else None
        cnt["single"] = False
        cnt["group"] = 4
        if fused:
            fz["SFp"] = allgather("SF", SF_d.rearrange("h p e -> (h p) e"), NH * 128, 128, F32,
                                  [dbuf(f"SF{h}") for h in range(NH)]).rearrange("(h p) e -> h p e", h=NH)

    def hgrn2():
        og = [A.view(h * 2048, T, BF16) for h in range(NH)]
        o32 = [X.view(0, T, F32), X.view(4096, T, F32), X.view(18432, T, F32)]
        qc = [X.view(8192 + i * 2048, T, BF16) for i in range(2)]
        gsl = [X.view(12288, T, BF16), X.view(14336, T, BF16), X.view(22528, T, BF16)]
        s32 = [X.view(16384 + i * 512, 128, F32) for i in range(2)]
        s16 = [X.view(17408 + i * 256, 128, BF16) for i in range(2)]
        def load(h):
            s = h % 2
            o_ap, o_b = o32[h % 3]; q_ap, q_b = qc[s]; g_ap, g_b = gsl[h % 3]; sa, sab = s32[s]; sh, shb = s16[s]
            SFp = fz["SFp"] if fused else SFp_d
            P.dma("sp", lambda e: e.dma_start(out=o_ap, in_=OL_d[h]), chan=f"ld_o{h % 3}", reads=[dbuf(f"OL{h}")], writes=o_b)
            P.dma("sp", lambda e: e.dma_start(out=q_ap, in_=QC_d[h]), chan=f"ld_q{s}", reads=[dbuf(f"QC{h}")], writes=q_b)
            P.dma("sp", lambda e: e.dma_start(out=g_ap, in_=G_d[h]), chan=f"ld_g{h % 3}", reads=[dbuf(f"G{h}")], writes=g_b)
            P.dma("sp", lambda e: e.dma_start(out=sa, in_=SFp[h]), chan=f"ld_s{s}", reads=[dbuf("SF_all")], writes=sab)

        def cvt(h):
            s = h % 2
            sa, sab = s32[s]; sh, shb = s16[s]
            P.op("dve", lambda e: e.tensor_scalar(out=sh, in0=sa, scalar1=vcol("flag"), scalar2=None, op0=ALU.mult), reads=sab + [vecbuf], writes=shb)

        unit_state = {}

        def front(h, hf):
            s = h % 2
            o_ap, o_b = o32[h % 3]; q_ap, q_b = qc[s]; sh, shb = s16[s]
            sl = slice(hf * 512, (hf + 1) * 512)
            pc = ps[4 + hf]; pcb = psbuf[4 + hf]
            pm = ps[6 + hf]; pmb = psbuf[6 + hf]
            P.op("pe", lambda e: e.matmul(pc[:, :], lhsT=sh, rhs=q_ap[:, sl], start=True, stop=True), reads=shb + q_b, writes=[pcb])
            P.op("dve", lambda e: e.tensor_tensor(out=o_ap[:, sl], in0=o_ap[:, sl], in1=pc[:, :], op=ALU.add), reads=[pcb] + o_b, writes=o_b)
            i = cnt["sq"] % 3; cnt["sq"] += 1
            P.op("dve", lambda e: e.tensor_tensor(out=sq[i][:, :], in0=o_ap[:, sl], in1=o_ap[:, sl], op=ALU.mult), reads=o_b, writes=[sqbuf[i]])
            P.op("pe", lambda e: e.matmul(pm[:, :], lhsT=ones_b[:, :], rhs=sq[i][:, :], start=True, stop=True), reads=[sqbuf[i], onesbuf], writes=[pmb])

        def back_act(h, hf):
            pm = ps[6 + hf]; pmb = psbuf[6 + hf]
            t, tb = newtmp()
            unit_state[(h, hf)] = (t, tb)
            P.op("act", lambda e: e.activation(out=t[:, :], in_=pm[:, :], func=AF.Ln, scale=1.0 / HD, bias=epsc[:, 0:1]), reads=[pmb, epsb], writes=[tb])
            P.op("act", lambda e: e.activation(out=t[:, :], in_=t[:, :], func=AF.Exp, scale=-0.5), reads=[tb], writes=[tb])

        def back_dve(h, hf):
            o_ap, o_b = o32[h % 3]; g_ap, g_b = gsl[h % 3]
            sl = slice(hf * 512, (hf + 1) * 512)
            t, tb = unit_state[(h, hf)]
            P.op("dve", lambda e: e.scalar_tensor_tensor(out=t[:, :], in0=o_ap[:, sl], scalar=vcol("hg"), in1=t[:, :], op0=ALU.mult, op1=ALU.mult),
                 reads=o_b + [tb, vecbuf], writes=[tb])
            P.op("dve", lambda e: e.tensor_tensor(out=og[h][0][:, sl], in0=t[:, :], in1=g_ap[:, sl], op=ALU.mult), reads=[tb] + g_b, writes=og[h][1])
        units = [(h, hf) for h in range(NH) for hf in range(2)]
        load(0)
        cvt(0)
        load(1)
        cvt(1)
        for k, (h, hf) in enumerate(units):
            if k > 1:
                back_act(*units[k - 2])
            front(h, hf)
            if k > 1:
                back_dve(*units[k - 2])
            if hf == 1 and h + 2 < NH:
                load(h + 2)
            if hf == 0 and h >= 1 and h + 1 < NH:
                cvt(h + 1)
        for u in units[-2:]:
            back_act(*u)
            back_dve(*u)
        cnt["single"] = False

        def rhs_og(k):
            return og[k][0], og[k][1]
        for o in range(16):
            dense(rhs_og, 16, epi_resid(o, final=True))

    for seg in segs:
        if seg == "S2b":
            rmsnorm("mlp0", stats="piped" if has("S2a") else "compute"); mlp(); rmsnorm("ple0", stats="piped"); ple(0)
        elif seg == "S3b":
            rmsnorm("mlp1", stats="piped" if has("S3a") else "compute"); mlp(); rmsnorm("ple1", stats="piped"); ple(1)
            rmsnorm("fin", out_f32_inplace=True, stats="piped")
        elif seg == "S2c":
            kv_phase()
        elif seg == "S3a":
            attn_phase()
        elif seg == "S1":
            hgrn1()
        elif seg == "S2a":
            hgrn2()

    for c in range(NCH):
        P.dma(("sp", "act", "pool")[c % 3], lambda e, c=c: e.dma_start(out=hout_d[c * 128:(c + 1) * 128, :], in_=hT[:, c, :]), chan="hout", reads=[hbuf[c]])
    finals = ["hout"] + [c for c in P.chan_count if c.startswith("out_")]
    assert widx["i"] == len(b.plan) or getattr(build, "H1LIM", None) is not None, (widx["i"], len(b.plan))
    P.emit_all(nc, finals)
    b.es.close()
    return b


NCORES = 8
_PROG_CACHE = {}


def _get_prog(segs, fused=False):
    key = (tuple(segs), fused)
    if key not in _PROG_CACHE:
        _PROG_CACHE[key] = build(list(segs), fused=fused)
    return _PROG_CACHE[key]


def _run(segs, inp, per_core, fused=False):
    b = _get_prog(segs, fused)
    slabs = [pack_slab(inp, s) for s in b.plan]
    walls = {f"wall{i}": np.stack(slabs[i * b.wch:(i + 1) * b.wch]) for i in range(len(b.walls))}
    wple = np.ascontiguousarray(np.stack([np.asarray(inp["w_ple_up"][l], np.float32).reshape(2, 128, 2048).transpose(1, 0, 2).reshape(128, 4096)
                                          for l in range(2)]))
    maps = []
    for r in range(NCORES):
        m = {"wple": wple, "vecs": pack_vecs(inp, r)}
        m.update(walls)
        m.update(per_core[r])
        maps.append(m)
    res = run_bass_kernel_spmd(b.nc, maps, core_ids=list(range(NCORES)))
    return res.results


ALL_SEGS = ["S1", "S2a", "S2b", "S2c", "S3a", "S3b"]


def kernel(**inputs):
    inp = {k: np.asarray(v) for k, v in inputs.items()}
    x = np.asarray(inp["x"], np.float32)
    p = np.asarray(inp["p"], np.float32)
    base = []
    for r in range(NCORES):
        bb, sl = r // 2, slice((r % 2) * T, (r % 2 + 1) * T)
        base.append({"hin": np.ascontiguousarray(x[bb, sl].T), "pT": np.ascontiguousarray(p[:, bb, sl].transpose(0, 2, 1))})
    o = _run(ALL_SEGS, inp, base, fused=True)
    out = np.empty((4, SEQ, D), np.float32)
    for r in range(NCORES):
        bb, sl = r // 2, slice((r % 2) * T, (r % 2 + 1) * T)
        out[bb, sl] = np.asarray(o[r]["hout"]).T
    return out
```

```python
import numpy as np
from contextlib import ExitStack
import concourse.bass as bass
import concourse.mybir as mybir
from concourse.bass_utils import run_bass_kernel_spmd

F32 = mybir.dt.float32
BF16 = mybir.dt.bfloat16
AF = mybir.ActivationFunctionType
ALU = mybir.AluOpType

D = 2048
NCH = 16
T = 1024
SEQ = 2048
NH = 16
HD = 128
PLE = 256
DFF = 8192
EPS = 1e-6
CH = 32
NCHUNK = T // CH
NSLOT = 6
NEG = -30000.0


class Buf:
    __slots__ = ("name", "w", "rs", "excl")

    def __init__(self, name, excl=False):
        self.name = name
        self.w = None
        self.rs = {}
        self.excl = excl


class Op:
    __slots__ = ("eng", "emit", "deps", "sig", "val", "chan", "idx", "inc")

    def __init__(self, eng, emit, chan=None):
        self.eng = eng
        self.emit = emit
        self.deps = []
        self.sig = False
        self.val = None
        self.chan = chan
        self.idx = None
        self.inc = 16


class _Rec:
    def __init__(self):
        self.calls = []

    def __getattr__(self, name):
        def f(*a, **k):
            self.calls.append((name, a, k))
            return self
        return f


def _freeze(emit):
    r = _Rec()
    emit(r)
    assert len(r.calls) == 1, r.calls
    name, a, k = r.calls[0]
    return lambda eng: getattr(eng, name)(*a, **k)


class Prog:
    ENGS = ("pe", "act", "dve", "pool", "sp")

    def __init__(self, same_engine_sync=True):
        self.ops = {e: [] for e in self.ENGS}
        self.chan_count = {}
        self.same_engine_sync = same_engine_sync
        self.nops = 0

    @staticmethod
    def _flat(x):
        out = []
        for b in x:
            if isinstance(b, (list, tuple)):
                out.extend(Prog._flat(b))
            else:
                out.append(b)
        return out

    def _track(self, op, reads, writes):
        reads = self._flat(reads)
        writes = self._flat(writes)
        writes = writes + [b for b in reads if b.excl]
        reads = [b for b in reads if not b.excl]
        deps = []
        for b in reads:
            if b.w is not None:
                deps.append(b.w)
        for b in writes:
            if b.w is not None:
                deps.append(b.w)
            deps.extend(b.rs.values())
        for b in writes:
            b.w = op
            b.rs = {}
        for b in reads:
            key = op.chan if op.chan is not None else op.eng
            b.rs[key] = op
        seen = set()
        for d in deps:
            if d is op or id(d) in seen:
                continue
            seen.add(id(d))
            if d.chan is None and d.eng == op.eng:
                if op.eng == "pe" or not self.same_engine_sync:
                    continue
                if op.chan is not None:
                    pass
            d.sig = True
            op.deps.append(d)

    def op(self, eng, emit, reads=(), writes=()):
        o = Op(eng, _freeze(emit))
        o.idx = self.nops
        self.nops += 1
        self._track(o, reads, writes)
        self.ops[eng].append(o)
        return o

    def dma(self, queue, emit, chan, reads=(), writes=()):
        o = Op(queue, _freeze(emit), chan=chan)
        o.idx = self.nops
        self.nops += 1
        self._track(o, reads, writes)
        c = self.chan_count.get(chan, 0) + 16
        self.chan_count[chan] = c
        o.val = c
        o.sig = True
        self.ops[queue].append(o)
        return o

    def coll(self, emit, chan, reads=(), writes=()):
        o = Op("pool", _freeze(emit), chan=chan)
        o.idx = self.nops
        self.nops += 1
        self._track(o, reads, writes)
        c = self.chan_count.get(chan, 0) + 1
        self.chan_count[chan] = c
        o.val = c
        o.sig = True
        o.inc = 1
        self.ops["pool"].append(o)
        return o

    def emit_all(self, nc, final_chans):
        with ExitStack() as es:
            esem = {e: es.enter_context(nc.semaphore("s_" + e)) for e in self.ENGS}
            csem = {c: es.enter_context(nc.semaphore("c_" + c)) for c in self.chan_count}
            for e in self.ENGS:
                cnt = 0
                for o in self.ops[e]:
                    if o.chan is None and o.sig:
                        cnt += 1
                        o.val = cnt

            def run(eng_name, eng):
                seen = {}
                for o in self.ops[eng_name]:
                    for d in o.deps:
                        if d.chan is not None:
                            key, sem = ("c", d.chan), csem[d.chan]
                        else:
                            key, sem = ("e", d.eng), esem[d.eng]
                        if seen.get(key, 0) >= d.val:
                            continue
                        seen[key] = d.val
                        eng.wait_ge(sem, d.val)
                    ins = o.emit(eng)
                    if o.chan is not None:
                        ins.then_inc(csem[o.chan], o.inc)
                    elif o.sig:
                        ins.then_inc(esem[eng_name], 1)
                if eng_name == "sp":
                    for c in final_chans:
                        eng.wait_ge(csem[c], self.chan_count[c])

            with nc.Block() as block:
                @block.tensor
                def _(e):
                    run("pe", e)

                @block.scalar
                def _(e):
                    run("act", e)

                @block.vector
                def _(e):
                    run("dve", e)

                @block.gpsimd
                def _(e):
                    run("pool", e)

                @block.sync
                def _(e):
                    run("sp", e)


VEC_COLS = {}
_c = 0
for _name, _n in [("mix0", 16), ("mix1", 16), ("mlp0", 16), ("mlp1", 16), ("ple0", 16), ("ple1", 16),
                  ("kvn", 16), ("fin", 16), ("lb0", 16), ("lb1", 16), ("hg", 1), ("bf", 1),
                  ("flag", 1), ("prevbias", 1)]:
    VEC_COLS[_name] = (_c, _n)
    _c += _n
NVEC = _c


def pack_vecs(inp, rank):
    v = np.zeros((128, NVEC), np.float32)

    def put(name, arr):
        c0, n = VEC_COLS[name]
        v[:, c0:c0 + n] = np.asarray(arr, np.float32).reshape(n, 128).T

    put("mix0", inp["mix_norm"][0]); put("mix1", inp["mix_norm"][1])
    put("mlp0", inp["mlp_norm"][0]); put("mlp1", inp["mlp_norm"][1])
    put("ple0", inp["ple_norm"][0]); put("ple1", inp["ple_norm"][1])
    put("kvn", inp["kv_norm"]); put("fin", inp["final_norm"])
    put("lb0", inp["a_lb_logits"][0]); put("lb1", inp["a_lb_logits"][1])
    put("hg", inp["a_head_gain"][0])
    c0, _ = VEC_COLS["bf"]
    v[:16, c0] = np.asarray(inp["b_f"], np.float32)
    odd = rank % 2
    v[:, VEC_COLS["flag"][0]] = 1.0 if odd else 0.0
    v[:, VEC_COLS["prevbias"][0]] = 0.0 if odd else NEG
    return v


def pack_slab(inp, spec):
    kind = spec[0]
    if kind == "std":
        _, name, layer, r0, c0 = spec
        W = inp[name] if layer is None else inp[name][layer]
        blk = np.asarray(W[r0:r0 + 2048, c0:c0 + 128], np.float32)
        return blk.reshape(16, 128, 128).transpose(1, 0, 2).reshape(128, 2048)
    if kind == "ple":
        _, layer, o8 = spec
        W = np.asarray(inp["w_ple_up"][layer][:, o8 * 1024:(o8 + 1) * 1024], np.float32)
        blk = W.reshape(2, 128, 8, 128).transpose(1, 2, 0, 3)
        return blk.reshape(128, 2048)
    if kind == "flog":
        W = np.asarray(inp["w_kvf"][:, 4096:4112], np.float32)
        out = np.zeros((128, 2048), np.float32)
        out[:, :256] = W.reshape(16, 128, 16).transpose(1, 0, 2).reshape(128, 256)
        return out
    raise ValueError(kind)


class Builder:
    def __init__(self, segs, fused):
        self.segs = segs
        self.fused = fused
        self.plan = []
        self.nc = bass.Bass("TRN2", target_bir_lowering=False)
        self.P = Prog()
        self.es = ExitStack()
        self.final_chans = []
        self._n = 0

    def sb(self, name, shape, dt):
        return self.es.enter_context(self.nc.sbuf_tensor(name, shape, dt))

    def dram(self, name, shape, dt, kind):
        if kind == "Internal":
            return self.nc.dram_tensor(name, shape, dt).ap()
        return self.nc.dram_tensor(name, shape, dt, kind=kind).ap()

    def uniq(self, s):
        self._n += 1
        return f"{s}{self._n}"

    def wplan(self, specs):
        i0 = len(self.plan)
        self.plan.extend(specs)
        return i0

    def wacquire(self, idx):
        hi = min(len(self.plan), idx + NSLOT)
        while self.w_issued < hi:
            j = self.w_issued
            s = j % NSLOT
            src = self.walls[j // self.wch][j % self.wch]
            dst = self.wslot[s]
            self.P.dma("pool", lambda e, dst=dst, src=src: e.dma_start(out=dst[:, :], in_=src),
                       chan=f"w{s}", reads=(self.w_first_reads if 0 < j < NSLOT else ()), writes=[self.wbuf[s]])
            self.w_issued += 1
        s = idx % NSLOT
        return self.wslot[s], self.wbuf[s]


def make_plan(segs):
    plan = []

    def mlp(l):
        for g in range(4):
            for j in range(16):
                plan.append(("std", "w_mlp_up", l, 0, (g * 16 + j) * 128))
            for o in range(16):
                plan.append(("std", "w_mlp_down", l, g * 2048, o * 128))

    def ple(l):
        for o in range(16):
            plan.append(("std", "w_ple_gate", l, 0, o * 128))

    for s in segs:
        if s == "S1":
            for h in range(NH):
                for j in (1, 0, 2, 3):
                    plan.append(("std", "w_a_in", 0, 0, j * 2048 + h * 128))
        elif s == "S2a":
            for o in range(16):
                plan.append(("std", "w_a_out", 0, 0, o * 128))
        elif s == "S2b":
            mlp(0); ple(0)
        elif s == "S2c":
            plan.append(("flog",))
            for h in range(NH):
                plan.append(("std", "w_kvf", None, 0, h * 128))
            for h in range(NH):
                plan.append(("std", "w_kvf", None, 0, 2048 + h * 128))
        elif s == "S3a":
            for h in range(NH):
                plan.append(("std", "w_b_q", 0, 0, h * 128))
            for o in range(16):
                plan.append(("std", "w_b_out", 0, 0, o * 128))
        elif s == "S3b":
            mlp(1); ple(1)
    return plan


class Arena:
    PAGE = 256

    def __init__(self, b, name, nbytes):
        self.t = b.sb(name, [128, nbytes // 4], F32)
        self.pages = [Buf(f"{name}_pg{i}") for i in range(nbytes // self.PAGE)]
        self.nbytes = nbytes

    def view(self, off, nelem, dt, parts=128):
        esz = 4 if dt == F32 else 2
        nb = nelem * esz
        assert off % 4 == 0 and off + nb <= self.nbytes, (off, nb, self.nbytes)
        ap = self.t[0:parts, off // 4:(off + nb + 3) // 4]
        if dt != F32:
            ap = ap.bitcast(dt)
        pgs = self.pages[off // self.PAGE:(off + nb - 1) // self.PAGE + 1]
        return ap, pgs


def build(segs, fused=False):
    b = Builder(segs, fused)
    nc, P = b.nc, b.P
    b.plan = make_plan(segs)
    nslab = max(1, len(b.plan))
    WCH = 224
    b.walls = [b.dram(f"wall{i}", [min(WCH, nslab - i * WCH), 128, 2048], F32, "ExternalInput") for i in range((nslab + WCH - 1) // WCH)]
    b.wch = WCH
    vecs_d = b.dram("vecs", [128, NVEC], F32, "ExternalInput")
    hin_d = b.dram("hin", [D, T], F32, "ExternalInput")
    hout_d = b.dram("hout", [D, T], F32, "ExternalOutput")
    pT_d = b.dram("pT", [2, PLE, T], F32, "ExternalInput")
    wple_d = b.dram("wple", [2, 128, 2 * D], F32, "ExternalInput")
    io = {}

    def xio(name, shape, dt, produced):
        kind = "Internal" if fused else ("ExternalOutput" if produced else "ExternalInput")
        io[name] = b.dram(name, shape, dt, kind)
        return io[name]

    PAIRS = [[2 * i, 2 * i + 1] for i in range(4)]
    dbufs = {}

    def dbuf(name):
        if name not in dbufs:
            dbufs[name] = Buf("dram_" + name)
        return dbufs[name]

    def allgather(name, src2d, rows, cols, dt, src_bufs):
        dst = b.dram(name + "_all", [2 * rows, cols], dt, "Internal")
        P.coll(lambda e: e.collective_compute("AllGather", ALU.bypass, replica_groups=PAIRS, ins=[src2d.opt()], outs=[dst.opt()]),
               chan="cc_" + name, reads=src_bufs, writes=[dbuf(name + "_all")])
        return dst[0:rows, :]

    has = lambda s: s in segs
    DBG = getattr(build, "DBG", False)

    def dbgout(name, ap, reads, shape, dt):
        if not DBG:
            return
        dd = b.dram("dbg_" + name, shape, dt, "ExternalOutput")
        P.dma("sp", lambda e: e.dma_start(out=dd, in_=ap), chan="out_dbg", reads=reads)
    if has("S2c") or has("S3a"):
        Kown_d = xio("Kown", [NH, 128, T], BF16, has("S2c"))
        Vown_d = xio("Vown", [NH, 128, 8 * 128], BF16, has("S2c"))
        Down_d = xio("Dloc", [NH, T], F32, has("S2c"))
    if has("S3a") and not fused:
        Kprev_d = xio("Kprev", [NH, 128, T], BF16, False)
        Vprev_d = xio("Vprev", [NH, 128, 8 * 128], BF16, False)
        Dprev_d = xio("Dprev", [NH, T], F32, False)
    fz = {}

    hT = b.sb("hT", [128, NCH, T], F32)
    hbuf = [Buf(f"h{c}") for c in range(NCH)]
    b.w_first_reads = [hbuf[NCH - 2], hbuf[NCH - 1]]
    uT = b.sb("uT", [128, NCH, T], BF16)
    ubuf = [Buf(f"u{c}") for c in range(NCH)]
    A = Arena(b, "A", 32 * 1024)
    X = Arena(b, "X", 24 * 1024)
    b.wslot = [b.sb(f"w{s}", [128, 2048], BF16) for s in range(NSLOT)]
    b.wbuf = [Buf(f"w{s}") for s in range(NSLOT)]
    b.w_issued = 0
    vecs = b.sb("vecs_sb", [128, NVEC], F32); vecbuf = Buf("vecs")
    ones_b = b.sb("ones_b", [128, 128], BF16); onesbuf = Buf("ones")
    ident_b = b.sb("ident_b", [128, 128], BF16); identbuf = Buf("ident")
    ident_f = b.sb("ident_f", [128, 128], F32); identfbuf = Buf("identf")
    rstd = b.sb("rstd", [128, T], F32); rstdbuf = Buf("rstd")
    epsc = b.sb("epsc", [128, 2], F32); epsb = Buf("epsc")
    sq = [b.sb(f"sq{i}", [128, 512], BF16) for i in range(3)]
    sqbuf = [Buf(f"sq{i}") for i in range(3)]
    tmp = [b.sb(f"tmp{i}", [128, 512], F32) for i in range(4)]
    tmpbuf = [Buf(f"tmp{i}") for i in range(4)]
    ps = [b.es.enter_context(nc.psum_tensor(f"ps{i}", [128, 512], F32)) for i in range(8)]
    psbuf = [[Buf(f"ps{i}", excl=True)] for i in range(8)]
    Y = Arena(b, "Y", 14080)
    cnt = {"tmp": 0, "sq": 0, "par": 0}

    def vcol(name, i=0):
        c0, _ = VEC_COLS[name]
        return vecs[:, c0 + i:c0 + i + 1]

    P.dma("sp", lambda e: e.dma_start(out=vecs[:, :], in_=vecs_d), chan="misc", writes=[vecbuf])
    P.op("pool", lambda e: e.memset(ones_b[:, :], 1.0), writes=[onesbuf])
    P.op("pool", lambda e: e.memset(epsc[:, 0:1], EPS), writes=[epsb])
    P.op("pool", lambda e: e.memset(epsc[:, 1:2], 1.0), writes=[epsb])
    zf = tmp[0]
    P.op("pool", lambda e: e.memset(zf[:, 0:128], 0.0), writes=[tmpbuf[0]])
    P.op("pool", lambda e: e.affine_select(out=ident_f[:, :], in_=zf[:, 0:128], pattern=[[-1, 128]],
                                           compare_op=ALU.not_equal, fill=1.0, base=0, channel_multiplier=1),
         reads=[tmpbuf[0]], writes=[identfbuf])
    P.op("pool", lambda e: e.tensor_copy(out=ident_b[:, :], in_=ident_f[:, :]), reads=[identfbuf], writes=[identbuf])
    for c in range(NCH):
        P.dma(("sp", "act")[c % 2], lambda e, c=c: e.dma_start(out=hT[:, c, :], in_=hin_d[c * 128:(c + 1) * 128, :]),
              chan=f"hin{c}", writes=[hbuf[c]])

    pending = []

    def flush_stats():
        while pending:
            stats_chunk(pending.pop(0))

    def stats_chunk(c, sb=6):
        for hf in range(2):
            i = cnt["sq"] % 3; cnt["sq"] += 1
            P.op("act", lambda e: e.activation(out=sq[i][:, :], in_=hT[:, c, hf * 512:(hf + 1) * 512], func=AF.Square),
                 reads=[hbuf[c]], writes=[sqbuf[i]])
            P.op("pe", lambda e: e.matmul(ps[sb + hf][:, :], lhsT=ones_b[:, :], rhs=sq[i][:, :], start=(c == 0), stop=(c == NCH - 1)),
                 reads=[sqbuf[i], onesbuf], writes=[psbuf[sb + hf]])

    def rmsnorm(gname, out_f32_inplace=False, stats="compute"):
        sb = 4
        if stats == "compute":
            for c in range(NCH):
                stats_chunk(c, sb=4)
        elif stats == "piped":
            flush_stats()
            sb = 6
        if stats != "reuse":
            for hf in range(2):
                P.op("act", lambda e, hf=hf: e.activation(out=rstd[:, hf * 512:(hf + 1) * 512], in_=ps[sb + hf][:, :], func=AF.Ln, scale=1.0 / D, bias=epsc[:, 0:1]),
                     reads=[psbuf[sb + hf], epsb], writes=[rstdbuf])
            P.op("act", lambda e: e.activation(out=rstd[:, :], in_=rstd[:, :], func=AF.Exp, scale=-0.5), reads=[rstdbuf], writes=[rstdbuf])
        for c in range(NCH):
            if out_f32_inplace:
                P.op("dve", lambda e, c=c: e.scalar_tensor_tensor(out=hT[:, c, :], in0=hT[:, c, :], scalar=vcol(gname, c), in1=rstd[:, :],
                                                                  op0=ALU.mult, op1=ALU.mult),
                     reads=[hbuf[c], rstdbuf, vecbuf], writes=[hbuf[c]])
            else:
                P.op("dve", lambda e, c=c: e.scalar_tensor_tensor(out=uT[:, c, :], in0=hT[:, c, :], scalar=vcol(gname, c), in1=rstd[:, :],
                                                                  op0=ALU.mult, op1=ALU.mult),
                     reads=[hbuf[c], rstdbuf, vecbuf], writes=[ubuf[c]])

    widx = {"i": 0}

    def dense_gen(rhs, kch, epi, mcols=128, group=None):
        idx = widx["i"]; widx["i"] += 1
        slot, wb = b.wacquire(idx)
        w3 = slot[:, :].rearrange("p (k n) -> p k n", k=16) if mcols == 128 else slot[:, 0:16 * mcols].rearrange("p (k n) -> p k n", k=16)
        par = 0 if cnt.get("single") else cnt["par"] % 2
        cnt["par"] += 1
        n = 0
        group = group or cnt.get("group", 4)
        for k in range(kch):
            rap, rb = rhs(k)
            for hf in range(2):
                pi = 2 * par + hf
                P.op("pe", lambda e: e.matmul(ps[pi][0:mcols, :], lhsT=w3[:, k, :], rhs=rap[:, hf * 512:(hf + 1) * 512],
                                              start=(k == 0), stop=(k == kch - 1)),
                     reads=[wb] + rb, writes=[psbuf[pi]])
                n += 1
                if n % group == 0 and n < 2 * kch:
                    yield
        flush_stats()
        for hf in range(2):
            pi = 2 * par + hf
            epi(hf, ps[pi], psbuf[pi])

    def dense(rhs, kch, epi, mcols=128):
        for _ in dense_gen(rhs, kch, epi, mcols=mcols):
            pass

    def rhs_u(k):
        return uT[:, k, :], [ubuf[k]]

    def newtmp():
        i = cnt["tmp"] % 4; cnt["tmp"] += 1
        return tmp[i], tmpbuf[i]

    def epi_resid(o, final=False):
        def f(hf, p, pb):
            P.op("dve", lambda e: e.tensor_tensor(out=hT[:, o, hf * 512:(hf + 1) * 512], in0=hT[:, o, hf * 512:(hf + 1) * 512], in1=p[:, :], op=ALU.add),
                 reads=[pb, hbuf[o]], writes=[hbuf[o]])
            if final and hf == 1:
                pending.append(o)
        return f

    def mlp():
        hid = [A.view(j * 2048, T, BF16) for j in range(16)]

        def rhs_a(k):
            return hid[k][0], hid[k][1]
        for g in range(4):
            for j in range(16):
                def epi(hf, p, pb, j=j):
                    t, tb = newtmp()
                    P.op("act", lambda e: e.activation(out=t[:, :], in_=p[:, :], func=AF.Relu), reads=[pb], writes=[tb])
                    P.op("dve", lambda e: e.tensor_tensor(out=hid[j][0][:, hf * 512:(hf + 1) * 512], in0=t[:, :], in1=t[:, :], op=ALU.mult),
                         reads=[tb], writes=hid[j][1])
                dense(rhs_u, 16, epi)
            for o in range(16):
                dense(rhs_a, 16, epi_resid(o, final=(g == 3)))

    def ple(l):
        pT_sb, pTb = X.view(0, 2 * T, BF16)
        pT3 = pT_sb.rearrange("p (k t) -> p k t", k=2)
        wp_sb, wpb = X.view(4096, 2 * D, BF16)
        wp3 = wp_sb.rearrange("p (k n) -> p k n", k=2)
        P.dma("pool", lambda e: e.dma_start(out=pT3, in_=pT_d[l].rearrange("(k p) t -> p k t", p=128)), chan="pT", writes=pTb)
        for k in range(2):
            P.dma("pool", lambda e, k=k: e.dma_start(out=wp3[:, k, :], in_=wple_d[l][:, k * D:(k + 1) * D]), chan="wple", writes=wpb)
        for o in range(16):
            def epi(hf, p, pb, o=o):
                pp = ps[4 + hf]; ppb = psbuf[4 + hf]
                for k in range(2):
                    P.op("pe", lambda e, k=k: e.matmul(pp[:, :], lhsT=wp3[:, k, o * 128:(o + 1) * 128], rhs=pT3[:, k, hf * 512:(hf + 1) * 512],
                                                       start=(k == 0), stop=(k == 1)), reads=wpb + pTb, writes=[ppb])
                t, tb = newtmp()
                P.op("act", lambda e: e.activation(out=t[:, :], in_=p[:, :], func=AF.Sigmoid), reads=[pb], writes=[tb])
                t2, tb2 = newtmp()
                P.op("dve", lambda e: e.tensor_tensor(out=t2[:, :], in0=t[:, :], in1=pp[:, :], op=ALU.mult), reads=[tb, ppb], writes=[tb2])
                P.op("dve", lambda e: e.tensor_tensor(out=hT[:, o, hf * 512:(hf + 1) * 512], in0=hT[:, o, hf * 512:(hf + 1) * 512], in1=t2[:, :], op=ALU.add),
                     reads=[tb2, hbuf[o]], writes=[hbuf[o]])
                if hf == 1:
                    pending.append(o)
            dense(rhs_u, 16, epi)


    negD_, negDb = Y.view(0, 16 * NH, F32)
    negD = negD_.rearrange("p (j h) -> p j h", j=16)
    Dsb, Dsbb = Y.view(2048, 2 * T, F32, parts=16)
    dadd, daddb = Y.view(10240, 1, F32, parts=16)
    Dbf, Dbfb = Y.view(10496, T, BF16)
    sel_, selb = X.view(20480, NH * 128, BF16)
    sel = sel_.rearrange("p (h m) -> p h m", h=NH)
    dprep_done = []

    def d_prep():
        dprep_done.append(1)
        Dp = fz["Dprev"] if fused else Dprev_d
        P.dma("sp", lambda e: e.dma_start(out=Dsb[:, 0:T], in_=Dp), chan="dld", reads=[dbuf("D_all")], writes=[Dsbb])
        P.dma("sp", lambda e: e.dma_start(out=Dsb[:, T:2 * T], in_=Down_d), chan="dld", reads=[dbuf("D")], writes=[Dsbb])
        P.op("dve", lambda e: e.tensor_tensor(out=dadd[:, :], in0=Dsb[:, T - 1:T], in1=vcol("flag")[0:16, :], op=ALU.mult), reads=[Dsbb, vecbuf], writes=[daddb])
        P.op("dve", lambda e: e.tensor_scalar(out=Dsb[:, T:2 * T], in0=Dsb[:, T:2 * T], scalar1=dadd[:, 0:1], scalar2=None, op0=ALU.add),
             reads=[Dsbb, daddb], writes=[Dsbb])
        P.op("pool", lambda e: e.memset(Dbf, 0.0), writes=Dbfb)
        P.op("dve", lambda e: e.tensor_copy(out=Dbf[0:16, :], in_=Dsb[:, T:2 * T]), reads=[Dsbb] + Dbfb, writes=Dbfb)
        P.op("dve", lambda e: e.tensor_copy(out=sel, in_=ident_b[:, 0:16].unsqueeze(2).broadcast_to([128, NH, 128])), reads=[identbuf], writes=selb)
        for j in range(16):
            pj = ps[6 + j % 2]
            P.op("pe", lambda e, j=j, pj=pj: e.transpose(out=pj[:, 0:16], in_=Dsb[:, j * 128:(j + 1) * 128], identity=ident_f[0:16, 0:16]),
                 reads=[Dsbb, identfbuf], writes=[psbuf[6 + j % 2]])
            if j < 8:
                P.op("dve", lambda e, j=j, pj=pj: e.tensor_scalar(out=negD[:, j, :], in0=pj[:, 0:16], scalar1=-1.0, scalar2=vcol("prevbias"),
                                                                  op0=ALU.mult, op1=ALU.add), reads=[psbuf[6 + j % 2], vecbuf], writes=[negDb])
            else:
                P.op("dve", lambda e, j=j, pj=pj: e.tensor_scalar(out=negD[:, j, :], in0=pj[:, 0:16], scalar1=-1.0, scalar2=None, op0=ALU.mult),
                     reads=[psbuf[6 + j % 2]], writes=[negDb])

    def kv_phase():
        rmsnorm("kvn", stats="piped" if has("S2b") else "compute")
        kst = [X.view(i * 2048, T, BF16) for i in range(2)]
        vst = [X.view(4096 + i * 2048, T, BF16) for i in range(2)]
        vtm = [X.view(8192 + i * 2048, T, BF16) for i in range(2)]
        ones_f, onesfb = X.view(12288, T, F32)
        sp1, sp1b = X.view(16384, T, F32)
        dl, dlb = X.view(20480, T, F32)
        nbf, nbfb = tmp[3][:, 0:1], tmpbuf[3]
        P.op("pool", lambda e: e.memset(ones_f[:, :], 1.0), writes=onesfb)
        P.op("dve", lambda e: e.tensor_scalar(out=nbf[0:16, :], in0=vcol("bf")[0:16, :], scalar1=-1.0, scalar2=None, op0=ALU.mult),
             reads=[vecbuf], writes=[nbfb])

        def epi(hf, p, pb):
            sl = slice(hf * 512, (hf + 1) * 512)
            P.op("act", lambda e: e.activation(out=sp1[0:16, sl], in_=p[0:16, :], func=AF.Exp, scale=-1.0, bias=nbf[0:16, :]),
                 reads=[pb, nbfb], writes=sp1b)
            P.op("act", lambda e: e.activation(out=sp1[0:16, sl], in_=sp1[0:16, sl], func=AF.Ln, bias=epsc[0:16, 1:2]), reads=sp1b + [epsb], writes=sp1b)
            if hf == 1:
                P.op("dve", lambda e: e.tensor_tensor_scan(out=dl[0:16, :], data0=ones_f[0:16, :], data1=sp1[0:16, :], initial=0.0,
                                                           op0=ALU.mult, op1=ALU.add), reads=sp1b + onesfb, writes=dlb)
                P.op("dve", lambda e: e.tensor_scalar(out=dl[0:16, :], in0=dl[0:16, :], scalar1=-1.0, scalar2=None, op0=ALU.mult), reads=dlb, writes=dlb)
                P.dma("sp", lambda e: e.dma_start(out=Down_d, in_=dl[0:16, :]), chan="out_d", reads=dlb, writes=[dbuf("D")])
        dense(rhs_u, 16, epi, mcols=16)
        if fused:
            fz["Dprev"] = allgather("D", Down_d, NH, T, F32, [dbuf("D")])
        if fused:
            d_prep()
        for h in range(NH):
            def epi(hf, p, pb, h=h):
                k_ap, kb = kst[h % 2]
                P.op("act", lambda e: e.activation(out=k_ap[:, hf * 512:(hf + 1) * 512], in_=p[:, :], func=AF.Copy), reads=[pb], writes=kb)
                if hf == 1:
                    P.dma("sp", lambda e: e.dma_start(out=Kown_d[h], in_=k_ap), chan=f"out_k{h % 2}", reads=kb, writes=[dbuf(f"K{h}")])
            dense(rhs_u, 16, epi)
            if fused and h % 4 == 3:
                g = h // 4
                fz[f"Kprev{g}"] = allgather(f"K{g}_", Kown_d[4 * g:4 * g + 4].rearrange("h p t -> (h p) t"), 512, T, BF16,
                                            [dbuf(f"K{hh}") for hh in range(4 * g, 4 * g + 4)]).rearrange("(h p) t -> h p t", h=4)
        for h in range(NH):
            def epi(hf, p, pb, h=h):
                v_ap, vb = vst[h % 2]
                t_ap, tb = vtm[h % 2]
                P.op("act", lambda e: e.activation(out=v_ap[:, hf * 512:(hf + 1) * 512], in_=p[:, :], func=AF.Copy), reads=[pb], writes=vb)
                pt = ps[6 + hf][:, :].bitcast(BF16)
                for j in range(4):
                    jj = hf * 4 + j
                    P.op("pe", lambda e, j=j, jj=jj: e.transpose(out=pt[:, j * 128:(j + 1) * 128], in_=v_ap[:, jj * 128:(jj + 1) * 128], identity=ident_b[:, :]),
                         reads=vb + [identbuf], writes=[psbuf[6 + hf]])
                P.op("dve", lambda e: e.tensor_copy(out=t_ap[:, hf * 512:(hf + 1) * 512], in_=pt[:, 0:512]), reads=[psbuf[6 + hf]], writes=tb)
                if hf == 1:
                    P.dma("sp", lambda e: e.dma_start(out=Vown_d[h], in_=t_ap), chan=f"out_v{h % 2}", reads=tb, writes=[dbuf(f"V{h}")])
            dense(rhs_u, 16, epi)
            if fused and h % 4 == 3:
                g = h // 4
                fz[f"Vprev{g}"] = allgather(f"V{g}_", Vown_d[4 * g:4 * g + 4].rearrange("h p t -> (h p) t"), 512, T, BF16,
                                            [dbuf(f"V{hh}") for hh in range(4 * g, 4 * g + 4)]).rearrange("(h p) t -> h p t", h=4)

    def attn_phase():
        rmsnorm("mix1", stats="reuse" if fused else "compute")
        kh = [X.view(i * 4096, 2 * T, BF16) for i in range(2)]
        vh = [X.view(8192 + i * 4096, 2 * T, BF16) for i in range(2)]
        qh = [X.view(16384 + i * 2048, T, BF16) for i in range(2)]
        masks = rstd[:, :].bitcast(BF16).rearrange("p (r t) -> p r t", r=4)
        maskb = [rstdbuf]
        pt, ptb = sq, sqbuf
        rden, rdenb = tmp[2], tmpbuf[2]
        og = [A.view(h * 2048, T, BF16) for h in range(NH)]
        P.op("pool", lambda e: e.memset(masks[:, :, :], 0.0), writes=maskb)
        for r in range(4):
            P.op("pool", lambda e, r=r: e.affine_select(out=masks[:, r, :], in_=masks[:, r, :], pattern=[[1, 512]], compare_op=ALU.is_ge,
                                                       fill=NEG, base=-128 * r, channel_multiplier=-1), reads=maskb, writes=maskb)
        if not dprep_done:
            d_prep()
        dbgout("negD", negD_, negDb, [128, 16 * NH], F32)
        dbgout("sel", sel_, selb, [128, NH * 128], BF16)
        dbgout("Dbf", Dbf, Dbfb, [128, T], BF16)
        dbgout("Dsb", Dsb, Dsbb, [16, 2 * T], F32)
        dbgout("masks", rstd[:, :], maskb, [128, T], F32)
        npt = [0]
        cnt["single"] = True
        SB = (6, 7, 2, 3)
        NHL = getattr(build, "NHL", NH)
        for h in range(NH):
            s = h % 2
            k_ap, kb = kh[s]; v_ap, vb = vh[s]; q_ap, qb_ = qh[s]
            if h >= NHL:
                def epi0(hf, p, pb, h=h):
                    P.op("act", lambda e: e.activation(out=og[h][0][:, hf * 512:(hf + 1) * 512], in_=p[:, :], func=AF.Copy), reads=[pb], writes=og[h][1])
                dense(rhs_u, 16, epi0)
                continue
            v3 = v_ap.rearrange("p (j e) -> p j e", j=16)
            Kph = fz[f"Kprev{h // 4}"][h % 4] if fused else Kprev_d[h]
            Vph = fz[f"Vprev{h // 4}"][h % 4] if fused else Vprev_d[h]
            P.dma("sp", lambda e, h=h, k_ap=k_ap: e.dma_start(out=k_ap[:, 0:T], in_=Kph), chan=f"kld{s}", reads=[dbuf(f"K{h // 4}__all")], writes=kb)
            P.dma("sp", lambda e, h=h, k_ap=k_ap: e.dma_start(out=k_ap[:, T:2 * T], in_=Kown_d[h]), chan=f"kld{s}", reads=[dbuf(f"K{h}")], writes=kb)
            P.dma("sp", lambda e, h=h, v_ap=v_ap: e.dma_start(out=v_ap[:, 0:T], in_=Vph), chan=f"vld{s}", reads=[dbuf(f"V{h // 4}__all")], writes=vb)
            P.dma("sp", lambda e, h=h, v_ap=v_ap: e.dma_start(out=v_ap[:, T:2 * T], in_=Vown_d[h]), chan=f"vld{s}", reads=[dbuf(f"V{h}")], writes=vb)

            def epi(hf, p, pb, q_ap=q_ap, qb_=qb_):
                P.op("act", lambda e: e.activation(out=q_ap[:, hf * 512:(hf + 1) * 512], in_=p[:, :], func=AF.Copy, scale=HD ** -0.5), reads=[pb], writes=qb_)
            dense(rhs_u, 16, epi)
            if h == 0:
                dbgout("kh", k_ap, kb, [128, 2 * T], BF16)
                dbgout("vh", v_ap, vb, [128, 2 * T], BF16)
                dbgout("qh", q_ap, qb_, [128, T], BF16)
            for qb in range(2):
                qs = slice(qb * 512, (qb + 1) * 512)
                tiles = [(j, None) for j in range(8)]
                for jj in range(4 * qb + 4):
                    tiles.append((8 + jj, jj - 4 * qb if jj >= 4 * qb else None))
                po, pd = ps[4], ps[5]
                ptidx = {}

                def qk(ti):
                    j, mr = tiles[ti]
                    pS = ps[SB[ti % 4]]; pSb = psbuf[SB[ti % 4]]
                    P.op("pe", lambda e: e.matmul(pS[:, :], lhsT=k_ap[:, j * 128:(j + 1) * 128], rhs=q_ap[:, qs], start=True, stop=False),
                         reads=kb + qb_, writes=[pSb])
                    P.op("pe", lambda e: e.matmul(pS[:, :], lhsT=sel[:, h, :], rhs=Dbf[:, qs], start=False, stop=True),
                         reads=selb + Dbfb, writes=[pSb])
                    if mr is not None:
                        P.op("dve", lambda e: e.tensor_tensor(out=pS[:, :], in0=pS[:, :], in1=masks[:, mr, :], op=ALU.add),
                             reads=[pSb] + maskb, writes=[pSb])

                def ex(ti):
                    j, mr = tiles[ti]
                    pS = ps[SB[ti % 4]]; pSb = psbuf[SB[ti % 4]]
                    i = npt[0] % 3; npt[0] += 1
                    ptidx[ti] = i
                    P.op("act", lambda e: e.activation(out=pt[i][:, :], in_=pS[:, :], func=AF.Exp, bias=negD[:, j, h:h + 1], scale=1.0),
                         reads=[pSb, negDb], writes=[ptb[i]])

                def pv(ti):
                    j, mr = tiles[ti]
                    i = ptidx[ti]
                    first, last = ti == 0, ti == len(tiles) - 1
                    P.op("pe", lambda e: e.matmul(po[:, :], lhsT=v3[:, j, :], rhs=pt[i][:, :], start=first, stop=last),
                         reads=vb + [ptb[i]], writes=[psbuf[4]])
                    P.op("pe", lambda e: e.matmul(pd[:, :], lhsT=ones_b[:, :], rhs=pt[i][:, :], start=first, stop=last),
                         reads=[onesbuf, ptb[i]], writes=[psbuf[5]])
                qk(0)
                if len(tiles) > 1:
                    qk(1)
                for ti in range(len(tiles)):
                    if ti + 2 < len(tiles):
                        qk(ti + 2)
                    ex(ti)
                    pv(ti)
                P.op("dve", lambda e: e.reciprocal(out=rden[:, :], in_=pd[:, :]), reads=[psbuf[5]], writes=[rdenb])
                P.op("dve", lambda e, h=h: e.tensor_tensor(out=og[h][0][:, qs], in0=po[:, :], in1=rden[:, :], op=ALU.mult),
                     reads=[psbuf[4], rdenb], writes=og[h][1])
                if h == 0 and qb == 0:
                    dbgout("rden", rden[:, :], [rdenb], [128, 512], F32)
                if h == 0 and qb == 1:
                    dbgout("og0", og[0][0], og[0][1], [128, T], BF16)

        cnt["single"] = False

        def rhs_og(k):
            return og[k][0], og[k][1]
        for o in range(16):
            dense(rhs_og, 16, epi_resid(o, final=True))


    if has("S1") or has("S2a"):
        OL_d = xio("OL", [NH, 128, T], F32, has("S1"))
        QC_d = xio("QC", [NH, 128, T], BF16, has("S1"))
        G_d = xio("G", [NH, 128, T], BF16, has("S1"))
    if has("S1"):
        SF_d = xio("SF", [NH, 128, 128], F32, True)
    if has("S2a") and not fused:
        SFp_d = xio("SFprev", [NH, 128, 128], F32, False)

    def hgrn1():
        rmsnorm("mix0")
        cnt["single"] = True
        cnt["group"] = 2
        lbv, lbb = Y.view(0, 16, F32)
        oml, omlb = Y.view(64, 16, F32)
        noml, nomlb = Y.view(128, 16, F32)
        dec = [Y.view(256 + i * 256, NCHUNK, F32) for i in range(2)]
        attm = [Y.view(768 + i * 256, 128, BF16) for i in range(2)]
        S32 = [Y.view(1280, 128, F32), Y.view(13568, 128, F32)]
        m01, m01b = Y.view(1792, 128, F32)
        cmask, cmb = Y.view(2304, T, F32)
        onesf, onfb = Y.view(6400, T, F32)
        S16 = [Y.view(10496 + i * 256, 128, BF16) for i in range(12)]
        constb = lbb
        P.op("dve", lambda e: e.tensor_tensor(out=lbv, in0=vecs[:, VEC_COLS["lb0"][0]:VEC_COLS["lb0"][0] + 16],
                                              in1=vecs[:, VEC_COLS["lb1"][0]:VEC_COLS["lb1"][0] + 16], op=ALU.subtract), reads=[vecbuf], writes=constb)
        P.op("act", lambda e: e.activation(out=lbv, in_=lbv, func=AF.Sigmoid), reads=constb, writes=constb)
        P.op("dve", lambda e: e.tensor_scalar(out=oml, in0=lbv, scalar1=-1.0, scalar2=1.0, op0=ALU.mult, op1=ALU.add), reads=constb, writes=constb)
        P.op("dve", lambda e: e.tensor_scalar(out=noml, in0=lbv, scalar1=-1.0, scalar2=None, op0=ALU.add), reads=constb, writes=constb)
        P.op("pool", lambda e: e.memset(onesf, 1.0), writes=onfb)
        P.op("pool", lambda e: e.memset(cmask, 1.0), writes=cmb)
        cm3 = cmask.rearrange("p (c i) -> p c i", i=CH)
        P.op("pool", lambda e: e.memset(cm3[:, :, 0:1], 0.0), writes=cmb)
        P.op("pool", lambda e: e.memset(m01, 1.0), writes=m01b)
        P.op("pool", lambda e: e.affine_select(out=m01, in_=m01, pattern=[[1, 128]], compare_op=ALU.is_ge, fill=0.0, base=0, channel_multiplier=-1),
             reads=m01b, writes=m01b)
        for g in range(4):
            P.op("pool", lambda e, g=g: e.affine_select(out=m01[32 * g:32 * g + 32, :], in_=m01[32 * g:32 * g + 32, :], pattern=[[-1, 128]],
                                                      compare_op=ALU.is_ge, fill=0.0, base=32 * (g + 1) - 1, channel_multiplier=0),
                 reads=m01b, writes=m01b)
        qs, qsb = X.view(0, T, F32)
        sg, sgb = X.view(4096, T, F32)
        lf, lfb = X.view(8192, T, F32)
        bb, bbb = X.view(12288, T, F32)
        Bc, Bcb = X.view(16384, T, F32)
        e1, e1b = X.view(20480, T, F32)
        bb3 = bb.rearrange("p (c i) -> p c i", i=CH)
        e13 = e1.rearrange("p (c i) -> p c i", i=CH)

        def hb(s, i):
            return A.view(s * 10240 + i * 2048, T, BF16)
        kec = [A.view(20480 + i * 2048, T, BF16, parts=32) for i in range(2)]
        vcl = [A.view(24576 + i * 2048, T, BF16, parts=32) for i in range(2)]
        o_out, oob = A.view(28672, T, F32)
        rb16 = rstd[:, :].bitcast(BF16)
        gs, gsb = rb16[:, 0:T], [rstdbuf]
        qcm, qcb = rb16[:, T:2 * T], [rstdbuf]
        scale = HD ** -0.5

        def stageA(h):
            s = h % 2
            q_in, qib = hb(s, 0); k_in, kib = hb(s, 1); vtm, vtb = hb(s, 2); keT, keTb = hb(s, 3); vT, vTb = hb(s, 4)
            dc, dcb = dec[s]

            def epi_act(dst, dstb, func):
                def f(hf, p, pb):
                    P.op("act", lambda e: e.activation(out=dst[:, hf * 512:(hf + 1) * 512], in_=p[:, :], func=func), reads=[pb], writes=dstb)
                return f
            def gate1():
                P.op("act", lambda e: e.activation(out=lf, in_=sg, func=AF.Ln, scale=oml[:, h:h + 1], bias=lbv[:, h:h + 1]), reads=sgb + constb, writes=lfb)
                yield
                P.op("dve", lambda e: e.tensor_scalar(out=sg, in0=sg, scalar1=noml[:, h:h + 1], scalar2=oml[:, h:h + 1], op0=ALU.mult, op1=ALU.add),
                     reads=sgb + constb, writes=sgb)
                yield
                P.op("dve", lambda e: e.tensor_tensor_scan(out=bb, data0=cmask, data1=lf, initial=0.0, op0=ALU.mult, op1=ALU.add), reads=lfb + cmb, writes=bbb)
                yield
                P.op("dve", lambda e: e.tensor_tensor_scan(out=Bc, data0=onesf, data1=lf, initial=0.0, op0=ALU.mult, op1=ALU.add), reads=lfb + onfb, writes=Bcb)
                yield
                P.op("act", lambda e: e.activation(out=lf, in_=bb, func=AF.Exp, scale=-1.0), reads=bbb, writes=lfb)
                yield
                P.op("dve", lambda e: e.tensor_tensor(out=k_in, in0=sg, in1=lf, op=ALU.mult), reads=sgb + lfb, writes=kib)
                yield
                P.op("act", lambda e: e.activation(out=dc, in_=bb3[:, :, CH - 1], func=AF.Exp), reads=bbb, writes=dcb)
                yield
                P.op("dve", lambda e: e.tensor_tensor(out=e13, in0=bb3[:, :, CH - 1:CH].broadcast_to([128, NCHUNK, CH]), in1=bb3, op=ALU.subtract),
                     reads=bbb, writes=e1b)
                yield
                P.op("act", lambda e: e.activation(out=e1, in_=e1, func=AF.Exp), reads=e1b, writes=e1b)
                yield
                P.op("dve", lambda e: e.tensor_tensor(out=keT, in0=sg, in1=e1, op=ALU.mult), reads=sgb + e1b, writes=keTb)
                yield
                P.op("act", lambda e: e.activation(out=e1, in_=bb, func=AF.Exp), reads=bbb, writes=e1b)
                yield
                P.op("act", lambda e: e.activation(out=lf, in_=Bc, func=AF.Exp), reads=Bcb, writes=lfb)
                yield

            def mix(ga, gb, ratio=2):
                alive_a = alive_b = True
                while alive_a or alive_b:
                    for _ in range(ratio):
                        if alive_a:
                            try:
                                next(ga)
                            except StopIteration:
                                alive_a = False
                        yield
                    if alive_b:
                        try:
                            next(gb)
                        except StopIteration:
                            alive_b = False

            def three_tiles():
                yield from dense_gen(rhs_u, 16, epi_act(qs, qsb, AF.Silu)); yield
                yield from dense_gen(rhs_u, 16, epi_act(vT, vTb, AF.Copy)); yield
                yield from dense_gen(rhs_u, 16, epi_act(gs, gsb, AF.Silu)); yield
            yield from dense_gen(rhs_u, 16, epi_act(sg, sgb, AF.Sigmoid)); yield
            yield from mix(three_tiles(), gate1(), ratio=2)
            P.dma("sp", lambda e: e.dma_start(out=G_d[h], in_=gs), chan="out_g", reads=gsb, writes=[dbuf(f"G{h}")])
            P.op("dve", lambda e: e.scalar_tensor_tensor(out=q_in, in0=qs, scalar=scale, in1=e1, op0=ALU.mult, op1=ALU.mult), reads=qsb + e1b, writes=qib)
            yield
            P.op("dve", lambda e: e.scalar_tensor_tensor(out=qcm, in0=qs, scalar=scale, in1=lf, op0=ALU.mult, op1=ALU.mult), reads=qsb + lfb, writes=qcb)
            yield
            P.dma("sp", lambda e: e.dma_start(out=QC_d[h], in_=qcm), chan="out_qc", reads=qcb, writes=[dbuf(f"QC{h}")])
            yield
            for (src, srcb, dst, dstb, bank) in ((vT, vTb, vtm, vtb, 4),):
                ptv = ps[bank][:, :].bitcast(BF16)
                for j in range(8):
                    P.op("pe", lambda e, j=j, src=src, ptv=ptv: e.transpose(out=ptv[:, j * 128:(j + 1) * 128], in_=src[:, j * 128:(j + 1) * 128], identity=ident_b[:, :]),
                         reads=srcb + [identbuf], writes=[psbuf[bank]])
                P.op("act", lambda e, dst=dst, ptv=ptv: e.activation(out=dst, in_=ptv, func=AF.Copy), reads=[psbuf[bank]], writes=dstb)
            yield

        def rec(h):
            s = h % 2
            q_in, qib = hb(s, 0); k_in, kib = hb(s, 1); vtm, vtb = hb(s, 2); keT, keTb = hb(s, 3); vT, vTb = hb(s, 4)
            dc, dcb = dec[s]

            def prep_batch(bt):
                for (src, srcb, (dst, dstb), bank) in ((keT, keTb, kec[bt % 2], 4), (vT, vTb, vcl[bt % 2], 4)):
                    ptv = ps[bank][:, :].bitcast(BF16)
                    for ci in range(8):
                        c = 8 * bt + ci
                        P.op("pe", lambda e: e.transpose(out=ptv[0:32, ci * 128:(ci + 1) * 128], in_=src[:, c * CH:(c + 1) * CH], identity=ident_b[:, :]),
                             reads=srcb + [identbuf], writes=[psbuf[bank]])
                    P.op("act", lambda e: e.activation(out=dst, in_=ptv[0:32, :], func=AF.Copy), reads=[psbuf[bank]], writes=dstb)
                    yield

            def kv_tile(jt):
                for cc in range(4):
                    c = 4 * jt + cc
                    bt, ci = c // 8, c % 8
                    kc, kcb = kec[bt % 2]; vc_, vcb = vcl[bt % 2]
                    kvbank = (2, 3, 7)[c % 3]
                    kvp = ps[kvbank][:, 0:128]; kvb = psbuf[kvbank]
                    so, sob = S32[c % 2]; sn32, sn32b = S32[(c + 1) % 2]
                    P.op("pe", lambda e: e.matmul(kvp, lhsT=kc[:, ci * 128:(ci + 1) * 128], rhs=vc_[:, ci * 128:(ci + 1) * 128], start=True, stop=True),
                         reads=kcb + vcb, writes=[kvb])
                    if c == 0:
                        P.op("dve", lambda e: e.tensor_copy(out=sn32, in_=kvp), reads=[kvb], writes=sn32b)
                    else:
                        P.op("dve", lambda e: e.scalar_tensor_tensor(out=sn32, in0=so, scalar=dc[:, c:c + 1], in1=kvp, op0=ALU.mult, op1=ALU.add),
                             reads=[kvb] + sob + dcb, writes=sn32b)
                    if c < NCHUNK - 1:
                        sn, snb = S16[(c + 1) % 12]
                        P.op("act", lambda e: e.activation(out=sn, in_=sn32, func=AF.Copy), reads=sn32b, writes=snb)
                    else:
                        P.dma("sp", lambda e: e.dma_start(out=SF_d[h], in_=sn32), chan="out_sf", reads=sn32b, writes=[dbuf(f"SF{h}")])
                    yield
            yield from prep_batch(0)
            yield from kv_tile(0)
            for jt in range(8):
                tsl = slice(jt * 128, (jt + 1) * 128)
                r = jt % 2
                ap_att = ps[5][:, r * 128:(r + 1) * 128]; ab = psbuf[5]
                ap_o = ps[6][:, r * 128:(r + 1) * 128]; ob = psbuf[6]
                am, amb = attm[r]
                P.op("pe", lambda e: e.matmul(ap_att, lhsT=k_in[:, tsl], rhs=q_in[:, tsl], start=True, stop=True), reads=kib + qib, writes=[ab])
                yield
                if jt % 2 == 0 and jt // 2 + 1 < 4:
                    yield from prep_batch(jt // 2 + 1)
                if jt < 7:
                    yield from kv_tile(jt + 1)
                P.op("dve", lambda e: e.tensor_tensor(out=am, in0=ap_att, in1=m01, op=ALU.mult), reads=[ab] + m01b, writes=amb)
                P.op("pe", lambda e: e.matmul(ap_o, lhsT=vtm[:, tsl], rhs=am, start=True, stop=(jt == 0), skip_group_check=True),
                     reads=vtb + amb, writes=[ob])
                for cc in range(4):
                    c = 4 * jt + cc
                    if c == 0:
                        continue
                    sc, scb = S16[c % 12]
                    last = cc == 3
                    P.op("pe", lambda e: e.matmul(ap_o[:, 32 * cc:32 * cc + 32], lhsT=sc, rhs=q_in[:, c * CH:(c + 1) * CH],
                                                  start=False, stop=last, skip_group_check=True),
                         reads=scb + qib, writes=[ob])
                P.op("act", lambda e: e.activation(out=o_out[:, tsl], in_=ap_o, func=AF.Copy), reads=[ob], writes=oob)
                yield
            P.dma("sp", lambda e: e.dma_start(out=OL_d[h], in_=o_out), chan="out_ol", reads=oob, writes=[dbuf(f"OL{h}")])

        H1LIM = getattr(build, "H1LIM", None)
        if H1LIM is not None:
            g_ = stageA(0)
            for _ in range(3):
                next(g_)
            if H1LIM >= 1:
                next(g_)
                dbgout("q_in", hb(0, 0)[0], hb(0, 0)[1], [128, T], BF16)
                dbgout("k_in", hb(0, 1)[0], hb(0, 1)[1], [128, T], BF16)
                dbgout("keT", hb(0, 3)[0], hb(0, 3)[1], [128, T], BF16)
                dbgout("vtm", hb(0, 2)[0], hb(0, 2)[1], [128, T], BF16)
                dbgout("bb", bb, bbb, [128, T], F32)
                dbgout("Bc", Bc, Bcb, [128, T], F32)
                dbgout("dec", dec[0][0], dec[0][1], [128, NCHUNK], F32)
                dbgout("m01", m01, m01b, [128, 128], F32)
            if H1LIM >= 2:
                r_ = rec(0)
                for _ in range(H1LIM - 1):
                    next(r_)
                dbgout("o_out", o_out, oob, [128, T], F32)
            return
        prev = None
        for h in range(NH + 1):
            gens = []
            if h < NH:
                gens.append(stageA(h))
            if prev is not None:
                gens.append(rec(prev))
            while gens:
                for g_ in list(gens):
                    try:
                        next(g_)
                    except StopIteration:
                        gens.remove(g_)
            prev = h if h < NH else None
        cnt["single"] = False
        cnt["group"] = 4
        if fused:
            fz["SFp"] = allgather("SF", SF_d.rearrange("h p e -> (h p) e"), NH * 128, 128, F32,
                                  [dbuf(f"SF{h}") for h in range(NH)]).rearrange("(h p) e -> h p e", h=NH)

    def hgrn2():
        og = [A.view(h * 2048, T, BF16) for h in range(NH)]
        o32 = [X.view(0, T, F32), X.view(4096, T, F32), X.view(18432, T, F32)]
        qc = [X.view(8192 + i * 2048, T, BF16) for i in range(2)]
        gsl = [X.view(12288, T, BF16), X.view(14336, T, BF16), X.view(22528, T, BF16)]
        s32 = [X.view(16384 + i * 512, 128, F32) for i in range(2)]
        s16 = [X.view(17408 + i * 256, 128, BF16) for i in range(2)]
        def load(h):
            s = h % 2
            o_ap, o_b = o32[h % 3]; q_ap, q_b = qc[s]; g_ap, g_b = gsl[h % 3]; sa, sab = s32[s]; sh, shb = s16[s]
            SFp = fz["SFp"] if fused else SFp_d
            P.dma("sp", lambda e: e.dma_start(out=o_ap, in_=OL_d[h]), chan=f"ld_o{h % 3}", reads=[dbuf(f"OL{h}")], writes=o_b)
            P.dma("sp", lambda e: e.dma_start(out=q_ap, in_=QC_d[h]), chan=f"ld_q{s}", reads=[dbuf(f"QC{h}")], writes=q_b)
            P.dma("sp", lambda e: e.dma_start(out=g_ap, in_=G_d[h]), chan=f"ld_g{h % 3}", reads=[dbuf(f"G{h}")], writes=g_b)
            P.dma("sp", lambda e: e.dma_start(out=sa, in_=SFp[h]), chan=f"ld_s{s}", reads=[dbuf("SF_all")], writes=sab)

        def cvt(h):
            s = h % 2
            sa, sab = s32[s]; sh, shb = s16[s]
            P.op("dve", lambda e: e.tensor_scalar(out=sh, in0=sa, scalar1=vcol("flag"), scalar2=None, op0=ALU.mult), reads=sab + [vecbuf], writes=shb)

        unit_state = {}

        def front_pc(h, hf):
            s = h % 2
            q_ap, q_b = qc[s]; sh, shb = s16[s]
            sl = slice(hf * 512, (hf + 1) * 512)
            pc = ps[4 + hf]; pcb = psbuf[4 + hf]
            P.op("pe", lambda e: e.matmul(pc[:, :], lhsT=sh, rhs=q_ap[:, sl], start=True, stop=True), reads=shb + q_b, writes=[pcb])

        def front(h, hf):
            s = h % 2
            o_ap, o_b = o32[h % 3]; q_ap, q_b = qc[s]; sh, shb = s16[s]
            sl = slice(hf * 512, (hf + 1) * 512)
            pc = ps[4 + hf]; pcb = psbuf[4 + hf]
            pm = ps[6 + hf]; pmb = psbuf[6 + hf]
            P.op("dve", lambda e: e.tensor_tensor(out=o_ap[:, sl], in0=o_ap[:, sl], in1=pc[:, :], op=ALU.add), reads=[pcb] + o_b, writes=o_b)
            i = cnt["sq"] % 3; cnt["sq"] += 1
            P.op("dve", lambda e: e.tensor_tensor(out=sq[i][:, :], in0=o_ap[:, sl], in1=o_ap[:, sl], op=ALU.mult), reads=o_b, writes=[sqbuf[i]])
            P.op("pe", lambda e: e.matmul(pm[:, :], lhsT=ones_b[:, :], rhs=sq[i][:, :], start=True, stop=True), reads=[sqbuf[i], onesbuf], writes=[pmb])

        def back_act(h, hf):
            pm = ps[6 + hf]; pmb = psbuf[6 + hf]
            t, tb = newtmp()
            unit_state[(h, hf)] = (t, tb)
            P.op("act", lambda e: e.activation(out=t[:, :], in_=pm[:, :], func=AF.Ln, scale=1.0 / HD, bias=epsc[:, 0:1]), reads=[pmb, epsb], writes=[tb])
            P.op("act", lambda e: e.activation(out=t[:, :], in_=t[:, :], func=AF.Exp, scale=-0.5), reads=[tb], writes=[tb])

        def back_dve(h, hf):
            o_ap, o_b = o32[h % 3]; g_ap, g_b = gsl[h % 3]
            sl = slice(hf * 512, (hf + 1) * 512)
            t, tb = unit_state[(h, hf)]
            P.op("dve", lambda e: e.scalar_tensor_tensor(out=t[:, :], in0=o_ap[:, sl], scalar=vcol("hg"), in1=t[:, :], op0=ALU.mult, op1=ALU.mult),
                 reads=o_b + [tb, vecbuf], writes=[tb])
            P.op("dve", lambda e: e.tensor_tensor(out=og[h][0][:, sl], in0=t[:, :], in1=g_ap[:, sl], op=ALU.mult), reads=[tb] + g_b, writes=og[h][1])
        units = [(h, hf) for h in range(NH) for hf in range(2)]
        load(0)
        cvt(0)
        load(1)
        cvt(1)
        front_pc(*units[0])
        for k, (h, hf) in enumerate(units):
            if k > 1:
                back_act(*units[k - 2])
            if k + 1 < len(units):
                front_pc(*units[k + 1])
            front(h, hf)
            if k > 1:
                back_dve(*units[k - 2])
            if hf == 1 and h + 2 < NH:
                load(h + 2)
            if hf == 0 and h >= 1 and h + 1 < NH:
                cvt(h + 1)
        for u in units[-2:]:
            back_act(*u)
            back_dve(*u)
        cnt["single"] = False

        def rhs_og(k):
            return og[k][0], og[k][1]
        for o in range(16):
            dense(rhs_og, 16, epi_resid(o, final=True))

    for seg in segs:
        if seg == "S2b":
            rmsnorm("mlp0", stats="piped" if has("S2a") else "compute"); mlp(); rmsnorm("ple0", stats="piped"); ple(0)
        elif seg == "S3b":
            rmsnorm("mlp1", stats="piped" if has("S3a") else "compute"); mlp(); rmsnorm("ple1", stats="piped"); ple(1)
            rmsnorm("fin", out_f32_inplace=True, stats="piped")
        elif seg == "S2c":
            kv_phase()
        elif seg == "S3a":
            attn_phase()
        elif seg == "S1":
            hgrn1()
        elif seg == "S2a":
            hgrn2()

    for c in range(NCH):
        P.dma(("sp", "act", "pool")[c % 3], lambda e, c=c: e.dma_start(out=hout_d[c * 128:(c + 1) * 128, :], in_=hT[:, c, :]), chan="hout", reads=[hbuf[c]])
    finals = ["hout"] + [c for c in P.chan_count if c.startswith("out_")]
    assert widx["i"] == len(b.plan) or getattr(build, "H1LIM", None) is not None, (widx["i"], len(b.plan))
    P.emit_all(nc, finals)
    b.es.close()
    return b


NCORES = 8
_PROG_CACHE = {}


def _get_prog(segs, fused=False):
    key = (tuple(segs), fused)
    if key not in _PROG_CACHE:
        _PROG_CACHE[key] = build(list(segs), fused=fused)
    return _PROG_CACHE[key]


def _run(segs, inp, per_core, fused=False):
    b = _get_prog(segs, fused)
    slabs = [pack_slab(inp, s) for s in b.plan]
    walls = {f"wall{i}": np.stack(slabs[i * b.wch:(i + 1) * b.wch]) for i in range(len(b.walls))}
    wple = np.ascontiguousarray(np.stack([np.asarray(inp["w_ple_up"][l], np.float32).reshape(2, 128, 2048).transpose(1, 0, 2).reshape(128, 4096)
                                          for l in range(2)]))
    maps = []
    for r in range(NCORES):
        m = {"wple": wple, "vecs": pack_vecs(inp, r)}
        m.update(walls)
        m.update(per_core[r])
        maps.append(m)
    res = run_bass_kernel_spmd(b.nc, maps, core_ids=list(range(NCORES)))
    return res.results


ALL_SEGS = ["S1", "S2a", "S2b", "S2c", "S3a", "S3b"]


def kernel(**inputs):
    inp = {k: np.asarray(v) for k, v in inputs.items()}
    x = np.asarray(inp["x"], np.float32)
    p = np.asarray(inp["p"], np.float32)
    base = []
    for r in range(NCORES):
        bb, sl = r // 2, slice((r % 2) * T, (r % 2 + 1) * T)
        base.append({"hin": np.ascontiguousarray(x[bb, sl].T), "pT": np.ascontiguousarray(p[:, bb, sl].transpose(0, 2, 1))})
    o = _run(ALL_SEGS, inp, base, fused=True)
    out = np.empty((4, SEQ, D), np.float32)
    for r in range(NCORES):
        bb, sl = r // 2, slice((r % 2) * T, (r % 2 + 1) * T)
        out[bb, sl] = np.asarray(o[r]["hout"]).T
    return out
```
